# Optimizing a Trainium2 kernel written in Bass

```python
import math
import jax, jax.numpy as jnp
from jax import lax
import numpy as np

D_MODEL = 1024
BATCH = 32
SEQ = 256
DEPTH = 1
DEC_BATCH = 8
DEC_SEQ = 4096
PAST_LEN = 256

GRID_W = 64
N_HEADS_A = 4
HD_A = 64
VD_A = 2 * HD_A
W_A = N_HEADS_A * VD_A
QK_COLS = N_HEADS_A * 2 * HD_A
N_HEADS_R = 8
HD_R = 64
W_R = N_HEADS_R * HD_R
LORA_W = 64
LORA_A = 64
D_FF = int(math.ceil(8 * D_MODEL / 3 / 256)) * 256
ROPE_THETA = 10000.0
Q_BLOCK = 128
EPS_RMS = 1e-6
EPS_GN = 64e-5
SPLIT_SIZES = (QK_COLS, QK_COLS, W_A, W_R, W_R, W_R, W_R, 2 * LORA_W, 2 * LORA_A, 2 * D_MODEL)
SPLIT_POINTS = [int(s) for s in np.cumsum(SPLIT_SIZES)[:-1]]
IN_COLS = int(sum(SPLIT_SIZES))

kernel_name = "diffusion_diffattn_rwkv7_hybrid_step"


def rms_norm(x, w, eps=EPS_RMS):
    xf = x.astype(jnp.float32)
    y = xf * lax.rsqrt(jnp.mean(xf * xf, axis=-1, keepdims=True) + eps)
    return (y * w.astype(jnp.float32)).astype(x.dtype)


def modulation(cond, ada_w, ada_b):
    m = jax.nn.silu(cond) @ ada_w + ada_b
    return jnp.split(m[..., None, :], 6, axis=-1)


def grid_angles(T):
    rows = T // GRID_W
    row = jnp.broadcast_to(jnp.arange(rows)[:, None], (rows, GRID_W)).reshape(-1).astype(jnp.float32)
    col = jnp.broadcast_to(jnp.arange(GRID_W)[None, :], (rows, GRID_W)).reshape(-1).astype(jnp.float32)
    nf = HD_A // 4
    inv = ROPE_THETA ** (-jnp.arange(nf, dtype=jnp.float32) / nf)
    return row[:, None] * inv, col[:, None] * inv


def rope_axial(x, ang_row, ang_col):
    def rot(xh, ang):
        cos = jnp.cos(ang)[None, :, None, None, :].astype(xh.dtype)
        sin = jnp.sin(ang)[None, :, None, None, :].astype(xh.dtype)
        x1, x2 = jnp.split(xh, 2, axis=-1)
        return jnp.concatenate([x1 * cos - x2 * sin, x1 * sin + x2 * cos], axis=-1)
    xr, xc = jnp.split(x, 2, axis=-1)
    return jnp.concatenate([rot(xr, ang_row), rot(xc, ang_col)], axis=-1)


def diff_attention(q, k, v, lam):
    B, Tq = q.shape[0], q.shape[1]
    nb = Tq // Q_BLOCK
    qb = q.reshape(B, nb, Q_BLOCK, N_HEADS_A, 2, HD_A).transpose(1, 0, 2, 3, 4, 5)
    kf = k.astype(jnp.float32)
    vf = v.astype(jnp.float32)
    scale = HD_A ** -0.5

    def block(qblk):
        s = jnp.einsum('bqhmd,bkhmd->bhmqk', qblk.astype(jnp.float32), kf) * scale
        p = jax.nn.softmax(s, axis=-1)
        a = p[:, :, 0] - lam * p[:, :, 1]
        return jnp.einsum('bhqk,bkhe->bqhe', a, vf)

    o = lax.map(block, qb)
    return o.transpose(1, 0, 2, 3, 4).reshape(B, Tq, N_HEADS_A, VD_A)


def rwkv_scan(s0, r, w, kk, b, k, v):
    def to_time(t):
        t = jnp.stack([t[:, 0], jnp.flip(t[:, 1], axis=1)], axis=1)
        return jnp.moveaxis(t, 2, 0)

    def step(S, inp):
        r_t, w_t, kk_t, b_t, k_t, v_t = inp
        sa = jnp.einsum('bzhvk,bzhk->bzhv', S, kk_t)
        S = S * w_t[..., None, :] - sa[..., :, None] * b_t[..., None, :] + v_t[..., :, None] * k_t[..., None, :]
        return S, jnp.einsum('bzhvk,bzhk->bzhv', S, r_t)

    xs = (to_time(r), to_time(w), to_time(kk), to_time(b), to_time(k), to_time(v))
    s_fin, ys = lax.scan(step, s0, xs)
    ys = jnp.moveaxis(ys, 0, 2)
    ys = jnp.stack([ys[:, 0], jnp.flip(ys[:, 1], axis=1)], axis=1)
    return ys, s_fin


def rwkv_mixer(r, k, v, g, wl, al, s0, lp):
    B, T = r.shape[0], r.shape[1]
    f32 = jnp.float32
    dt = r.dtype
    r = r.astype(f32)
    k = k.astype(f32)
    v = v.astype(f32)
    wl = jnp.tanh(wl.astype(f32)).reshape(B, T, 2, LORA_W)
    al = al.astype(f32).reshape(B, T, 2, LORA_A)
    w_log = -jax.nn.softplus(-(lp['w0'][None, :, None, :] + jnp.einsum('btzl,zlc->bztc', wl, lp['w_lora_up']))) - 0.5
    decay = jnp.exp(-jnp.exp(w_log))
    a = jax.nn.sigmoid(lp['a0'][None, :, None, :] + jnp.einsum('btzl,zlc->bztc', al, lp['a_lora_up']))
    kk = (k * lp['k_k']).reshape(B, T, N_HEADS_R, HD_R)
    kk = kk / jnp.maximum(jnp.sqrt(jnp.sum(kk * kk, axis=-1, keepdims=True)), 1e-12)
    kk = jnp.broadcast_to(kk.reshape(B, 1, T, W_R), (B, 2, T, W_R))
    k_d = k[:, None] * (1.0 + (a - 1.0) * lp['k_a'])
    r_b = jnp.broadcast_to(r[:, None], (B, 2, T, W_R))
    v_b = jnp.broadcast_to(v[:, None], (B, 2, T, W_R))

    def heads(t):
        return t.reshape(B, 2, T, N_HEADS_R, HD_R)

    ys, s_fin = rwkv_scan(s0.astype(f32), heads(r_b), heads(decay), heads(kk), heads(kk * a), heads(k_d), heads(v_b))
    y = ys.sum(axis=1)
    mu = jnp.mean(y, axis=-1, keepdims=True)
    var = jnp.mean(jnp.square(y - mu), axis=-1, keepdims=True)
    y = ((y - mu) * lax.rsqrt(var + EPS_GN)).reshape(B, T, W_R) * lp['ln_x_w'] + lp['ln_x_b']
    bonus = (jnp.sum(heads(r_b) * heads(k_d) * lp['r_k'], axis=-1, keepdims=True) * heads(v_b)).sum(axis=1)
    out = (y + bonus.reshape(B, T, W_R)) * jax.nn.sigmoid(g.astype(f32))
    return out.astype(dt), s_fin


def trunk_layer(x, cond, lp, li, ctx_k=None, ctx_v=None, ctx_state=None):
    latent = ctx_k is not None
    B, T, _ = x.shape
    sh1, sc1, g1, sh2, sc2, g2 = modulation(cond, lp['ada_w'], lp['ada_b'])
    h = rms_norm(x, lp['norm1_w']) * (1.0 + sc1) + sh1
    z = h @ lp['w_in']
    q, k, v, rr, kr, vr, gr, wl, al, bg = jnp.split(z, SPLIT_POINTS, axis=-1)

    q = rms_norm(q.reshape(B, T, N_HEADS_A, 2, HD_A), lp['q_norm_w'])
    k = rms_norm(k.reshape(B, T, N_HEADS_A, 2, HD_A), lp['k_norm_w'])
    v = v.reshape(B, T, N_HEADS_A, VD_A)
    own_k, own_v = k, v
    if latent:
        ang_r, ang_c = grid_angles(T)
        q = rope_axial(q, ang_r, ang_c)
        k = jnp.concatenate([ctx_k, rope_axial(k, ang_r, ang_c)], axis=1)
        v = jnp.concatenate([ctx_v, v], axis=1)
        s0 = ctx_state
    else:
        s0 = jnp.zeros((B, 2, N_HEADS_R, HD_R, HD_R), jnp.float32)
    lam_init = 0.8 - 0.6 * math.exp(-0.3 * li)
    f32 = jnp.float32
    lam = (jnp.exp(jnp.sum(lp['lambda_q1'].astype(f32) * lp['lambda_k1'].astype(f32)))
           - jnp.exp(jnp.sum(lp['lambda_q2'].astype(f32) * lp['lambda_k2'].astype(f32))) + lam_init)
    o_a = diff_attention(q, k, v, lam)
    o_a = (rms_norm(o_a, lp['subln_w']) * (1.0 - lam_init)).reshape(B, T, W_A).astype(x.dtype)

    o_r, s_fin = rwkv_mixer(rr, kr, vr, gr, wl, al, s0, lp)

    gate_a, gate_r = jnp.split(jax.nn.sigmoid(bg), 2, axis=-1)
    merged = gate_a * (o_a @ lp['w_attn_br']) + gate_r * (o_r @ lp['w_rwkv_br'])
    x = x + g1 * (merged @ lp['w_out'])

    h2 = rms_norm(x, lp['norm2_w']) * (1.0 + sc2) + sh2
    u, gt = jnp.split(h2 @ lp['w_ffn_in'], 2, axis=-1)
    x = x + g2 * ((jax.nn.silu(u) * gt) @ lp['w_ffn_out'])
    return x, own_k, own_v, s_fin


def setup_inputs(seed: int = 0) -> dict:
    key = jax.random.key(seed)
    ks = iter(jax.random.split(key, 48))

    def nrm(shape, scale):
        return jax.random.normal(next(ks), shape, jnp.float32) * scale

    L = DEPTH
    return {
        'x_prompt': nrm((BATCH, SEQ, D_MODEL), 1.0),
        'x_sample': nrm((DEC_BATCH, DEC_SEQ, D_MODEL), 1.0),
        'cache_k': nrm((DEC_BATCH, L, PAST_LEN, N_HEADS_A, 2, HD_A), 1.0),
        'cache_v': nrm((DEC_BATCH, L, PAST_LEN, N_HEADS_A, VD_A), 1.0),
        'state_rwkv': nrm((DEC_BATCH, L, 2, N_HEADS_R, HD_R, HD_R), 1.0),
        'c': nrm((DEC_BATCH, D_MODEL), 1.0),
        'c_ctx': nrm((D_MODEL,), 1.0),
        'ada_w': nrm((L, D_MODEL, 6 * D_MODEL), 0.5 * D_MODEL ** -0.5),
        'ada_b': nrm((L, 6 * D_MODEL), 0.02),
        'norm1_w': 1.0 + nrm((L, D_MODEL), 0.02),
        'norm2_w': 1.0 + nrm((L, D_MODEL), 0.02),
        'w_in': nrm((L, D_MODEL, IN_COLS), D_MODEL ** -0.5),
        'q_norm_w': 1.0 + nrm((L, HD_A), 0.02),
        'k_norm_w': 1.0 + nrm((L, HD_A), 0.02),
        'lambda_q1': nrm((L, HD_A), 0.1),
        'lambda_k1': nrm((L, HD_A), 0.1),
        'lambda_q2': nrm((L, HD_A), 0.1),
        'lambda_k2': nrm((L, HD_A), 0.1),
        'subln_w': 1.0 + nrm((L, VD_A), 0.02),
        'w_lora_up': nrm((L, 2, LORA_W, W_R), 0.1),
        'w0': jax.random.uniform(next(ks), (L, 2, W_R), jnp.float32, minval=-5.0, maxval=0.5),
        'a_lora_up': nrm((L, 2, LORA_A, W_R), 0.1),
        'a0': nrm((L, 2, W_R), 0.1),
        'k_k': 0.85 + nrm((L, W_R), 0.02),
        'k_a': 1.0 + nrm((L, W_R), 0.02),
        'r_k': nrm((L, N_HEADS_R, HD_R), 0.1),
        'ln_x_w': 1.0 + nrm((L, W_R), 0.02),
        'ln_x_b': nrm((L, W_R), 0.02),
        'w_attn_br': nrm((L, W_A, D_MODEL), W_A ** -0.5),
        'w_rwkv_br': nrm((L, W_R, D_MODEL), W_R ** -0.5),
        'w_out': nrm((L, D_MODEL, D_MODEL), D_MODEL ** -0.5),
        'w_ffn_in': nrm((L, D_MODEL, 2 * D_FF), D_MODEL ** -0.5),
        'w_ffn_out': nrm((L, D_FF, D_MODEL), D_FF ** -0.5),
    }


def reference(x_prompt, x_sample, cache_k, cache_v, state_rwkv, c, c_ctx, ada_w, ada_b, norm1_w, norm2_w,
              w_in, q_norm_w, k_norm_w, lambda_q1, lambda_k1, lambda_q2, lambda_k2, subln_w, w_lora_up, w0,
              a_lora_up, a0, k_k, k_a, r_k, ln_x_w, ln_x_b, w_attn_br, w_rwkv_br, w_out, w_ffn_in, w_ffn_out):
    y_prompt = x_prompt
    y_sample = x_sample
    ks_out, vs_out, ss_out = [], [], []
    for li in range(DEPTH):
        lp = {
            'ada_w': ada_w[li], 'ada_b': ada_b[li], 'norm1_w': norm1_w[li], 'norm2_w': norm2_w[li],
            'w_in': w_in[li], 'q_norm_w': q_norm_w[li], 'k_norm_w': k_norm_w[li],
            'lambda_q1': lambda_q1[li], 'lambda_k1': lambda_k1[li], 'lambda_q2': lambda_q2[li],
            'lambda_k2': lambda_k2[li], 'subln_w': subln_w[li], 'w_lora_up': w_lora_up[li], 'w0': w0[li],
            'a_lora_up': a_lora_up[li], 'a0': a0[li], 'k_k': k_k[li], 'k_a': k_a[li], 'r_k': r_k[li],
            'ln_x_w': ln_x_w[li], 'ln_x_b': ln_x_b[li], 'w_attn_br': w_attn_br[li], 'w_rwkv_br': w_rwkv_br[li],
            'w_out': w_out[li], 'w_ffn_in': w_ffn_in[li], 'w_ffn_out': w_ffn_out[li],
        }
        y_prompt, k_c, v_c, s_c = trunk_layer(y_prompt, c_ctx, lp, li)
        ks_out.append(k_c)
        vs_out.append(v_c)
        ss_out.append(s_c)
        y_sample, _, _, _ = trunk_layer(y_sample, c, lp, li, cache_k[:, li], cache_v[:, li], state_rwkv[:, li])
    new_k = jnp.stack(ks_out, axis=1)
    new_v = jnp.stack(vs_out, axis=1)
    new_state = jnp.stack(ss_out, axis=1)
    return (y_prompt, y_sample, new_k, new_v, new_state)
```

```python
import math
import numpy as np
import ml_dtypes
import concourse.bass as bass
import concourse.mybir as mybir
from concourse.bass_utils import run_bass_kernel_spmd
from contextlib import ExitStack

F32 = mybir.dt.float32
BF16 = mybir.dt.bfloat16
AF = mybir.ActivationFunctionType
ALU = mybir.AluOpType
AX = mybir.AxisListType

ENGS = ("pe", "act", "dve", "pool", "sp")
NT = 40
NTOK = NT * 128
C0 = math.exp(-0.5)
EPS_RMS = 1e-6
EPS_GN = 64e-5
D_FF = 2816
import os
STAGGER = 0
INTERLEAVE_POST = 1
HQ = 4


class Res:
    __slots__ = ("w", "r", "chan", "psum")

    def __init__(self, psum=False):
        self.w = None
        self.r = []
        self.chan = None
        self.psum = psum


class Sched:
    def __init__(self, nc, stack):
        self.nc = nc
        self.stack = stack
        self.cnt = {}
        self.known = {e: {} for e in ENGS}
        self.sems = {}
        self.nchan = 0
        self.free_chans = []
        self.phase_chans = None
        self.engobj = {"pe": nc.tensor, "act": nc.scalar, "dve": nc.vector, "pool": nc.gpsimd, "sp": nc.sync}
        for e in ("pe", "act", "dve", "pool"):
            self._mksem(e)

    def _mksem(self, key):
        s = self.stack.enter_context(self.nc.semaphore("s_" + str(key)))
        self.sems[key] = s
        self.cnt[key] = 0
        return s

    def chan_of(self, res):
        if res.chan is None:
            if self.free_chans:
                res.chan = self.free_chans.pop()
            else:
                self.nchan += 1
                res.chan = "c%d" % self.nchan
                self._mksem(res.chan)
            if self.phase_chans is not None:
                self.phase_chans.append(res.chan)
        return res.chan

    def begin_phase(self):
        self.phase_chans = []

    def end_phase(self):
        self.barrier()
        self.free_chans.extend(self.phase_chans)
        self.phase_chans = None

    def _deps(self, eng, reads, writes):
        deps = {}

        def add(kv):
            if kv is None:
                return
            k, v = kv
            if deps.get(k, 0) < v:
                deps[k] = v
        for r in reads:
            add(r.w)
            if r.psum:
                for x in r.r:
                    if x[0] != eng:
                        add(x)
        for w in writes:
            add(w.w)
            for x in w.r:
                add(x)
        out = []
        for k, v in deps.items():
            if k == eng and eng == "pe":
                continue
            if self.known[eng].get(k, 0) >= v:
                continue
            self.known[eng][k] = v
            out.append((self.sems[k], v))
        return out

    def op(self, eng, fn, reads=(), writes=()):
        waits = self._deps(eng, reads, writes)
        self.cnt[eng] += 1
        v = self.cnt[eng]
        e = self.engobj[eng]
        for s, val in waits:
            e.wait_ge(s, val)
        fn(e).then_inc(self.sems[eng], 1)
        for r in reads:
            r.r.append((eng, v))
            if len(r.r) > 64:
                r.r = r.r[-48:] if False else r.r
        for w in writes:
            w.w = (eng, v)
            w.r = []

    def dma(self, q, out, in_, reads=(), writes=(), chan_res=None, **kw):
        waits = self._deps(q, reads, writes)
        ch = self.chan_of(chan_res)
        self.cnt[ch] += 16
        v = self.cnt[ch]
        e = self.engobj[q]
        for s, val in waits:
            e.wait_ge(s, val)
        e.dma_start(out=out, in_=in_, **kw).then_inc(self.sems[ch], 16)
        for r in reads:
            r.r.append((ch, v))
        for w in writes:
            w.w = (ch, v)
            w.r = []

    def barrier(self, engs=ENGS):
        for eng in engs:
            e = self.engobj[eng]
            for k, s in self.sems.items():
                v = self.cnt[k]
                if v > 0 and self.known[eng].get(k, 0) < v:
                    if k == eng:
                        continue
                    self.known[eng][k] = v
                    e.wait_ge(s, v)


class Ring:
    def __init__(self, alloc, n, psum=False):
        self.items = [(alloc(i), Res(psum)) for i in range(n)]
        self.i = 0

    def next(self):
        it = self.items[self.i % len(self.items)]
        self.i += 1
        return it


def bc(ap, shape, axis):
    return ap.unsqueeze(axis).to_broadcast(list(shape))


class _Stop(Exception):
    pass


def build_nc(upto=99):
    nc = bass.Bass("TRN2", target_bir_lowering=False)
    try:
        _build(nc, upto)
    except _Stop:
        pass
    return nc


def _build(nc, upto):

    def DT(name, shape, dt, kind):
        return nc.dram_tensor(name, list(shape), dt, kind=kind).ap()

    I = "ExternalInput"
    O = "ExternalOutput"
    xall = DT("xall", [NTOK, 1024], F32, I)
    ck = DT("ck", [256, 512], F32, I)
    cv = DT("cv", [256, 512], F32, I)
    st0 = DT("st0", [16, 64, 64], F32, I)
    condT_d = DT("condT", [128, 16], F32, I)
    colp_d = DT("colp", [128, 64], F32, I)
    rowp_d = DT("rowp", [1, 5120], F32, I)
    cst_d = DT("cst", [128, 2184], F32, I)
    rope_d = DT("rope", [128, 32, 2, 64], F32, I)
    ada_w = DT("ada_w", [1024, 6144], F32, I)
    w_in = DT("w_in", [1024, 5888], F32, I)
    wlu_d = DT("wlu", [128, 512], F32, I)
    alu_d = DT("alu", [128, 512], F32, I)
    w_abr = DT("w_abr", [512, 1024], F32, I)
    w_rbr = DT("w_rbr", [512, 1024], F32, I)
    w_out = DT("w_out", [1024, 1024], F32, I)
    w_f1 = DT("w_f1", [1024, 2 * D_FF], F32, I)
    w_f2 = DT("w_f2", [D_FF, 1024], F32, I)

    yall = DT("yall", [NTOK, 1024], F32, O)
    nk = DT("nk", [1024, 512], F32, O)
    nv = DT("nv", [1024, 512], F32, O)
    ns = DT("ns", [4, 16, 64, 64], F32, O)

    X = "Internal"
    QT_d = DT("QT_d", [4, 128, NTOK], BF16, X)
    KT_d = DT("KT_d", [4, 128, NTOK], BF16, X)
    V_d = DT("V_d", [NTOK, 512], BF16, X)
    TM_d = DT("TM_d", [NTOK, 2, 3, 512], BF16, X)
    VR_d = DT("VR_d", [NTOK, 512], BF16, X)
    FM_d = DT("FM_d", [NT, 2, 4, 512, 128], BF16, X)
    SG_d = DT("SG_d", [NTOK, 512], BF16, X)
    BON_d = DT("BON_d", [NTOK, 512], BF16, X)
    GT_d = DT("GT_d", [16, 128, NTOK], BF16, X)
    ORT_d = DT("ORT_d", [4, 128, NTOK], BF16, X)
    OAT_d = DT("OAT_d", [4, 128, NTOK], BF16, X)
    X1_d = DT("X1_d", [NTOK, 1024], F32, X)
    YF_d = DT("YF_d", [NTOK, 512], F32, X)
    w_in_b = DT("w_in_b", [1024, 5888], BF16, X)
    w_abr_b = DT("w_abr_b", [512, 1024], BF16, X)
    w_rbr_b = DT("w_rbr_b", [512, 1024], BF16, X)
    w_out_b = DT("w_out_b", [1024, 1024], BF16, X)
    w_f1_b = DT("w_f1_b", [1024, 2 * D_FF], BF16, X)
    w_f2_b = DT("w_f2_b", [D_FF, 1024], BF16, X)

    with ExitStack() as st:
        S = Sched(nc, st)

        def chk(level):
            if upto <= level:
                S.barrier()
                raise _Stop()

        def SB(stack, name, shape, dt):
            return stack.enter_context(nc.sbuf_tensor("sb_" + name, list(shape), dt))

        def PS(stack, name, shape, dt=F32):
            return stack.enter_context(nc.psum_tensor("ps_" + name, list(shape), dt))

        r_wcv = {}
        for nm, src, dst, rows in (("w_in", w_in, w_in_b, 1024), ("w_abr", w_abr, w_abr_b, 512), ("w_rbr", w_rbr, w_rbr_b, 512),
                                   ("w_out", w_out, w_out_b, 1024), ("w_f1", w_f1, w_f1_b, 1024), ("w_f2", w_f2, w_f2_b, D_FF)):
            r_ = Res()
            r_wcv[nm] = r_
            step = 256
            for r0 in range(0, rows, step):
                r1 = min(rows, r0 + step)
                S.dma("pool", dst[r0:r1, :], src[r0:r1, :], writes=[], chan_res=r_)
            r_.w = (r_.chan, S.cnt[r_.chan])
        cst = SB(st, "cst", [128, 2184], F32)
        r_cst = Res()
        S.dma("sp", cst[:], cst_d[:, :], writes=[r_cst], chan_res=r_cst)
        identf = cst[:, 0:128]
        def CM(i):
            return cst[:, 128 + i * 128: 256 + i * 128]
        ccol = cst[:, 896:900]
        def MSK(z, a, b):
            o = 904 + z * 640
            return cst[:, o + a: o + b]
        identb = SB(st, "identb", [128, 128], BF16)
        r_idb = Res()
        S.op("dve", lambda e: e.tensor_copy(out=identb[:], in_=identf), reads=[r_cst], writes=[r_idb])
        rowp = SB(st, "rowp", [128, 5120], F32)
        r_rowp = Res()
        S.dma("sp", rowp[:], rowp_d.partition_broadcast(128), writes=[r_rowp], chan_res=r_rowp)
        RP = {}
        o = 0
        for nm, n in (("qnw", 64), ("knw", 64), ("subw", 128), ("k_k", 512), ("k_a", 512), ("r_k", 512),
                      ("lnw", 512), ("lnb", 512), ("lam", 256), ("w0", 1024), ("a0", 1024)):
            RP[nm] = (o, o + n)
            o += n
        def RW(nm, a=0, b=None):
            lo, hi = RP[nm]
            return rowp[:, lo + a: (lo + b) if b is not None else hi]
        ones_f = SB(st, "ones_f", [128, 128], F32)
        r_ones = Res()
        S.op("dve", lambda e: e.memset(ones_f[:], 1.0), writes=[r_ones])
        mod = SB(st, "mod", [128, 48, 2], F32)
        r_mod = Res()
        s1 = SB(st, "s1", [128, 8, 2], F32)
        s2 = SB(st, "s2", [128, 8, 2], F32)
        PCM = SB(st, "PCM", [64, NT, 16, 2], F32)
        r_pcm = [Res() for _ in range(NT)]
        lamc = SB(st, "lamc", [128, 2], F32)
        r_lamc = Res()

        if upto <= -3:
            S.barrier()
            return nc
        S.begin_phase()
        with ExitStack() as ph:
            condT = SB(ph, "condT", [128, 8, 2], F32)
            r_cond = Res()
            S.dma("sp", condT[:], condT_d.rearrange("p (k c) -> p k c", c=2), writes=[r_cond], chan_res=r_cond)
            colp = SB(ph, "colp", [128, 64], F32)
            r_colp = Res()
            S.dma("sp", colp[:], colp_d[:, :], writes=[r_colp], chan_res=r_colp)
            scn = SB(ph, "scn", [128, 8, 2], F32)
            r_scn = Res()
            S.op("act", lambda e: e.activation(out=scn[:], in_=condT[:], func=AF.Silu), reads=[r_cond], writes=[r_scn])
            if upto <= -2:
                S.barrier()
                return nc
            awr = Ring(lambda i: SB(ph, "aw%d" % i, [128, 8, 512], F32), 2)
            pm = PS(ph, "pm", [128, 48, 2])
            r_pm = Res(True)
            for c in range(12):
                aw, r_aw = awr.next()
                S.dma("sp", aw[:], ada_w[:, c * 512:(c + 1) * 512].rearrange("(k p) n -> p k n", p=128),
                      writes=[r_aw], chan_res=r_aw)
                for jj in range(4):
                    j = c * 4 + jj
                    for kc in range(8):
                        S.op("pe", lambda e, aw=aw, jj=jj, j=j, kc=kc: e.matmul(
                            pm[:, j, :], lhsT=aw[:, kc, jj * 128:(jj + 1) * 128], rhs=scn[:, kc, :],
                            start=(kc == 0), stop=(kc == 7)), reads=[r_aw, r_scn], writes=[r_pm])
            if upto <= -1:
                S.barrier()
                return nc
            S.op("dve", lambda e: e.tensor_tensor(out=mod[:], in0=pm[:], in1=bc(colp[:, 16:64], [128, 48, 2], 2),
                                                  op=ALU.add), reads=[r_pm, r_colp], writes=[r_mod])
            S.op("dve", lambda e: e.scalar_tensor_tensor(out=s1[:], in0=mod[:, 8:16, :], scalar=1.0,
                                                         in1=bc(colp[:, 0:8], [128, 8, 2], 2),
                                                         op0=ALU.add, op1=ALU.mult), reads=[r_mod, r_colp], writes=[r_mod])
            S.op("dve", lambda e: e.scalar_tensor_tensor(out=s2[:], in0=mod[:, 32:40, :], scalar=1.0,
                                                         in1=bc(colp[:, 8:16], [128, 8, 2], 2),
                                                         op0=ALU.add, op1=ALU.mult), reads=[r_mod, r_colp], writes=[r_mod])
            if upto <= -0.5:
                S.barrier()
                return nc
            lt = SB(ph, "lt", [128, 128], F32)
            r_lt = Res()
            l2 = SB(ph, "l2", [128, 2], F32)
            S.op("dve", lambda e: e.tensor_tensor(out=lt[:].rearrange("p (a d) -> p a d", a=2),
                                                  in0=RW("lam").rearrange("p (a b d) -> p a b d", a=2, b=2)[:, :, 0, :],
                                                  in1=RW("lam").rearrange("p (a b d) -> p a b d", a=2, b=2)[:, :, 1, :],
                                                  op=ALU.mult), reads=[r_rowp], writes=[r_lt])
            S.op("dve", lambda e: e.tensor_reduce(out=l2[:], in_=lt[:].rearrange("p (a d) -> p a d", a=2),
                                                  axis=AX.X, op=ALU.add), reads=[r_lt], writes=[r_lt])
            if upto <= -0.3:
                S.barrier()
                return nc
            l3 = SB(ph, "l3", [128, 2], F32)
            S.op("act", lambda e: e.activation(out=l3[:], in_=l2[:], func=AF.Exp), reads=[r_lt], writes=[r_lamc])
            if upto <= -0.2:
                S.barrier()
                return nc
            S.op("dve", lambda e: e.tensor_scalar(out=lamc[:, 0:1], in0=l3[:, 1:2], scalar1=l3[:, 0:1], scalar2=-0.2,
                                                  op0=ALU.subtract, op1=ALU.add), reads=[r_lamc], writes=[r_lamc])
            S.end_phase()
        if upto <= 0:
            S.barrier()
            return nc

        S.begin_phase()
        with ExitStack() as ph:
            roper = Ring(lambda i: SB(ph, "ropet%d" % i, [128, 4, 2, 64], F32), 2)
            wlu = SB(ph, "wlu", [128, 512], F32)
            alu = SB(ph, "alu", [128, 512], F32)
            r_lu = Res()
            S.dma("sp", wlu[:], wlu_d[:, :], writes=[r_lu], chan_res=r_lu)
            r_lu2 = Res()
            S.dma("sp", alu[:], alu_d[:, :], writes=[r_lu2], chan_res=r_lu2)
            oma = SB(ph, "oma", [128, 512], F32)
            r_oma = Res()
            S.op("dve", lambda e: e.tensor_scalar(out=oma[:], in0=RW("k_a"), scalar1=-1.0, scalar2=1.0,
                                                  op0=ALU.mult, op1=ALU.add), reads=[r_rowp], writes=[r_oma])
            xr = Ring(lambda i: SB(ph, "x%d" % i, [128, 1024], F32), 2)
            xb = SB(ph, "xb", [128, 1024], BF16)
            r_xb = Res()
            junk = xb
            r_junk = r_xb
            st4 = Ring(lambda i: SB(ph, "st4_%d" % i, [128, 4], F32), 4)
            hTr = Ring(lambda i: SB(ph, "hT%d" % i, [128, 8, 512], BF16), 2)
            wr = Ring(lambda i: SB(ph, "w%d" % i, [128, 8, 512], BF16), 3)
            wlT = SB(ph, "wlT", [128, 512], F32)
            alT = SB(ph, "alT", [128, 512], F32)
            r_wlT = Res()
            r_alT = Res()
            rsb = SB(ph, "rsb", [128, 4, 512], F32)
            ksb = SB(ph, "ksb", [128, 4, 512], F32)
            vsb = SB(ph, "vsb", [128, 4, 512], F32)
            r_rsb = [Res() for _ in range(4)]
            r_ksb = [Res() for _ in range(4)]
            r_vsb = [Res() for _ in range(4)]
            tf = Ring(lambda i: SB(ph, "tf%d" % i, [128, 512], F32), 18)
            kkrk = Ring(lambda i: SB(ph, "kkrk%d" % i, [128, 512], F32), 2)
            tb = Ring(lambda i: SB(ph, "tb%d" % i, [128, 512], BF16), 8)
            tmz = Ring(lambda i: SB(ph, "tmz%d" % i, [128, 3, 512], BF16), 3)
            fmz = Ring(lambda i: SB(ph, "fmz%d" % i, [128, 16, 128], BF16), 3)
            qkst = Ring(lambda i: SB(ph, "qkst%d" % i, [128, 4, 512], BF16), 2)
            gst = Ring(lambda i: SB(ph, "gst%d" % i, [128, 512], BF16), 2)
            vbr = Ring(lambda i: SB(ph, "vbr%d" % i, [128, 512], BF16), 2)
            vfr = Ring(lambda i: SB(ph, "vfr%d" % i, [128, 512], F32), 1)
            s8 = Ring(lambda i: SB(ph, "s8_%d" % i, [128, 8], F32), 8)
            pT = PS(ph, "pT", [128, 8, 128], BF16)
            r_pT = Res(True)
            pr = Ring(lambda i: PS(ph, "pr%d" % i, [128, 512]), 6, psum=True)
            pcc = PS(ph, "pcc", [64, 16, 2])
            r_pcc = Res(True)

            def rstd_from(ssum, r_ss, n, eps):
                S.op("dve", lambda e: e.tensor_scalar(out=ssum, in0=ssum, scalar1=1.0 / n, scalar2=eps,
                                                      op0=ALU.mult, op1=ALU.add), reads=[r_ss], writes=[r_ss])
                S.op("act", lambda e: e.activation(out=ssum, in_=ssum, func=AF.Sqrt), reads=[r_ss], writes=[r_ss])
                S.op("dve", lambda e: e.reciprocal(out=ssum, in_=ssum), reads=[r_ss], writes=[r_ss])

            def wload(c0, ncol):
                w, r_w = wr.next()
                S.dma("pool", w[:, :, 0:ncol], w_in_b[:, c0:c0 + ncol].rearrange("(k p) n -> p k n", p=128),
                      reads=[r_wcv["w_in"]], writes=[r_w], chan_res=r_w)
                return w, r_w

            def xnorm_gen(blk, out):
                cnd = 0 if blk < 8 else 1
                hT, r_hT = hTr.next()
                out["hT"] = (hT, r_hT)
                for ti in range(4):
                    g = blk * 4 + ti
                    xt, r_xt = xr.next()
                    S.dma("sp", xt[:], xall[g * 128:(g + 1) * 128, :], writes=[r_xt], chan_res=r_xt)
                    ss, r_ss = st4.next()
                    S.op("act", lambda e, xt=xt, ss=ss: e.activation(out=junk[:], in_=xt[:], func=AF.Square,
                                                                     accum_out=ss[:, 0:1]),
                         reads=[r_xt], writes=[r_junk, r_ss])
                    rstd_from(ss[:, 0:1], r_ss, 1024, EPS_RMS)
                    S.op("act", lambda e, xt=xt, ss=ss: e.activation(out=xb[:], in_=xt[:], func=AF.Copy, scale=ss[:, 0:1]),
                         reads=[r_xt, r_ss], writes=[r_xb])
                    yield
                    for kc in range(8):
                        S.op("pe", lambda e, kc=kc: e.transpose(out=pT[:, kc, :], in_=xb[:, kc * 128:(kc + 1) * 128],
                                                                identity=identb[:]),
                             reads=[r_xb, r_idb], writes=[r_pT])
                    for kc in range(8):
                        S.op("dve", lambda e, kc=kc, hT=hT, ti=ti: e.tensor_scalar(
                            out=hT[:, kc, ti * 128:(ti + 1) * 128], in0=pT[:, kc, :],
                            scalar1=s1[:, kc, cnd:cnd + 1], scalar2=mod[:, kc, cnd:cnd + 1],
                            op0=ALU.mult, op1=ALU.add), reads=[r_pT, r_mod], writes=[r_hT])
                    yield

            def drain(gen):
                if gen is None:
                    return
                for _ in gen:
                    pass

            xo = {}
            drain(xnorm_gen(0, xo))
            for blk in range(10):
                latent = blk < 8
                cnd = 0 if latent else 1
                hT, r_hT = xo["hT"]
                xo = {}
                xg = xnorm_gen(blk + 1, xo) if blk + 1 < 10 else None
                if latent:
                    ropet, r_rope = roper.next()
                    S.dma("sp", ropet[:], rope_d[:, blk * 4:(blk + 1) * 4, :, :], writes=[r_rope], chan_res=r_rope)

                chk(0.1)
                def mm_tok(w, r_w, ti, ncol=512, wc0=0):
                    p, r_p = pr.next()
                    for kc in range(8):
                        S.op("pe", lambda e, kc=kc, p=p: e.matmul(p[:, 0:ncol], lhsT=hT[:, kc, ti * 128:(ti + 1) * 128],
                                                                  rhs=w[:, kc, wc0:wc0 + ncol], start=(kc == 0), stop=(kc == 7)),
                             reads=[r_hT, r_w], writes=[r_p])
                    return p, r_p

                def mm_feat(w, r_w, wc0):
                    p, r_p = pr.next()
                    for kc in range(8):
                        S.op("pe", lambda e, kc=kc, p=p: e.matmul(p[:, :], lhsT=w[:, kc, wc0:wc0 + 128], rhs=hT[:, kc, :],
                                                                  start=(kc == 0), stop=(kc == 7)),
                             reads=[r_hT, r_w], writes=[r_p])
                    return p, r_p

                w, r_w = wload(3584, 256)
                p, r_p = mm_feat(w, r_w, 0)
                S.op("act", lambda e, p=p: e.activation(out=wlT[:], in_=p[:], func=AF.Tanh), reads=[r_p], writes=[r_wlT])
                p, r_p = mm_feat(w, r_w, 128)
                S.op("act", lambda e, p=p: e.activation(out=alT[:], in_=p[:], func=AF.Copy), reads=[r_p], writes=[r_alT])
                chk(0.2)
                w, r_w = wload(1536, 512)
                for ti in range(4):
                    p, r_p = mm_tok(w, r_w, ti)
                    S.op("act", lambda e, p=p, ti=ti: e.activation(out=rsb[:, ti, :], in_=p[:], func=AF.Copy),
                         reads=[r_p], writes=[r_rsb[ti]])
                chk(0.22)
                w, r_w = wload(2048, 512)
                for ti in range(4):
                    p, r_p = mm_tok(w, r_w, ti)
                    S.op("act", lambda e, p=p, ti=ti: e.activation(out=ksb[:, ti, :], in_=p[:], func=AF.Copy),
                         reads=[r_p], writes=[r_ksb[ti]])
                chk(0.24)
                w, r_w = wload(2560, 512)
                for ti in range(4):
                    g = blk * 4 + ti
                    p, r_p = mm_tok(w, r_w, ti)
                    S.op("act", lambda e, p=p, ti=ti: e.activation(out=vsb[:, ti, :], in_=p[:], func=AF.Copy),
                         reads=[r_p], writes=[r_vsb[ti]])
                    vb, r_vb = tb.next()
                    S.op("dve", lambda e, ti=ti, vb=vb: e.tensor_copy(out=vb[:], in_=vsb[:, ti, :]), reads=[r_vsb[ti]], writes=[r_vb])
                    S.dma("sp", VR_d[g * 128:(g + 1) * 128, :], vb[:], reads=[r_vb], chan_res=r_vb)
                chk(0.26)
                w, r_w = wload(3072, 512)
                for ti in range(4):
                    g = blk * 4 + ti
                    p, r_p = mm_tok(w, r_w, ti)
                    sg, r_sg = tb.next()
                    S.op("act", lambda e, p=p, sg=sg: e.activation(out=sg[:], in_=p[:], func=AF.Sigmoid),
                         reads=[r_p], writes=[r_sg])
                    S.dma("sp", SG_d[g * 128:(g + 1) * 128, :], sg[:], reads=[r_sg], chan_res=r_sg)

                def filler():
                    w, r_w = wload(1024, 512)
                    for ti in range(4):
                        g = blk * 4 + ti
                        p, r_p = mm_tok(w, r_w, ti)
                        vb, r_vb = vbr.next()
                        S.op("act", lambda e, p=p, vb=vb: e.activation(out=vb[:], in_=p[:], func=AF.Copy), reads=[r_p], writes=[r_vb])
                        S.dma("sp", V_d[g * 128:(g + 1) * 128, :], vb[:], reads=[r_vb], chan_res=r_vb)
                        if not latent:
                            vf, r_vf = vfr.next()
                            S.op("act", lambda e, p=p, vf=vf: e.activation(out=vf[:], in_=p[:], func=AF.Copy), reads=[r_p], writes=[r_vf])
                            S.dma("sp", nv[(g - 32) * 128:(g - 31) * 128, :], vf[:], reads=[r_vf], chan_res=r_vf)
                        yield
                    for c in range(4):
                        w, r_w = wload(3840 + c * 512, 512)
                        for jj in range(4):
                            j = c * 4 + jj
                            p, r_p = mm_feat(w, r_w, jj * 128)
                            gs, r_gs = gst.next()
                            S.op("act", lambda e, p=p, gs=gs: e.activation(out=gs[:], in_=p[:], func=AF.Sigmoid), reads=[r_p], writes=[r_gs])
                            S.dma("sp", GT_d[j, :, blk * 512:(blk + 1) * 512], gs[:], reads=[r_gs], chan_res=r_gs)
                            yield
                fill = filler()
                chk(0.3)
                for ti in range(4):
                    g = blk * 4 + ti
                    rt = rsb[:, ti, :]
                    kt = ksb[:, ti, :]
                    vt = vsb[:, ti, :]
                    kkr, r_kkr = tf.next()
                    S.op("dve", lambda e, kkr=kkr: e.tensor_tensor(out=kkr[:], in0=kt, in1=RW("k_k"), op=ALU.mult),
                         reads=[r_ksb[ti], r_rowp], writes=[r_kkr])
                    sq, r_sq = tf.next()
                    S.op("act", lambda e, sq=sq, kkr=kkr: e.activation(out=sq[:], in_=kkr[:], func=AF.Square),
                         reads=[r_kkr], writes=[r_sq])
                    rn, r_rn = s8.next()
                    S.op("dve", lambda e, sq=sq, rn=rn: e.tensor_reduce(out=rn[:], in_=sq[:].rearrange("p (h d) -> p h d", h=8),
                                                                        axis=AX.X, op=ALU.add), reads=[r_sq], writes=[r_rn])
                    S.op("act", lambda e, rn=rn: e.activation(out=rn[:], in_=rn[:], func=AF.Sqrt), reads=[r_rn], writes=[r_rn])
                    S.op("dve", lambda e, rn=rn: e.tensor_scalar(out=rn[:], in0=rn[:], scalar1=1e-12, scalar2=None,
                                                                 op0=ALU.max), reads=[r_rn], writes=[r_rn])
                    S.op("dve", lambda e, rn=rn: e.reciprocal(out=rn[:], in_=rn[:]), reads=[r_rn], writes=[r_rn])
                    kk, r_kk = kkrk.next()
                    S.op("dve", lambda e, kk=kk, kkr=kkr, rn=rn: e.tensor_tensor(
                        out=kk[:].rearrange("p (h d) -> p h d", h=8), in0=kkr[:].rearrange("p (h d) -> p h d", h=8),
                        in1=bc(rn[:], [128, 8, 64], 2), op=ALU.mult), reads=[r_kkr, r_rn], writes=[r_kk])
                    rk, r_rk = kkrk.next()
                    S.op("dve", lambda e, rk=rk: e.tensor_tensor(out=rk[:], in0=rt, in1=RW("r_k"), op=ALU.mult),
                         reads=[r_rsb[ti], r_rowp], writes=[r_rk])
                    chk(0.4)
                    sz = []
                    def zprep(z):
                        pw, r_pw = pr.next()
                        S.op("pe", lambda e, pw=pw, z=z: e.matmul(pw[:], lhsT=wlT[z * 64:(z + 1) * 64, ti * 128:(ti + 1) * 128],
                                                                  rhs=wlu[z * 64:(z + 1) * 64, :], start=True, stop=False),
                             reads=[r_wlT, r_lu], writes=[r_pw])
                        S.op("pe", lambda e, pw=pw, z=z: e.matmul(pw[:], lhsT=ones_f[0:1, :],
                                                                  rhs=RW("w0", z * 512, (z + 1) * 512)[0:1, :],
                                                                  start=False, stop=True),
                             reads=[r_ones, r_rowp], writes=[r_pw])
                        sgw, r_sgw = tf.next()
                        S.op("act", lambda e, pw=pw, sgw=sgw: e.activation(out=sgw[:], in_=pw[:], func=AF.Sigmoid),
                             reads=[r_pw], writes=[r_sgw])
                        yield
                        pa, r_pa = pr.next()
                        S.op("pe", lambda e, pa=pa, z=z: e.matmul(pa[:], lhsT=alT[z * 64:(z + 1) * 64, ti * 128:(ti + 1) * 128],
                                                                  rhs=alu[z * 64:(z + 1) * 64, :], start=True, stop=False),
                             reads=[r_alT, r_lu2], writes=[r_pa])
                        S.op("pe", lambda e, pa=pa, z=z: e.matmul(pa[:], lhsT=ones_f[0:1, :],
                                                                  rhs=RW("a0", z * 512, (z + 1) * 512)[0:1, :],
                                                                  start=False, stop=True),
                             reads=[r_ones, r_rowp], writes=[r_pa])
                        az, r_az = tf.next()
                        S.op("act", lambda e, pa=pa, az=az: e.activation(out=az[:], in_=pa[:], func=AF.Sigmoid),
                             reads=[r_pa], writes=[r_az])
                        yield
                        p1, r_p1 = pr.next()
                        S.op("pe", lambda e, p1=p1, z=z, sgw=sgw: e.matmul(p1[:], lhsT=CM(3 * z + 0), rhs=sgw[:], start=True, stop=True),
                             reads=[r_cst, r_sgw], writes=[r_p1])
                        p0, r_p0 = pr.next()
                        S.op("pe", lambda e, p0=p0, z=z, sgw=sgw: e.matmul(p0[:], lhsT=CM(3 * z + 1), rhs=sgw[:], start=True, stop=True),
                             reads=[r_cst, r_sgw], writes=[r_p0])
                        p2, r_p2 = pr.next()
                        S.op("pe", lambda e, p2=p2, z=z, sgw=sgw: e.matmul(p2[:], lhsT=CM(3 * z + 2), rhs=sgw[:], start=True, stop=True),
                             reads=[r_cst, r_sgw], writes=[r_p2])
                        for h in range(8):
                            S.op("pe", lambda e, z=z, h=h, sgw=sgw: e.matmul(pcc[:, z * 8 + h, :], lhsT=sgw[:, h * 64:(h + 1) * 64],
                                                                             rhs=ccol[:, 2 * z:2 * z + 2], start=True, stop=True),
                                 reads=[r_cst, r_sgw], writes=[r_pcc])
                        E1, r_E1 = tf.next()
                        S.op("act", lambda e, p1=p1, E1=E1: e.activation(out=E1[:], in_=p1[:], func=AF.Exp), reads=[r_p1], writes=[r_E1])
                        Ei, r_Ei = tf.next()
                        S.op("act", lambda e, p1=p1, Ei=Ei: e.activation(out=Ei[:], in_=p1[:], func=AF.Exp, scale=-1.0),
                             reads=[r_p1], writes=[r_Ei])
                        E0, r_E0 = tf.next()
                        S.op("act", lambda e, p0=p0, E0=E0: e.activation(out=E0[:], in_=p0[:], func=AF.Exp), reads=[r_p0], writes=[r_E0])
                        Et, r_Et = tf.next()
                        S.op("act", lambda e, p2=p2, Et=Et: e.activation(out=Et[:], in_=p2[:], func=AF.Exp), reads=[r_p2], writes=[r_Et])
                        yield
                        chk(0.5)
                        kd, r_kd = tf.next()
                        S.op("dve", lambda e, kd=kd, az=az: e.tensor_tensor(out=kd[:], in0=az[:], in1=RW("k_a"), op=ALU.mult),
                             reads=[r_az, r_rowp], writes=[r_kd])
                        S.op("dve", lambda e, kd=kd: e.tensor_tensor(out=kd[:], in0=kd[:], in1=oma[:], op=ALU.add),
                             reads=[r_kd, r_oma], writes=[r_kd])
                        S.op("dve", lambda e, kd=kd: e.tensor_tensor(out=kd[:], in0=kd[:], in1=kt, op=ALU.mult),
                             reads=[r_kd, r_ksb[ti]], writes=[r_kd])
                        S.op("dve", lambda e, az=az, kk=kk: e.tensor_tensor(out=az[:], in0=az[:], in1=kk[:], op=ALU.mult),
                             reads=[r_az, r_kk], writes=[r_az])
                        yield
                        tm, r_tm = tmz.next()
                        S.op("dve", lambda e, tm=tm, kk=kk, E0=E0: e.tensor_tensor(out=tm[:, 0, :], in0=kk[:], in1=E0[:], op=ALU.mult),
                             reads=[r_kk, r_E0], writes=[r_tm])
                        S.op("dve", lambda e, tm=tm, kd=kd, Et=Et: e.tensor_tensor(out=tm[:, 1, :], in0=kd[:], in1=Et[:], op=ALU.mult),
                             reads=[r_kd, r_Et], writes=[r_tm])
                        S.op("dve", lambda e, tm=tm, az=az, Et=Et: e.tensor_tensor(out=tm[:, 2, :], in0=az[:], in1=Et[:], op=ALU.mult),
                             reads=[r_az, r_Et], writes=[r_tm])
                        S.dma("sp", TM_d[g * 128:(g + 1) * 128, z, :, :], tm[:], reads=[r_tm], chan_res=r_tm)
                        yield
                        hb = []
                        for (a_, b_, ra, rb) in ((rt, E1, r_rsb[ti], r_E1), (az[:], Ei, r_az, r_Ei), (kd[:], Ei, r_kd, r_Ei)):
                            t_, r_t = tb.next()
                            S.op("dve", lambda e, t_=t_, a_=a_, b_=b_: e.tensor_tensor(out=t_[:], in0=a_, in1=b_[:], op=ALU.mult),
                                 reads=[ra, rb], writes=[r_t])
                            hb.append((t_, r_t))
                        yield
                        chk(0.6)
                        fm, r_fm = fmz.next()
                        srcs = [(tm[:, 0, :], r_tm), (hb[0][0][:], hb[0][1]), (hb[1][0][:], hb[1][1]), (hb[2][0][:], hb[2][1])]
                        for half in range(2):
                            for qi in range(2):
                                src, r_src = srcs[half * 2 + qi]
                                for cb in range(4):
                                    S.op("pe", lambda e, src=src, cb=cb, qi=qi: e.transpose(
                                        out=pT[:, qi * 4 + cb, :], in_=src[:, cb * 128:(cb + 1) * 128], identity=identb[:]),
                                        reads=[r_src, r_idb], writes=[r_pT])
                            S.op("act", lambda e, fm=fm, half=half: e.activation(out=fm[:, half * 8:(half + 1) * 8, :], in_=pT[:],
                                                                                 func=AF.Copy), reads=[r_pT], writes=[r_fm])
                        S.dma("sp", FM_d[g, z].rearrange("t (cb p) k -> p (t cb) k", p=128), fm[:], reads=[r_fm], chan_res=r_fm)
                        yield
                        S.op("dve", lambda e, kd=kd, rk=rk: e.tensor_tensor(out=kd[:], in0=kd[:], in1=rk[:], op=ALU.mult),
                             reads=[r_kd, r_rk], writes=[r_kd])
                        s_, r_s = s8.next()
                        S.op("dve", lambda e, kd=kd, s_=s_: e.tensor_reduce(out=s_[:], in_=kd[:].rearrange("p (h d) -> p h d", h=8),
                                                                            axis=AX.X, op=ALU.add), reads=[r_kd], writes=[r_s])
                        sz.append((s_, r_s))
                        yield
                    gens_ = [zprep(0), zprep(1)]
                    rnd = 0
                    while gens_:
                        for gn_ in list(gens_):
                            try:
                                next(gn_)
                            except StopIteration:
                                gens_.remove(gn_)
                        rnd += 1
                        if fill is not None and rnd % 2 == 0:
                            try:
                                next(fill)
                            except StopIteration:
                                fill = None
                    S.op("act", lambda e, g=g: e.activation(out=PCM[:, g, :, :], in_=pcc[:], func=AF.Exp),
                         reads=[r_pcc], writes=[r_pcm[g]])
                    S.op("dve", lambda e: e.tensor_tensor(out=sz[0][0][:], in0=sz[0][0][:], in1=sz[1][0][:], op=ALU.add),
                         reads=[sz[0][1], sz[1][1]], writes=[sz[0][1]])
                    bon, r_bon = tb.next()
                    S.op("dve", lambda e, bon=bon: e.tensor_tensor(out=bon[:].rearrange("p (h d) -> p h d", h=8),
                                                                   in0=vt.rearrange("p (h d) -> p h d", h=8),
                                                                   in1=bc(sz[0][0][:], [128, 8, 64], 2), op=ALU.mult),
                         reads=[r_vsb[ti], sz[0][1]], writes=[r_bon])
                    S.dma("sp", BON_d[g * 128:(g + 1) * 128, :], bon[:], reads=[r_bon], chan_res=r_bon)

                while fill is not None:
                    try:
                        next(fill)
                    except StopIteration:
                        fill = None
                chk(0.7)
                def qkgen(which, tis, w, r_w, qs, r_qs):
                    nwn = "qnw" if which == 0 else "knw"
                    for ti in tis:
                        g = blk * 4 + ti
                        p, r_p = mm_tok(w, r_w, ti)
                        sq, r_sq = tf.next()
                        S.op("act", lambda e, p=p, sq=sq: e.activation(out=sq[:], in_=p[:], func=AF.Square), reads=[r_p], writes=[r_sq])
                        rs, r_rs = s8.next()
                        S.op("dve", lambda e, sq=sq, rs=rs: e.tensor_reduce(out=rs[:], in_=sq[:].rearrange("p (h d) -> p h d", h=8),
                                                                            axis=AX.X, op=ALU.add), reads=[r_sq], writes=[r_rs])
                        rstd_from(rs[:], r_rs, 64, EPS_RMS)
                        yield
                        xw, r_xw = tf.next()
                        S.op("dve", lambda e, p=p, xw=xw: e.tensor_tensor(out=xw[:].rearrange("p (h d) -> p h d", h=8),
                                                                          in0=p[:].rearrange("p (h d) -> p h d", h=8),
                                                                          in1=bc(RW(nwn), [128, 8, 64], 1), op=ALU.mult),
                             reads=[r_p, r_rowp], writes=[r_xw])
                        yield
                        ob, r_ob = tb.next()
                        if latent:
                            t1, r_t1 = tf.next()
                            S.op("dve", lambda e, t1=t1, xw=xw, ti=ti: e.tensor_tensor(
                                out=t1[:].rearrange("p (h d) -> p h d", h=8), in0=xw[:].rearrange("p (h d) -> p h d", h=8),
                                in1=bc(ropet[:, ti, 0, :], [128, 8, 64], 1), op=ALU.mult), reads=[r_xw, r_rope], writes=[r_t1])
                            yield
                            t2, r_t2 = tf.next()
                            for bl in range(2):
                                S.op("dve", lambda e, t2=t2, xw=xw, ti=ti, bl=bl: e.tensor_tensor(
                                    out=t2[:].rearrange("p (h a b i) -> p h a b i", h=8, a=2, b=2)[:, :, :, bl, :],
                                    in0=xw[:].rearrange("p (h a b i) -> p h a b i", h=8, a=2, b=2)[:, :, :, 1 - bl, :],
                                    in1=bc(ropet[:, ti, 1, :].rearrange("p (a b i) -> p a b i", a=2, b=2)[:, :, bl, :], [128, 8, 2, 16], 1),
                                    op=ALU.mult), reads=[r_xw, r_rope], writes=[r_t2])
                            yield
                            S.op("dve", lambda e, t1=t1, t2=t2: e.tensor_tensor(out=t1[:], in0=t1[:], in1=t2[:], op=ALU.add),
                                 reads=[r_t1, r_t2], writes=[r_t1])
                            yield
                            S.op("dve", lambda e, t1=t1, ob=ob, rs=rs: e.tensor_tensor(
                                out=ob[:].rearrange("p (h d) -> p h d", h=8), in0=t1[:].rearrange("p (h d) -> p h d", h=8),
                                in1=bc(rs[:], [128, 8, 64], 2), op=ALU.mult), reads=[r_t1, r_rs], writes=[r_ob])
                        else:
                            S.op("dve", lambda e, xw=xw, rs=rs: e.tensor_tensor(
                                out=xw[:].rearrange("p (h d) -> p h d", h=8), in0=xw[:].rearrange("p (h d) -> p h d", h=8),
                                in1=bc(rs[:], [128, 8, 64], 2), op=ALU.mult), reads=[r_xw, r_rs], writes=[r_xw])
                            if which == 1:
                                S.dma("sp", nk[(g - 32) * 128:(g - 31) * 128, :], xw[:], reads=[r_xw], chan_res=r_xw)
                            S.op("act", lambda e, xw=xw, ob=ob: e.activation(out=ob[:], in_=xw[:], func=AF.Copy),
                                 reads=[r_xw], writes=[r_ob])
                        yield
                        for h in range(4):
                            S.op("pe", lambda e, ob=ob, h=h: e.transpose(out=pT[:, h, :], in_=ob[:, h * 128:(h + 1) * 128],
                                                                         identity=identb[:]), reads=[r_ob, r_idb], writes=[r_pT])
                        S.op("act", lambda e, qs=qs, ti=ti: e.activation(out=qs[:, :, ti * 128:(ti + 1) * 128], in_=pT[:, 0:4, :],
                                                                         func=AF.Copy), reads=[r_pT], writes=[r_qs])
                    yield
                qkw = []
                gens_ = []
                for which in range(2):
                    w, r_w = wload(which * 512, 512)
                    qs, r_qs = qkst.next()
                    qkw.append((qs, r_qs))
                    gens_.append(qkgen(which, [0, 2], w, r_w, qs, r_qs))
                    gens_.append(qkgen(which, [1, 3], w, r_w, qs, r_qs))
                while gens_:
                    for gn_ in list(gens_):
                        try:
                            next(gn_)
                        except StopIteration:
                            gens_.remove(gn_)
                    if xg is not None:
                        try:
                            next(xg)
                        except StopIteration:
                            xg = None
                drain(xg)
                for which in range(2):
                    qs, r_qs = qkw[which]
                    dst = QT_d if which == 0 else KT_d
                    S.dma("sp", dst[:, :, blk * 512:(blk + 1) * 512].rearrange("h p t -> p h t"), qs[:], reads=[r_qs], chan_res=r_qs)
            S.end_phase()
        chk(1)

        S.begin_phase()
        with ExitStack() as ph:
            def mkring(name, shape, dt, n):
                return Ring(lambda i: SB(ph, "%s_%d" % (name, i), shape, dt), n)
            fmr = [mkring("fm%d" % z, [64, 4, 8, 128], BF16, 2) for z in range(2)]
            tmr = [mkring("tm%d" % z, [128, 3, 512], BF16, 2) for z in range(2)]
            vrr = [mkring("vr%d" % z, [128, 512], BF16, 2) for z in range(2)]
            gr = [[mkring("gr%d%d" % (z, q), [128, HQ, 128], BF16, 3) for q in range(8 // HQ)] for z in range(2)]
            sqr = [[mkring("sq%d%d" % (z, q), [128, HQ, 128], BF16, 4) for q in range(8 // HQ)] for z in range(2)]
            wrg = [[mkring("wg%d%d" % (z, q), [128, HQ, 128], BF16, 2) for q in range(8 // HQ)] for z in range(2)]
            avr = [[mkring("av%d%d" % (z, q), [128, HQ, 64], BF16, 1) for q in range(8 // HQ)] for z in range(2)]
            apr = [[mkring("ap%d%d" % (z, q), [128, HQ, 64], BF16, 1) for q in range(8 // HQ)] for z in range(2)]
            vnr = [[mkring("vn%d%d" % (z, q), [128, HQ, 64], BF16, 1) for q in range(8 // HQ)] for z in range(2)]
            mtr = [mkring("mt%d" % z, [64, 8, 64], F32, 2) for z in range(2)]
            gsr = [mkring("gs%d" % z, [64, 8, 64], F32, 2) for z in range(2)]
            rpr = [mkring("rp%d" % z, [64, 8, 128], BF16, 2) for z in range(2)]
            y0r = [mkring("y0%d" % z, [128, 512], F32, 2) for z in range(2)]
            hmr = [mkring("hm%d" % z, [64, 8, 64], F32, 1) for z in range(2)]
            hbr = [mkring("hb%d" % z, [64, 8, 64], BF16, 1) for z in range(2)]
            h1r = [mkring("h1%d" % z, [64, 8, 64], F32, 1) for z in range(2)]
            Hr = [mkring("H%d" % z, [64, 8, 64], F32, 2) for z in range(2)]
            stl = SB(ph, "stl", [64, 16, 64], F32)
            r_stl = Res()
            f5 = mkring("f5", [128, 512], F32, 6)
            b5 = mkring("b5", [128, 512], BF16, 2)
            bsr = mkring("bsr", [128, 512], BF16, 6)
            pend = {}
            yfr = mkring("yf", [128, 512], F32, 2)
            r_yfd = [Res() for _ in range(NT)]
            ydone = {}
            s8d = mkring("s8d", [128, 8], F32, 8)
            ostr = mkring("ost", [128, 4, 128], BF16, 2)
            pTd = PS(ph, "pTd", [128, 4, 128], BF16)
            r_pTd = Res(True)
            prd = Ring(lambda i: PS(ph, "prd%d" % i, [128, 512]), 7, psum=True)

            def v3(t, n=4):
                return t[:].rearrange("p (a b) -> p a b", a=n)

            def loads(g, z, second):
                fm, r_fm = fmr[z].next()
                S.dma("sp", fm[:], FM_d[g, z].rearrange("t (h p) k -> p t h k", p=64), writes=[r_fm], chan_res=r_fm)
                tm, r_tm = tmr[z].next()
                S.dma("sp", tm[:], TM_d[g * 128:(g + 1) * 128, z, :, :], writes=[r_tm], chan_res=r_tm)
                vr, r_vr = vrr[z].next()
                S.dma("sp", vr[:], VR_d[g * 128:(g + 1) * 128, :], writes=[r_vr], chan_res=r_vr)
                L = dict(fm=fm, r_fm=r_fm, tm=tm, r_tm=r_tm, vr=vr, r_vr=r_vr, second=second)
                return L

            def pre(g, z, L):
                fm, r_fm, tm, r_tm, vr, r_vr = L["fm"], L["r_fm"], L["tm"], L["r_tm"], L["vr"], L["r_vr"]
                MT, r_MT = mtr[z].next()
                Gs, r_Gs = gsr[z].next()
                Rp, r_Rp = rpr[z].next()
                Y0, r_Y0 = y0r[z].next()
                QS = tuple(range(8 // HQ))
                v3 = lambda t: t[:, 0:HQ * 128].rearrange("p (a b) -> p a b", a=HQ)
                hsl = [list(range(q * HQ, q * HQ + HQ)) for q in QS]
                st_ = [dict() for _ in QS]

                def gram(q, lt, rt_, off, ring=None):
                    p, r_p = prd.next()
                    for i, h in enumerate(hsl[q]):
                        S.op("pe", lambda e, p=p, i=i, h=h: e.matmul(p[:, i * 128:(i + 1) * 128], lhsT=fm[:, lt, h, :],
                                                                     rhs=fm[:, rt_, h, :], start=True, stop=True),
                             reads=[r_fm], writes=[r_p])
                    o_, r_o = (ring or gr[z][q]).next()
                    S.op("dve", lambda e, p=p, o_=o_: e.tensor_tensor(out=o_[:], in0=v3(p), in1=bc(MSK(z, off, off + 128), [128, HQ, 128], 1),
                                                                     op=ALU.mult), reads=[r_p, r_cst], writes=[r_o])
                    return o_, r_o
                for q in QS:
                    st_[q]["P"] = gram(q, 0, 2, 0, sqr[z][q])
                    st_[q]["Q"] = gram(q, 2, 0, 128, sqr[z][q])
                yield
                for q in QS:
                    st_[q]["ArbT"] = gram(q, 2, 1, 256)
                    st_[q]["AakT"] = gram(q, 3, 0, 384)
                    st_[q]["ArkT"] = gram(q, 3, 1, 512)
                    W, r_W = wrg[z][q].next()
                    Q, r_Q = st_[q]["Q"]
                    S.op("pool", lambda e, W=W, Q=Q: e.tensor_tensor(out=W[:], in0=Q[:], in1=bc(identb[:], [128, HQ, 128], 1), op=ALU.add),
                         reads=[r_Q, r_idb], writes=[r_W])
                    st_[q]["W"] = (W, r_W)
                yield
                for j in range(1, 7):
                    for q in QS:
                        P, r_P = st_[q]["P"]
                        Q, r_Q = st_[q]["Q"]
                        pP, r_pP = prd.next()
                        for i in range(HQ):
                            S.op("pe", lambda e, pP=pP, i=i, P=P, Q=Q: e.matmul(pP[:, i * 128:(i + 1) * 128], lhsT=Q[:, i, :], rhs=P[:, i, :],
                                                                               start=True, stop=True), reads=[r_P, r_Q], writes=[r_pP])
                        Pn, r_Pn = sqr[z][q].next()
                        S.op("act", lambda e, pP=pP, Pn=Pn: e.activation(out=Pn[:], in_=v3(pP), func=AF.Copy), reads=[r_pP], writes=[r_Pn])
                        if j < 6:
                            pQ, r_pQ = prd.next()
                            for i in range(HQ):
                                S.op("pe", lambda e, pQ=pQ, i=i, P=P, Q=Q: e.matmul(pQ[:, i * 128:(i + 1) * 128], lhsT=P[:, i, :], rhs=Q[:, i, :],
                                                                                   start=True, stop=True), reads=[r_P, r_Q], writes=[r_pQ])
                            Qn, r_Qn = sqr[z][q].next()
                            if q % 2 == 0:
                                S.op("act", lambda e, pQ=pQ, Qn=Qn: e.activation(out=Qn[:], in_=v3(pQ), func=AF.Copy), reads=[r_pQ], writes=[r_Qn])
                            else:
                                S.op("dve", lambda e, pQ=pQ, Qn=Qn: e.tensor_copy(out=Qn[:], in_=v3(pQ)), reads=[r_pQ], writes=[r_Qn])
                            st_[q]["Q"] = (Qn, r_Qn)
                        st_[q]["P"] = (Pn, r_Pn)
                    for q in QS:
                        P, r_P = st_[q]["P"]
                        W, r_W = st_[q]["W"]
                        pW, r_pW = prd.next()
                        for i in range(HQ):
                            S.op("pe", lambda e, pW=pW, i=i, P=P, W=W: e.matmul(pW[:, i * 128:(i + 1) * 128], lhsT=P[:, i, :], rhs=W[:, i, :],
                                                                               start=True, stop=True), reads=[r_P, r_W], writes=[r_pW])
                        Wn, r_Wn = wrg[z][q].next()
                        S.op("dve", lambda e, pW=pW, W=W, Wn=Wn: e.tensor_tensor(out=Wn[:], in0=v3(pW), in1=W[:], op=ALU.add),
                             reads=[r_pW, r_W], writes=[r_Wn])
                        st_[q]["W"] = (Wn, r_Wn)
                    yield
                for q in QS:
                    AakT, r_AakT = st_[q]["AakT"]
                    pA, r_pA = prd.next()
                    for i, h in enumerate(hsl[q]):
                        S.op("pe", lambda e, i=i, h=h, pA=pA, AakT=AakT: e.matmul(pA[:, i * 64:(i + 1) * 64], lhsT=AakT[:, i, :], rhs=vr[:, h * 64:(h + 1) * 64],
                                                                                 start=True, stop=True), reads=[r_AakT, r_vr], writes=[r_pA])
                    av, r_av = avr[z][q].next()
                    S.op("act", lambda e, pA=pA, av=av: e.activation(out=av[:], in_=pA[:, 0:HQ * 64].rearrange("p (a b) -> p a b", a=HQ), func=AF.Copy),
                         reads=[r_pA], writes=[r_av])
                    st_[q]["av"] = (av, r_av)
                yield
                for q in QS:
                    W, r_W = st_[q]["W"]
                    av, r_av = st_[q]["av"]
                    pZ, r_pZ = prd.next()
                    for i, h in enumerate(hsl[q]):
                        S.op("pe", lambda e, i=i, h=h, pZ=pZ, W=W: e.matmul(pZ[:, i * 128:i * 128 + 64], lhsT=W[:, i, :],
                                                                          rhs=tm[:, 0, h * 64:(h + 1) * 64], start=True, stop=True),
                             reads=[r_W, r_tm], writes=[r_pZ])
                        S.op("pe", lambda e, i=i, h=h, pZ=pZ, W=W, av=av: e.matmul(pZ[:, i * 128 + 64:(i + 1) * 128], lhsT=W[:, i, :],
                                                                                 rhs=av[:, i, :], start=True, stop=True),
                             reads=[r_W, r_av], writes=[r_pZ])
                    Ap, r_Ap = apr[z][q].next()
                    Vn, r_Vn = vnr[z][q].next()
                    S.op("act", lambda e, pZ=pZ, Ap=Ap: e.activation(out=Ap[:], in_=v3(pZ)[:, :, 0:64], func=AF.Copy), reads=[r_pZ], writes=[r_Ap])
                    S.op("act", lambda e, pZ=pZ, Vn=Vn: e.activation(out=Vn[:], in_=v3(pZ)[:, :, 64:128], func=AF.Copy, scale=-1.0),
                         reads=[r_pZ], writes=[r_Vn])
                    st_[q]["Ap"] = (Ap, r_Ap)
                    st_[q]["Vn"] = (Vn, r_Vn)
                yield
                for q in QS:
                    Ap, r_Ap = st_[q]["Ap"]
                    Vn, r_Vn = st_[q]["Vn"]
                    ArbT, r_ArbT = st_[q]["ArbT"]
                    ArkT, r_ArkT = st_[q]["ArkT"]
                    pM, r_pM = prd.next()
                    for i, h in enumerate(hsl[q]):
                        S.op("pe", lambda e, i=i, h=h, pM=pM, Ap=Ap: e.matmul(pM[0:64, i * 64:(i + 1) * 64], lhsT=Ap[:, i, :], rhs=tm[:, 2, h * 64:(h + 1) * 64],
                                                                             start=True, stop=True), reads=[r_Ap, r_tm], writes=[r_pM])
                    S.op("act", lambda e, pM=pM, q=q: e.activation(out=MT[:, q * HQ:(q + 1) * HQ, :], in_=pM[0:64, 0:HQ * 64].rearrange("p (a b) -> p a b", a=HQ),
                                                                   func=AF.Copy), reads=[r_pM], writes=[r_MT])
                    pR, r_pR = prd.next()
                    for i, h in enumerate(hsl[q]):
                        S.op("pe", lambda e, i=i, h=h, pR=pR, Ap=Ap, ArbT=ArbT: e.matmul(pR[0:64, i * 128:(i + 1) * 128], lhsT=Ap[:, i, :], rhs=ArbT[:, i, :],
                                                                                       start=True, stop=True), reads=[r_Ap, r_ArbT], writes=[r_pR])
                    S.op("dve", lambda e, pR=pR, q=q: e.tensor_tensor(out=Rp[:, q * HQ:(q + 1) * HQ, :], in0=fm[:, 1, q * HQ:(q + 1) * HQ, :],
                                                                      in1=pR[0:64, 0:HQ * 128].rearrange("p (a b) -> p a b", a=HQ), op=ALU.subtract),
                         reads=[r_pR, r_fm], writes=[r_Rp])
                    pG, r_pG = prd.next()
                    for i, h in enumerate(hsl[q]):
                        S.op("pe", lambda e, i=i, h=h, pG=pG: e.matmul(pG[0:64, i * 64:(i + 1) * 64], lhsT=tm[:, 1, h * 64:(h + 1) * 64],
                                                                     rhs=vr[:, h * 64:(h + 1) * 64], start=True, stop=False),
                             reads=[r_tm, r_vr], writes=[r_pG])
                        S.op("pe", lambda e, i=i, h=h, pG=pG, Vn=Vn: e.matmul(pG[0:64, i * 64:(i + 1) * 64], lhsT=tm[:, 2, h * 64:(h + 1) * 64],
                                                                             rhs=Vn[:, i, :], start=False, stop=True),
                             reads=[r_tm, r_Vn], writes=[r_pG])
                    S.op("act", lambda e, pG=pG, q=q: e.activation(out=Gs[:, q * HQ:(q + 1) * HQ, :], in_=pG[0:64, 0:HQ * 64].rearrange("p (a b) -> p a b", a=HQ),
                                                                   func=AF.Copy), reads=[r_pG], writes=[r_Gs])
                    pY0, r_pY0 = prd.next()
                    for i, h in enumerate(hsl[q]):
                        S.op("pe", lambda e, i=i, h=h, pY0=pY0, ArkT=ArkT: e.matmul(pY0[:, i * 64:(i + 1) * 64], lhsT=ArkT[:, i, :], rhs=vr[:, h * 64:(h + 1) * 64],
                                                                                   start=True, stop=False), reads=[r_ArkT, r_vr], writes=[r_pY0])
                        S.op("pe", lambda e, i=i, h=h, pY0=pY0, ArbT=ArbT, Vn=Vn: e.matmul(pY0[:, i * 64:(i + 1) * 64], lhsT=ArbT[:, i, :], rhs=Vn[:, i, :],
                                                                                          start=False, stop=True), reads=[r_ArbT, r_Vn], writes=[r_pY0])
                    S.op("act", lambda e, pY0=pY0, q=q: e.activation(out=Y0[:, q * HQ * 64:(q + 1) * HQ * 64], in_=pY0[:, 0:HQ * 64], func=AF.Copy),
                         reads=[r_pY0], writes=[r_Y0])
                yield
                return dict(MT=MT, r_MT=r_MT, Gs=Gs, r_Gs=r_Gs, Rp=Rp, r_Rp=r_Rp, Y0=Y0, r_Y0=r_Y0)

            def post(g, ysum, r_ysum):
                bon, r_bon = bsr.next()
                S.dma("sp", bon[:], BON_d[g * 128:(g + 1) * 128, :], writes=[r_bon], chan_res=r_bon)
                sg, r_sg = bsr.next()
                S.dma("sp", sg[:], SG_d[g * 128:(g + 1) * 128, :], writes=[r_sg], chan_res=r_sg)
                yf, r_yf = yfr.next()
                S.dma("sp", yf[:], YF_d[g * 128:(g + 1) * 128, :], reads=[r_yfd[g]], writes=[r_yf], chan_res=r_yf)
                yield
                yt, r_yt = ysum, r_ysum
                S.op("pool", lambda e: e.tensor_tensor(out=yt[:], in0=yt[:], in1=yf[:], op=ALU.add),
                     reads=[r_yt, r_yf], writes=[r_yt])
                yield
                sm, r_sm = s8d.next()
                S.op("dve", lambda e: e.tensor_reduce(out=sm[:], in_=yt[:].rearrange("p (h d) -> p h d", h=8), axis=AX.X, op=ALU.add),
                     reads=[r_yt], writes=[r_sm])
                yield
                S.op("dve", lambda e: e.tensor_scalar(out=sm[:], in0=sm[:], scalar1=-1.0 / 64, scalar2=None, op0=ALU.mult),
                     reads=[r_sm], writes=[r_sm])
                yield
                S.op("pool", lambda e: e.tensor_tensor(out=yt[:].rearrange("p (h d) -> p h d", h=8), in0=yt[:].rearrange("p (h d) -> p h d", h=8),
                                                       in1=bc(sm[:], [128, 8, 64], 2), op=ALU.add), reads=[r_yt, r_sm], writes=[r_yt])
                yield
                sq, r_sq = f5.next()
                S.op("act", lambda e: e.activation(out=sq[:], in_=yt[:], func=AF.Square), reads=[r_yt], writes=[r_sq])
                yield
                vv, r_vv = s8d.next()
                S.op("dve", lambda e: e.tensor_reduce(out=vv[:], in_=sq[:].rearrange("p (h d) -> p h d", h=8), axis=AX.X, op=ALU.add),
                     reads=[r_sq], writes=[r_vv])
                yield
                S.op("dve", lambda e: e.tensor_scalar(out=vv[:], in0=vv[:], scalar1=1.0 / 64, scalar2=EPS_GN, op0=ALU.mult, op1=ALU.add),
                     reads=[r_vv], writes=[r_vv])
                yield
                S.op("act", lambda e: e.activation(out=vv[:], in_=vv[:], func=AF.Sqrt), reads=[r_vv], writes=[r_vv])
                yield
                S.op("dve", lambda e: e.reciprocal(out=vv[:], in_=vv[:]), reads=[r_vv], writes=[r_vv])
                yield
                S.op("pool", lambda e: e.tensor_tensor(out=yt[:].rearrange("p (h d) -> p h d", h=8), in0=yt[:].rearrange("p (h d) -> p h d", h=8),
                                                       in1=bc(vv[:], [128, 8, 64], 2), op=ALU.mult), reads=[r_yt, r_vv], writes=[r_yt])
                yield
                S.op("pool", lambda e: e.tensor_tensor(out=yt[:], in0=yt[:], in1=RW("lnw"), op=ALU.mult), reads=[r_yt, r_rowp], writes=[r_yt])
                yield
                S.op("pool", lambda e: e.tensor_tensor(out=yt[:], in0=yt[:], in1=RW("lnb"), op=ALU.add), reads=[r_yt, r_rowp], writes=[r_yt])
                yield
                S.op("pool", lambda e: e.tensor_tensor(out=yt[:], in0=yt[:], in1=bon[:], op=ALU.add), reads=[r_yt, r_bon], writes=[r_yt])
                yield
                ob, r_ob = b5.next()
                S.op("pool", lambda e: e.tensor_tensor(out=ob[:], in0=yt[:], in1=sg[:], op=ALU.mult), reads=[r_yt, r_sg], writes=[r_ob])
                yield
                for cb in range(4):
                    S.op("pe", lambda e, cb=cb: e.transpose(out=pTd[:, cb, :], in_=ob[:, cb * 128:(cb + 1) * 128], identity=identb[:]),
                         reads=[r_ob, r_idb], writes=[r_pTd])
                ost, r_ost = ostr.next()
                S.op("act", lambda e: e.activation(out=ost[:], in_=pTd[:], func=AF.Copy), reads=[r_pTd], writes=[r_ost])
                yield
                S.dma("sp", ORT_d[:, :, g * 128:(g + 1) * 128].rearrange("h p t -> p h t"), ost[:], reads=[r_ost], chan_res=r_ost)
                yield

            def seq(g, z, B, H, r_H):
                Hm, r_Hm = hmr[z].next()
                S.op("dve", lambda e: e.tensor_tensor(out=Hm[:], in0=H[:], in1=bc(PCM[:, g, z * 8:(z + 1) * 8, 1], [64, 8, 64], 2), op=ALU.mult),
                     reads=[r_H, r_pcm[g]], writes=[r_Hm])
                Hb, r_Hb = hbr[z].next()
                S.op("act", lambda e: e.activation(out=Hb[:], in_=Hm[:], func=AF.Copy), reads=[r_Hm], writes=[r_Hb])
                MT, Gs, Rp, Y0 = B["MT"], B["Gs"], B["Rp"], B["Y0"]
                pH, r_pH = prd.next()
                for h in range(8):
                    S.op("pe", lambda e, h=h: e.matmul(pH[0:64, h * 64:(h + 1) * 64], lhsT=MT[:, h, :], rhs=Hm[:, h, :], start=True, stop=True),
                         reads=[B["r_MT"], r_Hm], writes=[r_pH])
                h1, r_h1 = h1r[z].next()
                S.op("dve", lambda e: e.tensor_tensor(out=h1[:], in0=H[:], in1=bc(PCM[:, g, z * 8:(z + 1) * 8, 0], [64, 8, 64], 2), op=ALU.mult),
                     reads=[r_H, r_pcm[g]], writes=[r_h1])
                S.op("dve", lambda e: e.tensor_tensor(out=h1[:], in0=h1[:], in1=pH[0:64, :].rearrange("p (a b) -> p a b", a=8), op=ALU.subtract),
                     reads=[r_h1, r_pH], writes=[r_h1])
                S.op("dve", lambda e: e.tensor_tensor(out=H[:], in0=h1[:], in1=Gs[:], op=ALU.add), reads=[r_h1, B["r_Gs"]], writes=[r_H])
                pY, r_pY = prd.next()
                for h in range(8):
                    S.op("pe", lambda e, h=h: e.matmul(pY[:, h * 64:(h + 1) * 64], lhsT=Rp[:, h, :], rhs=Hb[:, h, :], start=True, stop=True),
                         reads=[B["r_Rp"], r_Hb], writes=[r_pY])
                ys, r_ys = f5.next()
                S.op("dve", lambda e: e.tensor_tensor(out=ys[:], in0=pY[:], in1=Y0[:], op=ALU.add), reads=[r_pY, B["r_Y0"]], writes=[r_ys])
                if g not in ydone:
                    ydone[g] = True
                    S.dma("sp", YF_d[g * 128:(g + 1) * 128, :], ys[:], reads=[r_ys], writes=[r_yfd[g]], chan_res=r_ys)
                else:
                    return post(g, ys, r_ys)
                return None

            segs = [(0, 32, True)] + [(32 + 2 * i, 2, False) for i in range(4)]

            def init_H(z, lat):
                H, r_H = Hr[z].next()
                if lat:
                    if z == 0:
                        S.dma("sp", stl[:], st0.rearrange("a v k -> v a k"), writes=[r_stl], chan_res=r_stl)
                    for hp in range(4):
                        pS, r_pS = prd.next()
                        for i in range(2):
                            h = hp * 2 + i
                            S.op("pe", lambda e, pS=pS, i=i, h=h: e.transpose(out=pS[0:64, i * 64:(i + 1) * 64], in_=stl[:, z * 8 + h, :],
                                                                             identity=identf[0:64, 0:64]), reads=[r_stl, r_cst], writes=[r_pS])
                        S.op("act", lambda e, pS=pS, hp=hp: e.activation(out=H[:, hp * 2:hp * 2 + 2, :],
                                                                         in_=pS[0:64, 0:128].rearrange("p (a b) -> p a b", a=2), func=AF.Copy),
                             reads=[r_pS], writes=[r_H])
                else:
                    S.op("dve", lambda e: e.memset(H[:], 0.0), writes=[r_H])
                return H, r_H

            def final_out(z, si, H, r_H):
                so, r_so = stl[:, 0:8, :], r_stl
                for hp in range(4):
                    pS, r_pS = prd.next()
                    for i in range(2):
                        h = hp * 2 + i
                        S.op("pe", lambda e, pS=pS, i=i, h=h: e.transpose(out=pS[0:64, i * 64:(i + 1) * 64], in_=H[:, h, :],
                                                                         identity=identf[0:64, 0:64]), reads=[r_H, r_cst], writes=[r_pS])
                    S.op("act", lambda e, pS=pS, hp=hp: e.activation(out=so[:, hp * 2:hp * 2 + 2, :],
                                                                     in_=pS[0:64, 0:128].rearrange("p (a b) -> p a b", a=2), func=AF.Copy),
                         reads=[r_pS], writes=[r_so])
                S.dma("sp", ns[si - 1, z * 8:(z + 1) * 8, :, :].rearrange("h v k -> v h k"), so, reads=[r_so], chan_res=r_so)

            def chain(z):
                items = []
                for si, (g0, n, lat) in enumerate(segs):
                    order = list(range(g0, g0 + n)) if z == 0 else list(range(g0 + n - 1, g0 - 1, -1))
                    for i_, g in enumerate(order):
                        items.append((si, g, i_, n, lat))

                pp = [None]

                def drain_post():
                    if pp[0] is not None:
                        for _ in pp[0]:
                            pass
                        pp[0] = None

                def do_seq(p):
                    g, B, (H, r_H), si, last, lat = p
                    drain_post()
                    pp[0] = seq(g, z, B, H, r_H)
                    if last and not lat:
                        final_out(z, si, H, r_H)
                prev = None
                cur = loads(items[0][1], z, items[0][2] * 2 >= items[0][3])
                Hc = None
                for k, (si, g, i_, n, lat) in enumerate(items):
                    if i_ == 0:
                        Hc = init_H(z, lat)
                        yield
                    nxt = None
                    if k + 1 < len(items):
                        nxt = loads(items[k + 1][1], z, items[k + 1][2] * 2 >= items[k + 1][3])
                    pg_ = pre(g, z, cur)
                    while True:
                        try:
                            next(pg_)
                        except StopIteration as e_:
                            B = e_.value
                            break
                        if pp[0] is not None and INTERLEAVE_POST:
                            try:
                                next(pp[0])
                            except StopIteration:
                                pp[0] = None
                        yield
                    cur = nxt
                    if prev is not None:
                        do_seq(prev)
                        yield
                    prev = (g, B, Hc, si, i_ == n - 1, lat)
                do_seq(prev)
                drain_post()
                yield

            gens = [chain(0), chain(1)]
            for _ in range(STAGGER):
                next(gens[0])
            while gens:
                for gn in list(gens):
                    try:
                        next(gn)
                    except StopIteration:
                        gens.remove(gn)
            S.end_phase()
        chk(2)

        S.begin_phase()
        with ExitStack() as ph:
            KT = SB(ph, "KTs", [128, 4, 4352], BF16)
            r_KTk = [Res() for _ in range(34)]
            r_VAk = [Res() for _ in range(34)]
            kch = [Res() for _ in range(4)]
            vch = [Res() for _ in range(8)]
            VA = SB(ph, "VAs", [128, 34, 4, 129], BF16)
            S.op("pool", lambda e: e.memset(VA[:, :, :, 128:129], 1.0), writes=r_VAk)
            qtr = Ring(lambda i: SB(ph, "qt%d" % i, [128, 4, 512], BF16), 2)
            ptr_ = Ring(lambda i: SB(ph, "pt%d" % i, [128, 2, 512], BF16), 3)
            ckf = Ring(lambda i: SB(ph, "ckf%d" % i, [128, 512], F32), 2)
            ckb = Ring(lambda i: SB(ph, "ckb%d" % i, [128, 512], BF16), 2)
            ot = SB(ph, "ot", [128, 4, 4, 128], F32)
            r_ot = Res()
            t128 = Ring(lambda i: SB(ph, "t128_%d" % i, [128, 128], F32), 3)
            rrr = Ring(lambda i: SB(ph, "rr%d" % i, [128, 8], F32), 2)
            accr = Ring(lambda i: SB(ph, "accs%d" % i, [128, 3, 387], F32), 2)
            osq = SB(ph, "osq", [128, 2048], F32)
            r_osq = Res()
            s16 = Ring(lambda i: SB(ph, "s16_%d" % i, [128, 16], F32), 2)
            onb = SB(ph, "onb", [128, 4, 4, 128], BF16)
            r_onb = Res()
            oast = Ring(lambda i: SB(ph, "oast%d" % i, [128, 4, 512], BF16), 2)
            subw8 = SB(ph, "subw8", [128, 128], F32)
            r_subw8 = Res()
            S.op("dve", lambda e: e.tensor_scalar(out=subw8[:], in0=RW("subw"), scalar1=0.8, scalar2=None, op0=ALU.mult),
                 reads=[r_rowp], writes=[r_subw8])
            pTc = PS(ph, "pTc", [128, 4, 128], BF16)
            r_pTc = Res(True)
            psr = Ring(lambda i: PS(ph, "psr%d" % i, [128, 2, 512]), 2, psum=True)
            accb = [PS(ph, "acc%d" % i, [128, 512]) for i in range(3)]
            r_acc = [Res(True) for _ in range(3)]
            slot = {}
            for idx in range(8):
                slot[(idx // 4, idx % 4)] = (idx // 3, (idx % 3) * 129)

            segs = [(0, 4096, True)] + [(4096 + 256 * i, 256, False) for i in range(4)]
            for si_, (t0, T, lat) in enumerate(segs):
                nkt = (T + (256 if lat else 0)) // 128
                koff = 256 if lat else 0
                kb = 0 if lat else 2 * (si_ - 1)
                if lat:
                    for i in range(2):
                        cf, r_cf = ckf.next()
                        S.dma("sp", cf[:], ck[i * 128:(i + 1) * 128, :], writes=[r_cf], chan_res=r_cf)
                        cb_, r_cb = ckb.next()
                        S.op("dve", lambda e, cf=cf, cb_=cb_: e.tensor_copy(out=cb_[:], in_=cf[:]), reads=[r_cf], writes=[r_cb])
                        for h in range(4):
                            S.op("pe", lambda e, cb_=cb_, h=h: e.transpose(out=pTc[:, h, :], in_=cb_[:, h * 128:(h + 1) * 128], identity=identb[:]),
                                 reads=[r_cb, r_idb], writes=[r_pTc])
                        S.op("act", lambda e, i=i: e.activation(out=KT[:, :, i * 128:(i + 1) * 128], in_=pTc[:], func=AF.Copy),
                             reads=[r_pTc], writes=[r_KTk[i]])
                    for kt in range(2):
                        S.dma("pool", VA[:, kt, :, 0:128], cv[kt * 128:(kt + 1) * 128, :].rearrange("p (h e) -> p h e", h=4),
                              writes=[r_VAk[kt], vch[kt % 8]], chan_res=vch[kt % 8])
                k0 = kb + koff // 128
                nown = T // 128
                grp = 8 if lat else 2
                for gi in range(nown // grp):
                    ks = list(range(k0 + gi * grp, k0 + (gi + 1) * grp))
                    S.dma("sp", KT[:, :, ks[0] * 128:(ks[-1] + 1) * 128],
                          KT_d[:, :, t0 + gi * grp * 128:t0 + (gi + 1) * grp * 128].rearrange("h p t -> p h t"),
                          writes=[r_KTk[k] for k in ks] + [kch[gi % 4]], chan_res=kch[gi % 4])
                for kt in range(nown):
                    S.dma("sp", VA[:, k0 + kt, :, 0:128], V_d[t0 + kt * 128:t0 + (kt + 1) * 128, :].rearrange("p (h e) -> p h e", h=4),
                          writes=[r_VAk[k0 + kt], vch[kt % 8]], chan_res=vch[kt % 8])
                QB = min(512, T)
                nqs = QB // 128
                def qload(qb_):
                    q0_ = t0 + qb_ * QB
                    QT_, r_QT_ = qtr.next()
                    S.dma("sp", QT_[:, :, 0:QB], QT_d[:, :, q0_:q0_ + QB].rearrange("h p t -> p h t"), writes=[r_QT_], chan_res=r_QT_)
                    return QT_, r_QT_
                qnext = qload(0)
                for qb in range(T // QB):
                    q0 = t0 + qb * QB
                    QT, r_QT = qnext
                    if qb + 1 < T // QB:
                        qnext = qload(qb + 1)
                    for h in range(4):
                        started = set()

                        def qk(kt):
                            pS, r_pS = psr.next()
                            for m in range(2):
                                S.op("pe", lambda e, pS=pS, m=m: e.matmul(pS[:, m, 0:QB], lhsT=KT[m * 64:(m + 1) * 64, h, (kb + kt) * 128:(kb + kt + 1) * 128],
                                                                          rhs=QT[m * 64:(m + 1) * 64, h, 0:QB], start=True, stop=True),
                                     reads=[r_KTk[kb + kt], r_QT], writes=[r_pS])
                            PT, r_PT = ptr_.next()
                            S.op("act", lambda e, pS=pS, PT=PT: e.activation(out=PT[:, :, 0:QB], in_=pS[:, :, 0:QB], func=AF.Exp, scale=0.125),
                                 reads=[r_pS], writes=[r_PT])
                            return PT, r_PT

                        def av(kt, PT, r_PT):
                            last = (kt == nkt - 1)
                            for m in range(2):
                                for qs in range(nqs):
                                    bk, co = slot[(m, qs)]
                                    first = bk not in started
                                    started.add(bk)
                                    S.op("pe", lambda e, qs=qs, bk=bk, co=co, first=first, m=m: e.matmul(
                                        accb[bk][:, co:co + 129], lhsT=PT[:, m, qs * 128:(qs + 1) * 128], rhs=VA[:, kb + kt, h, :],
                                        start=first, stop=last, skip_group_check=True), reads=[r_PT, r_VAk[kb + kt]], writes=[r_acc[bk]])
                        cur = qk(0)
                        for kt in range(nkt):
                            nxt = qk(kt + 1) if kt + 1 < nkt else None
                            av(kt, cur[0], cur[1])
                            cur = nxt
                        accs, r_accs = accr.next()
                        for bk in range(3):
                            S.op("dve", lambda e, bk=bk, accs=accs: e.tensor_copy(out=accs[:, bk, :], in_=accb[bk][:, 0:387]),
                                 reads=[r_acc[bk]], writes=[r_accs])
                        rr, r_rr = rrr.next()
                        nacc = 4 + nqs
                        av_ = accs[:].rearrange("p a b -> p (a b)")[:, 0:8 * 129].rearrange("p (i c) -> p i c", c=129)
                        S.op("dve", lambda e, rr=rr, av_=av_: e.reciprocal(out=rr[:, 0:nacc], in_=av_[:, 0:nacc, 128]),
                             reads=[r_accs], writes=[r_rr])
                        S.op("dve", lambda e, rr=rr: e.tensor_scalar(out=rr[:, 4:8], in0=rr[:, 4:8], scalar1=lamc[:, 0:1], scalar2=None, op0=ALU.mult),
                             reads=[r_rr, r_lamc], writes=[r_rr])
                        for qs in range(nqs):
                            tt_, r_tt = t128.next()
                            S.op("dve", lambda e, qs=qs, tt_=tt_, av_=av_, rr=rr: e.tensor_scalar(out=tt_[:], in0=av_[:, 4 + qs, 0:128],
                                                                                            scalar1=rr[:, 4 + qs:5 + qs], scalar2=None, op0=ALU.mult),
                                 reads=[r_accs, r_rr], writes=[r_tt])
                            S.op("dve", lambda e, qs=qs, tt_=tt_, av_=av_, rr=rr: e.scalar_tensor_tensor(
                                out=ot[:, qs, h, :], in0=av_[:, qs, 0:128], scalar=rr[:, qs:qs + 1], in1=tt_[:],
                                op0=ALU.mult, op1=ALU.add), reads=[r_accs, r_rr, r_tt], writes=[r_ot])
                    nn = nqs * 4
                    S.op("act", lambda e: e.activation(out=osq[:, 0:nn * 128], in_=ot[:, 0:nqs, :, :].rearrange("p a b c -> p (a b c)"), func=AF.Square),
                         reads=[r_ot], writes=[r_osq])
                    ss, r_ss = s16.next()
                    S.op("dve", lambda e: e.tensor_reduce(out=ss[:, 0:nn], in_=osq[:, 0:nn * 128].rearrange("p (a c) -> p a c", c=128), axis=AX.X, op=ALU.add),
                         reads=[r_osq], writes=[r_ss])
                    S.op("dve", lambda e: e.tensor_scalar(out=ss[:, 0:nn], in0=ss[:, 0:nn], scalar1=1.0 / 128, scalar2=EPS_RMS, op0=ALU.mult, op1=ALU.add),
                         reads=[r_ss], writes=[r_ss])
                    S.op("act", lambda e: e.activation(out=ss[:, 0:nn], in_=ss[:, 0:nn], func=AF.Sqrt), reads=[r_ss], writes=[r_ss])
                    S.op("dve", lambda e: e.reciprocal(out=ss[:, 0:nn], in_=ss[:, 0:nn]), reads=[r_ss], writes=[r_ss])
                    S.op("dve", lambda e: e.tensor_tensor(out=osq[:, 0:nn * 128].rearrange("p (a c) -> p a c", c=128),
                                                          in0=ot[:, 0:nqs, :, :].rearrange("p a b c -> p (a b) c"),
                                                          in1=bc(ss[:, 0:nn], [128, nn, 128], 2), op=ALU.mult), reads=[r_ot, r_ss], writes=[r_osq])
                    S.op("dve", lambda e: e.tensor_tensor(out=onb[:, 0:nqs, :, :].rearrange("p a b c -> p (a b) c"),
                                                          in0=osq[:, 0:nn * 128].rearrange("p (a c) -> p a c", c=128),
                                                          in1=bc(subw8[:], [128, nn, 128], 1), op=ALU.mult), reads=[r_osq, r_subw8], writes=[r_onb])
                    oa, r_oa = oast.next()
                    for qs in range(nqs):
                        for h in range(4):
                            S.op("pe", lambda e, qs=qs, h=h: e.transpose(out=pTc[:, h, :], in_=onb[:, qs, h, :], identity=identb[:]),
                                 reads=[r_onb, r_idb], writes=[r_pTc])
                        S.op("act", lambda e, qs=qs, oa=oa: e.activation(out=oa[:, :, qs * 128:(qs + 1) * 128], in_=pTc[:], func=AF.Copy),
                             reads=[r_pTc], writes=[r_oa])
                    S.dma("sp", OAT_d[:, :, q0:q0 + QB].rearrange("h p t -> p h t"), oa[:, :, 0:QB], reads=[r_oa], chan_res=r_oa)
            S.end_phase()
        chk(3)

        S.begin_phase()
        with ExitStack() as ph:
            wf2 = SB(ph, "wf2", [128, 22, 1024], BF16)
            r_wf2 = Res()
            for c in range(6):
                n = min(4, 22 - c * 4)
                r_tmp = Res()
                S.dma("pool", wf2[:, c * 4:c * 4 + n, :], w_f2_b[c * 512:c * 512 + n * 128, :].rearrange("(k p) n -> p k n", p=128),
                      reads=[r_wcv["w_f2"]], writes=[r_wf2], chan_res=r_wf2)
            wre = Ring(lambda i: SB(ph, "we%d" % i, [128, 8, 512], BF16), 4)
            gbc = SB(ph, "gbc", [128, 2, 1024], F32)
            r_gbc = Res()
            gb = Ring(lambda i: SB(ph, "gb%d" % i, [128, 128], F32), 2)
            aT = SB(ph, "aT", [128, 22, 512], BF16)
            r_aT = Res()
            h2T = SB(ph, "h2T", [128, 8, 512], BF16)
            r_h2T = Res()
            mgT = SB(ph, "mgT", [128, 8, 512], BF16)
            r_mgT = Res()
            oar = SB(ph, "oar", [128, 4, 512], BF16)
            orr = SB(ph, "orr", [128, 4, 512], BF16)
            r_oar = Res()
            r_orr = Res()
            gtl = Ring(lambda i: SB(ph, "gtl%d" % i, [128, 512], BF16), 8)
            f6 = Ring(lambda i: SB(ph, "f6_%d" % i, [128, 512], F32), 3)
            xe = Ring(lambda i: SB(ph, "xe%d" % i, [128, 1024], F32), 4)
            x1r = Ring(lambda i: SB(ph, "x1_%d" % i, [128, 1024], F32), 2)
            xb2 = SB(ph, "xb2", [128, 1024], BF16)
            r_xb2 = Res()
            junk2 = xb2
            r_junk2 = r_xb2
            st5 = Ring(lambda i: SB(ph, "st5_%d" % i, [128, 4], F32), 4)
            pTe = PS(ph, "pTe", [128, 8, 128], BF16)
            r_pTe = Res(True)
            pe_ = Ring(lambda i: PS(ph, "pe%d" % i, [128, 512]), 6, psum=True)
            r_x1d = [Res() for _ in range(NT)]

            def fill_gbc(cnd, wh_):
                for wh, base in (((0, 16),) if wh_ == 0 else ((1, 40),)):
                    for half in range(2):
                        p, r_p = pe_.next()
                        for jj in range(4):
                            j = half * 4 + jj
                            g_, r_g = gb.next()
                            S.op("dve", lambda e, g_=g_, j=j, base=base, cnd=cnd: e.tensor_scalar(
                                out=g_[:], in0=ones_f[:], scalar1=mod[:, base + j, cnd:cnd + 1], scalar2=None, op0=ALU.mult),
                                reads=[r_ones, r_mod], writes=[r_g])
                            S.op("pe", lambda e, p=p, jj=jj, g_=g_: e.matmul(p[:, jj * 128:(jj + 1) * 128], lhsT=g_[:], rhs=identf,
                                                                             start=True, stop=True), reads=[r_g, r_cst], writes=[r_p])
                        S.op("act", lambda e, p=p, cnd=cnd, wh=wh, half=half: e.activation(
                            out=gbc[:, wh, half * 512:(half + 1) * 512], in_=p[:], func=AF.Copy), reads=[r_p], writes=[r_gbc])

            def wl_(nm, shape_k, c0, ncol):
                src = {"w_abr": w_abr_b, "w_rbr": w_rbr_b, "w_out": w_out_b, "w_f1": w_f1_b}[nm]
                w, r_w = wre.next()
                S.dma("pool", w[:, 0:shape_k, 0:ncol], src[:, c0:c0 + ncol].rearrange("(k p) n -> p k n", p=128),
                      reads=[r_wcv[nm]], writes=[r_w], chan_res=r_w)
                return w, r_w

            XT = {}

            def merge_part(blk):
                cnd = 0 if blk < 8 else 1
                tk0 = blk * 512
                S.dma("sp", oar[:], OAT_d[:, :, tk0:tk0 + 512].rearrange("h p t -> p h t"), writes=[r_oar], chan_res=r_oar)
                S.dma("sp", orr[:], ORT_d[:, :, tk0:tk0 + 512].rearrange("h p t -> p h t"), writes=[r_orr], chan_res=r_orr)
                for half in range(2):
                    wa, r_wa = wl_("w_abr", 4, half * 512, 512)
                    wb, r_wb = wl_("w_rbr", 4, half * 512, 512)
                    gl = []
                    for jj in range(4):
                        j = half * 4 + jj
                        ga, r_ga = gtl.next()
                        S.dma("sp", ga[:], GT_d[j, :, tk0:tk0 + 512], writes=[r_ga], chan_res=r_ga)
                        gr_, r_gr = gtl.next()
                        S.dma("sp", gr_[:], GT_d[8 + j, :, tk0:tk0 + 512], writes=[r_gr], chan_res=r_gr)
                        gl.append((ga, r_ga, gr_, r_gr))
                    for jj in range(4):
                        j = half * 4 + jj
                        ga, r_ga, gr_, r_gr = gl[jj]
                        pa_, r_pa = pe_.next()
                        for kc in range(4):
                            S.op("pe", lambda e, pa_=pa_, kc=kc, wa=wa, jj=jj: e.matmul(pa_[:], lhsT=wa[:, kc, jj * 128:(jj + 1) * 128], rhs=oar[:, kc, :],
                                                                                       start=(kc == 0), stop=(kc == 3)), reads=[r_wa, r_oar], writes=[r_pa])
                        pb_, r_pb = pe_.next()
                        for kc in range(4):
                            S.op("pe", lambda e, pb_=pb_, kc=kc, wb=wb, jj=jj: e.matmul(pb_[:], lhsT=wb[:, kc, jj * 128:(jj + 1) * 128], rhs=orr[:, kc, :],
                                                                                       start=(kc == 0), stop=(kc == 3)), reads=[r_wb, r_orr], writes=[r_pb])
                        ta, r_ta = f6.next()
                        S.op("dve", lambda e, ta=ta, pa_=pa_, ga=ga: e.tensor_tensor(out=ta[:], in0=pa_[:], in1=ga[:], op=ALU.mult),
                             reads=[r_pa, r_ga], writes=[r_ta])
                        tb_, r_tb = f6.next()
                        S.op("dve", lambda e, tb_=tb_, pb_=pb_, gr_=gr_: e.tensor_tensor(out=tb_[:], in0=pb_[:], in1=gr_[:], op=ALU.mult),
                             reads=[r_pb, r_gr], writes=[r_tb])
                        S.op("dve", lambda e, ta=ta, tb_=tb_, j=j: e.tensor_tensor(out=mgT[:, j, :], in0=ta[:], in1=tb_[:], op=ALU.add),
                             reads=[r_ta, r_tb], writes=[r_mgT])

            def outproj_part(blk):
                cnd = 0 if blk < 8 else 1
                tk0 = blk * 512
                xts = []
                for ti in range(4):
                    g = blk * 4 + ti
                    xt, r_xt = xe.next()
                    S.dma("sp", xt[:], xall[g * 128:(g + 1) * 128, :], writes=[r_xt], chan_res=r_xt)
                    xts.append((xt, r_xt))
                XT[blk] = xts
                wo = [wl_("w_out", 8, nb * 512, 512) for nb in range(2)]
                for ti in range(4):
                    g = blk * 4 + ti
                    xt, r_xt = xts[ti]
                    x1, r_x1 = x1r.next()
                    for nb in range(2):
                        po, r_po = pe_.next()
                        for j in range(8):
                            S.op("pe", lambda e, po=po, j=j, nb=nb: e.matmul(po[:], lhsT=mgT[:, j, ti * 128:(ti + 1) * 128], rhs=wo[nb][0][:, j, :],
                                                                            start=(j == 0), stop=(j == 7)), reads=[r_mgT, wo[nb][1]], writes=[r_po])
                        tt_, r_tt = f6.next()
                        S.op("dve", lambda e, tt_=tt_, po=po, nb=nb: e.tensor_tensor(out=tt_[:], in0=po[:], in1=gbc[:, 0, nb * 512:(nb + 1) * 512],
                                                                                    op=ALU.mult), reads=[r_po, r_gbc], writes=[r_tt])
                        S.op("dve", lambda e, tt_=tt_, nb=nb, x1=x1, xt=xt: e.tensor_tensor(out=x1[:, nb * 512:(nb + 1) * 512], in0=tt_[:],
                                                                                            in1=xt[:, nb * 512:(nb + 1) * 512], op=ALU.add),
                             reads=[r_tt, r_xt], writes=[r_x1])
                    S.dma("sp", X1_d[g * 128:(g + 1) * 128, :], x1[:], reads=[r_x1], writes=[r_x1d[g]], chan_res=r_x1)
                    ss, r_ss = st5.next()
                    S.op("act", lambda e, x1=x1, ss=ss: e.activation(out=junk2[:], in_=x1[:], func=AF.Square, accum_out=ss[:, 0:1]),
                         reads=[r_x1], writes=[r_junk2, r_ss])
                    S.op("dve", lambda e, ss=ss: e.tensor_scalar(out=ss[:, 0:1], in0=ss[:, 0:1], scalar1=1.0 / 1024, scalar2=EPS_RMS,
                                                                 op0=ALU.mult, op1=ALU.add), reads=[r_ss], writes=[r_ss])
                    S.op("act", lambda e, ss=ss: e.activation(out=ss[:, 0:1], in_=ss[:, 0:1], func=AF.Sqrt), reads=[r_ss], writes=[r_ss])
                    S.op("dve", lambda e, ss=ss: e.reciprocal(out=ss[:, 0:1], in_=ss[:, 0:1]), reads=[r_ss], writes=[r_ss])
                    S.op("act", lambda e, x1=x1, ss=ss: e.activation(out=xb2[:], in_=x1[:], func=AF.Copy, scale=ss[:, 0:1]),
                         reads=[r_x1, r_ss], writes=[r_xb2])
                    for kc in range(8):
                        S.op("pe", lambda e, kc=kc: e.transpose(out=pTe[:, kc, :], in_=xb2[:, kc * 128:(kc + 1) * 128], identity=identb[:]),
                             reads=[r_xb2, r_idb], writes=[r_pTe])
                    for kc in range(8):
                        S.op("dve", lambda e, kc=kc, ti=ti: e.tensor_scalar(out=h2T[:, kc, ti * 128:(ti + 1) * 128], in0=pTe[:, kc, :],
                                                                           scalar1=s2[:, kc, cnd:cnd + 1], scalar2=mod[:, 24 + kc, cnd:cnd + 1],
                                                                           op0=ALU.mult, op1=ALU.add), reads=[r_pTe, r_mod], writes=[r_h2T])

            def ffn_in_part(blk):
                cnd = 0 if blk < 8 else 1
                for c in range(6):
                    n = min(4, 22 - c * 4)
                    wu, r_wu = wl_("w_f1", 8, c * 512, n * 128)
                    wg, r_wg = wl_("w_f1", 8, D_FF + c * 512, n * 128)
                    for jj in range(n):
                        j = c * 4 + jj
                        pu, r_pu = pe_.next()
                        for kc in range(8):
                            S.op("pe", lambda e, pu=pu, kc=kc, wu=wu, jj=jj: e.matmul(pu[:], lhsT=wu[:, kc, jj * 128:(jj + 1) * 128], rhs=h2T[:, kc, :],
                                                                                     start=(kc == 0), stop=(kc == 7)), reads=[r_wu, r_h2T], writes=[r_pu])
                        pg, r_pg = pe_.next()
                        for kc in range(8):
                            S.op("pe", lambda e, pg=pg, kc=kc, wg=wg, jj=jj: e.matmul(pg[:], lhsT=wg[:, kc, jj * 128:(jj + 1) * 128], rhs=h2T[:, kc, :],
                                                                                     start=(kc == 0), stop=(kc == 7)), reads=[r_wg, r_h2T], writes=[r_pg])
                        su, r_su = f6.next()
                        S.op("act", lambda e, su=su, pu=pu: e.activation(out=su[:], in_=pu[:], func=AF.Silu), reads=[r_pu], writes=[r_su])
                        S.op("dve", lambda e, su=su, pg=pg, j=j: e.tensor_tensor(out=aT[:, j, :], in0=pg[:], in1=su[:], op=ALU.mult),
                             reads=[r_pg, r_su], writes=[r_aT])

            def ffn_out_part(blk, tis=(0, 1, 2, 3)):
                cnd = 0 if blk < 8 else 1
                xts = XT[blk]
                for ti in tis:
                    g = blk * 4 + ti
                    x1, r_x1 = x1r.next()
                    S.dma("sp", x1[:], X1_d[g * 128:(g + 1) * 128, :], reads=[r_x1d[g]], writes=[r_x1], chan_res=r_x1)
                    yo, r_yo = xts[ti]
                    for nb in range(2):
                        po, r_po = pe_.next()
                        for j in range(22):
                            S.op("pe", lambda e, po=po, j=j, nb=nb: e.matmul(po[:], lhsT=aT[:, j, ti * 128:(ti + 1) * 128], rhs=wf2[:, j, nb * 512:(nb + 1) * 512],
                                                                            start=(j == 0), stop=(j == 21)), reads=[r_aT, r_wf2], writes=[r_po])
                        tt_, r_tt = f6.next()
                        S.op("dve", lambda e, tt_=tt_, po=po, nb=nb: e.tensor_tensor(out=tt_[:], in0=po[:], in1=gbc[:, 1, nb * 512:(nb + 1) * 512],
                                                                                    op=ALU.mult), reads=[r_po, r_gbc], writes=[r_tt])
                        S.op("dve", lambda e, tt_=tt_, nb=nb, x1=x1, yo=yo: e.tensor_tensor(out=yo[:, nb * 512:(nb + 1) * 512], in0=tt_[:],
                                                                                            in1=x1[:, nb * 512:(nb + 1) * 512], op=ALU.add),
                             reads=[r_tt, r_x1], writes=[r_yo])
                    S.dma("sp", yall[g * 128:(g + 1) * 128, :], yo[:], reads=[r_yo], chan_res=r_yo)

            fill_gbc(0, 0)
            fill_gbc(0, 1)
            merge_part(0)
            outproj_part(0)
            for blk in range(10):
                ffn_in_part(blk)
                if blk + 1 < 10:
                    merge_part(blk + 1)
                if blk == 8:
                    fill_gbc(1, 1)
                ffn_out_part(blk, (0, 1))
                if blk + 1 < 10:
                    if blk + 1 == 8:
                        fill_gbc(1, 0)
                    outproj_part(blk + 1)
                ffn_out_part(blk, (2, 3))
            S.end_phase()
        chk(5)
        S.barrier()


def _host_consts():
    c0 = math.exp(-0.5)
    s = np.arange(128)[:, None]
    t = np.arange(128)[None, :]
    cst = np.zeros((128, 2184), np.float32)
    cst[:, 0:128] = np.eye(128, dtype=np.float32)
    f = lambda m: m.astype(np.float32)
    mats = [
        -c0 * (f(s <= t) - f(s <= 63)), -c0 * (f(s < t) - f(s <= 63)), -c0 * f(s > t),
        -c0 * (f(s >= t) - f(s >= 64)), -c0 * (f(s > t) - f(s >= 64)), -c0 * f(s < t),
    ]
    for i, m in enumerate(mats):
        cst[:, 128 + i * 128: 256 + i * 128] = m
    sv = np.arange(128)
    cst[:, 896] = -c0
    cst[:, 897] = -c0 * (sv <= 63)
    cst[:, 898] = -c0
    cst[:, 899] = -c0 * (sv >= 64)
    for z in range(2):
        o = 904 + z * 640
        r = np.arange(128)[:, None]
        c = np.arange(128)[None, :]
        if z == 0:
            before_rc = f(c < r)
            before_st = f(r < c)
            incl_st = f(r <= c)
        else:
            before_rc = f(c > r)
            before_st = f(r > c)
            incl_st = f(r >= c)
        cst[:, o:o + 128] = -before_rc
        cst[:, o + 128:o + 256] = -before_st
        cst[:, o + 256:o + 384] = incl_st
        cst[:, o + 384:o + 512] = before_st
        cst[:, o + 512:o + 640] = incl_st
    tok = np.arange(4096)
    row = (tok // 64).astype(np.float32)
    col = (tok % 64).astype(np.float32)
    inv = (np.float32(10000.0) ** (-np.arange(16, dtype=np.float32) / np.float32(16))).astype(np.float32)
    ar = (row[:, None] * inv).astype(np.float32)
    ac = (col[:, None] * inv).astype(np.float32)
    cos = np.concatenate([np.cos(ar), np.cos(ar), np.cos(ac), np.cos(ac)], -1)
    sin = np.concatenate([-np.sin(ar), np.sin(ar), -np.sin(ac), np.sin(ac)], -1)
    rope = np.stack([cos, sin], 1).astype(np.float32)
    rope = rope.reshape(32, 128, 2, 64).transpose(1, 0, 2, 3)
    return cst, np.ascontiguousarray(rope)


def _in_maps(inp):
    A = lambda a: np.ascontiguousarray(np.asarray(a, dtype=np.float32))
    cst, rope = _host_consts()
    colp = np.concatenate([A(inp["norm1_w"])[0].reshape(8, 128).T, A(inp["norm2_w"])[0].reshape(8, 128).T,
                           A(inp["ada_b"])[0].reshape(48, 128).T], 1)
    rowp = np.concatenate([A(inp["q_norm_w"])[0], A(inp["k_norm_w"])[0], A(inp["subln_w"])[0], A(inp["k_k"])[0],
                           A(inp["k_a"])[0], A(inp["r_k"])[0].reshape(-1), A(inp["ln_x_w"])[0], A(inp["ln_x_b"])[0],
                           A(inp["lambda_q1"])[0], A(inp["lambda_k1"])[0], A(inp["lambda_q2"])[0], A(inp["lambda_k2"])[0],
                           A(inp["w0"])[0].reshape(-1), A(inp["a0"])[0].reshape(-1)])[None, :]
    shared = {
        "colp": A(colp), "rowp": A(rowp), "cst": cst, "rope": rope,
        "ada_w": A(inp["ada_w"])[0], "w_in": A(inp["w_in"])[0],
        "wlu": A(inp["w_lora_up"])[0].reshape(128, 512), "alu": A(inp["a_lora_up"])[0].reshape(128, 512),
        "w_abr": A(inp["w_attn_br"])[0], "w_rbr": A(inp["w_rwkv_br"])[0], "w_out": A(inp["w_out"])[0],
        "w_f1": A(inp["w_ffn_in"])[0], "w_f2": A(inp["w_ffn_out"])[0],
    }
    maps = []
    xp = A(inp["x_prompt"])
    xs = A(inp["x_sample"])
    for b in range(8):
        m = dict(shared)
        m["xall"] = np.concatenate([xs[b], xp[4 * b:4 * b + 4].reshape(1024, 1024)], 0)
        m["ck"] = A(inp["cache_k"])[b, 0].reshape(256, 512)
        m["cv"] = A(inp["cache_v"])[b, 0].reshape(256, 512)
        m["st0"] = A(inp["state_rwkv"])[b, 0].reshape(16, 64, 64)
        cond = np.stack([A(inp["c"])[b], A(inp["c_ctx"])], 0)
        m["condT"] = np.ascontiguousarray(cond.reshape(2, 8, 128).transpose(2, 1, 0).reshape(128, 16))
        maps.append(m)
    return maps


_NC_CACHE = {}


def kernel(**inputs):
    if "nc" not in _NC_CACHE:
        _NC_CACHE["nc"] = build_nc()
    nc = _NC_CACHE["nc"]
    maps = _in_maps(inputs)
    res = run_bass_kernel_spmd(nc, maps, core_ids=list(range(8)))
    R = res.results
    y_sample = np.stack([R[b]["yall"][:4096] for b in range(8)], 0)
    y_prompt = np.concatenate([R[b]["yall"][4096:].reshape(4, 256, 1024) for b in range(8)], 0)
    new_k = np.concatenate([R[b]["nk"].reshape(4, 1, 256, 4, 2, 64) for b in range(8)], 0)
    new_v = np.concatenate([R[b]["nv"].reshape(4, 1, 256, 4, 128) for b in range(8)], 0)
    new_s = np.concatenate([R[b]["ns"].reshape(4, 1, 2, 8, 64, 64) for b in range(8)], 0)
    return (y_prompt.astype(np.float32), y_sample.astype(np.float32), new_k.astype(np.float32),
            new_v.astype(np.float32), new_s.astype(np.float32))
```

```python
import math
import numpy as np
import ml_dtypes
import concourse.bass as bass
import concourse.mybir as mybir
from concourse.bass_utils import run_bass_kernel_spmd
from contextlib import ExitStack

F32 = mybir.dt.float32
BF16 = mybir.dt.bfloat16
AF = mybir.ActivationFunctionType
ALU = mybir.AluOpType
AX = mybir.AxisListType

ENGS = ("pe", "act", "dve", "pool", "sp")
NT = 40
NTOK = NT * 128
C0 = math.exp(-0.5)
EPS_RMS = 1e-6
EPS_GN = 64e-5
D_FF = 2816
import os
STAGGER = 0
INTERLEAVE_POST = 1
HQ = 4


class Res:
    __slots__ = ("w", "r", "chan", "psum")

    def __init__(self, psum=False):
        self.w = None
        self.r = []
        self.chan = None
        self.psum = psum


class Sched:
    def __init__(self, nc, stack):
        self.nc = nc
        self.stack = stack
        self.cnt = {}
        self.known = {e: {} for e in ENGS}
        self.sems = {}
        self.nchan = 0
        self.free_chans = []
        self.phase_chans = None
        self.engobj = {"pe": nc.tensor, "act": nc.scalar, "dve": nc.vector, "pool": nc.gpsimd, "sp": nc.sync}
        for e in ("pe", "act", "dve", "pool"):
            self._mksem(e)

    def _mksem(self, key):
        s = self.stack.enter_context(self.nc.semaphore("s_" + str(key)))
        self.sems[key] = s
        self.cnt[key] = 0
        return s

    def chan_of(self, res):
        if res.chan is None:
            if self.free_chans:
                res.chan = self.free_chans.pop()
            else:
                self.nchan += 1
                res.chan = "c%d" % self.nchan
                self._mksem(res.chan)
            if self.phase_chans is not None:
                self.phase_chans.append(res.chan)
        return res.chan

    def begin_phase(self):
        self.phase_chans = []

    def end_phase(self):
        self.barrier()
        self.free_chans.extend(self.phase_chans)
        self.phase_chans = None

    def _deps(self, eng, reads, writes):
        deps = {}

        def add(kv):
            if kv is None:
                return
            k, v = kv
            if deps.get(k, 0) < v:
                deps[k] = v
        for r in reads:
            add(r.w)
            if r.psum:
                for x in r.r:
                    if x[0] != eng:
                        add(x)
        for w in writes:
            add(w.w)
            for x in w.r:
                add(x)
        out = []
        for k, v in deps.items():
            if k == eng and eng == "pe":
                continue
            if self.known[eng].get(k, 0) >= v:
                continue
            self.known[eng][k] = v
            out.append((self.sems[k], v))
        return out

    def op(self, eng, fn, reads=(), writes=()):
        waits = self._deps(eng, reads, writes)
        self.cnt[eng] += 1
        v = self.cnt[eng]
        e = self.engobj[eng]
        for s, val in waits:
            e.wait_ge(s, val)
        fn(e).then_inc(self.sems[eng], 1)
        for r in reads:
            r.r.append((eng, v))
            if len(r.r) > 64:
                r.r = r.r[-48:] if False else r.r
        for w in writes:
            w.w = (eng, v)
            w.r = []

    def dma(self, q, out, in_, reads=(), writes=(), chan_res=None, **kw):
        waits = self._deps(q, reads, writes)
        ch = self.chan_of(chan_res)
        self.cnt[ch] += 16
        v = self.cnt[ch]
        e = self.engobj[q]
        for s, val in waits:
            e.wait_ge(s, val)
        e.dma_start(out=out, in_=in_, **kw).then_inc(self.sems[ch], 16)
        for r in reads:
            r.r.append((ch, v))
        for w in writes:
            w.w = (ch, v)
            w.r = []

    def barrier(self, engs=ENGS):
        for eng in engs:
            e = self.engobj[eng]
            for k, s in self.sems.items():
                v = self.cnt[k]
                if v > 0 and self.known[eng].get(k, 0) < v:
                    if k == eng:
                        continue
                    self.known[eng][k] = v
                    e.wait_ge(s, v)


class Ring:
    def __init__(self, alloc, n, psum=False):
        self.items = [(alloc(i), Res(psum)) for i in range(n)]
        self.i = 0

    def next(self):
        it = self.items[self.i % len(self.items)]
        self.i += 1
        return it


def bc(ap, shape, axis):
    return ap.unsqueeze(axis).to_broadcast(list(shape))


class _Stop(Exception):
    pass


def build_nc(upto=99):
    nc = bass.Bass("TRN2", target_bir_lowering=False)
    try:
        _build(nc, upto)
    except _Stop:
        pass
    return nc


def _build(nc, upto):

    def DT(name, shape, dt, kind):
        return nc.dram_tensor(name, list(shape), dt, kind=kind).ap()

    I = "ExternalInput"
    O = "ExternalOutput"
    xall = DT("xall", [NTOK, 1024], F32, I)
    ck = DT("ck", [256, 512], F32, I)
    cv = DT("cv", [256, 512], F32, I)
    st0 = DT("st0", [16, 64, 64], F32, I)
    condT_d = DT("condT", [128, 16], F32, I)
    colp_d = DT("colp", [128, 64], F32, I)
    rowp_d = DT("rowp", [1, 5120], F32, I)
    cst_d = DT("cst", [128, 2184], F32, I)
    rope_d = DT("rope", [128, 32, 2, 64], F32, I)
    ada_w = DT("ada_w", [1024, 6144], F32, I)
    w_in = DT("w_in", [1024, 5888], F32, I)
    wlu_d = DT("wlu", [128, 512], F32, I)
    alu_d = DT("alu", [128, 512], F32, I)
    w_abr = DT("w_abr", [512, 1024], F32, I)
    w_rbr = DT("w_rbr", [512, 1024], F32, I)
    w_out = DT("w_out", [1024, 1024], F32, I)
    w_f1 = DT("w_f1", [1024, 2 * D_FF], F32, I)
    w_f2 = DT("w_f2", [D_FF, 1024], F32, I)

    yall = DT("yall", [NTOK, 1024], F32, O)
    nk = DT("nk", [1024, 512], F32, O)
    nv = DT("nv", [1024, 512], F32, O)
    ns = DT("ns", [4, 16, 64, 64], F32, O)

    X = "Internal"
    QT_d = DT("QT_d", [4, 128, NTOK], BF16, X)
    KT_d = DT("KT_d", [4, 128, NTOK], BF16, X)
    V_d = DT("V_d", [NTOK, 512], BF16, X)
    TM_d = DT("TM_d", [NTOK, 2, 3, 512], BF16, X)
    VR_d = DT("VR_d", [NTOK, 512], BF16, X)
    FM_d = DT("FM_d", [NT, 2, 4, 512, 128], BF16, X)
    SG_d = DT("SG_d", [NTOK, 512], BF16, X)
    BON_d = DT("BON_d", [NTOK, 512], BF16, X)
    GT_d = DT("GT_d", [16, 128, NTOK], BF16, X)
    ORT_d = DT("ORT_d", [4, 128, NTOK], BF16, X)
    OAT_d = DT("OAT_d", [4, 128, NTOK], BF16, X)
    X1_d = DT("X1_d", [NTOK, 1024], F32, X)
    YF_d = DT("YF_d", [NTOK, 512], F32, X)
    w_in_b = DT("w_in_b", [1024, 5888], BF16, X)
    w_abr_b = DT("w_abr_b", [512, 1024], BF16, X)
    w_rbr_b = DT("w_rbr_b", [512, 1024], BF16, X)
    w_out_b = DT("w_out_b", [1024, 1024], BF16, X)
    w_f1_b = DT("w_f1_b", [1024, 2 * D_FF], BF16, X)
    w_f2_b = DT("w_f2_b", [D_FF, 1024], BF16, X)

    with ExitStack() as st:
        S = Sched(nc, st)

        def chk(level):
            if upto <= level:
                S.barrier()
                raise _Stop()

        def SB(stack, name, shape, dt):
            return stack.enter_context(nc.sbuf_tensor("sb_" + name, list(shape), dt))

        def PS(stack, name, shape, dt=F32):
            return stack.enter_context(nc.psum_tensor("ps_" + name, list(shape), dt))

        r_wcv = {}
        for nm, src, dst, rows in (("w_in", w_in, w_in_b, 1024), ("w_abr", w_abr, w_abr_b, 512), ("w_rbr", w_rbr, w_rbr_b, 512),
                                   ("w_out", w_out, w_out_b, 1024), ("w_f1", w_f1, w_f1_b, 1024), ("w_f2", w_f2, w_f2_b, D_FF)):
            r_ = Res()
            r_wcv[nm] = r_
            step = 256
            for r0 in range(0, rows, step):
                r1 = min(rows, r0 + step)
                S.dma("pool", dst[r0:r1, :], src[r0:r1, :], writes=[], chan_res=r_)
            r_.w = (r_.chan, S.cnt[r_.chan])
        cst = SB(st, "cst", [128, 2184], F32)
        r_cst = Res()
        S.dma("sp", cst[:], cst_d[:, :], writes=[r_cst], chan_res=r_cst)
        identf = cst[:, 0:128]
        def CM(i):
            return cst[:, 128 + i * 128: 256 + i * 128]
        ccol = cst[:, 896:900]
        def MSK(z, a, b):
            o = 904 + z * 640
            return cst[:, o + a: o + b]
        identb = SB(st, "identb", [128, 128], BF16)
        r_idb = Res()
        S.op("dve", lambda e: e.tensor_copy(out=identb[:], in_=identf), reads=[r_cst], writes=[r_idb])
        rowp = SB(st, "rowp", [128, 5120], F32)
        r_rowp = Res()
        S.dma("sp", rowp[:], rowp_d.partition_broadcast(128), writes=[r_rowp], chan_res=r_rowp)
        RP = {}
        o = 0
        for nm, n in (("qnw", 64), ("knw", 64), ("subw", 128), ("k_k", 512), ("k_a", 512), ("r_k", 512),
                      ("lnw", 512), ("lnb", 512), ("lam", 256), ("w0", 1024), ("a0", 1024)):
            RP[nm] = (o, o + n)
            o += n
        def RW(nm, a=0, b=None):
            lo, hi = RP[nm]
            return rowp[:, lo + a: (lo + b) if b is not None else hi]
        ones_f = SB(st, "ones_f", [128, 128], F32)
        r_ones = Res()
        S.op("dve", lambda e: e.memset(ones_f[:], 1.0), writes=[r_ones])
        mhalf = SB(st, "mhalf", [128, 16], F32)
        r_mh = Res()
        S.op("dve", lambda e: e.memset(mhalf[:], -0.5), writes=[r_mh])

        def rsqrt_pool(ap, r_ap, w):
            S.op("pool", lambda e: e.tensor_tensor(out=ap, in0=ap, in1=mhalf[:, 0:w], op=ALU.pow), reads=[r_ap, r_mh], writes=[r_ap])
        mod = SB(st, "mod", [128, 48, 2], F32)
        r_mod = Res()
        s1 = SB(st, "s1", [128, 8, 2], F32)
        s2 = SB(st, "s2", [128, 8, 2], F32)
        PCM = SB(st, "PCM", [64, NT, 16, 2], F32)
        r_pcm = [Res() for _ in range(NT)]
        lamc = SB(st, "lamc", [128, 2], F32)
        r_lamc = Res()

        if upto <= -3:
            S.barrier()
            return nc
        S.begin_phase()
        with ExitStack() as ph:
            condT = SB(ph, "condT", [128, 8, 2], F32)
            r_cond = Res()
            S.dma("sp", condT[:], condT_d.rearrange("p (k c) -> p k c", c=2), writes=[r_cond], chan_res=r_cond)
            colp = SB(ph, "colp", [128, 64], F32)
            r_colp = Res()
            S.dma("sp", colp[:], colp_d[:, :], writes=[r_colp], chan_res=r_colp)
            scn = SB(ph, "scn", [128, 8, 2], F32)
            r_scn = Res()
            S.op("act", lambda e: e.activation(out=scn[:], in_=condT[:], func=AF.Silu), reads=[r_cond], writes=[r_scn])
            if upto <= -2:
                S.barrier()
                return nc
            awr = Ring(lambda i: SB(ph, "aw%d" % i, [128, 8, 512], F32), 2)
            pm = PS(ph, "pm", [128, 48, 2])
            r_pm = Res(True)
            for c in range(12):
                aw, r_aw = awr.next()
                S.dma("sp", aw[:], ada_w[:, c * 512:(c + 1) * 512].rearrange("(k p) n -> p k n", p=128),
                      writes=[r_aw], chan_res=r_aw)
                for jj in range(4):
                    j = c * 4 + jj
                    for kc in range(8):
                        S.op("pe", lambda e, aw=aw, jj=jj, j=j, kc=kc: e.matmul(
                            pm[:, j, :], lhsT=aw[:, kc, jj * 128:(jj + 1) * 128], rhs=scn[:, kc, :],
                            start=(kc == 0), stop=(kc == 7)), reads=[r_aw, r_scn], writes=[r_pm])
            if upto <= -1:
                S.barrier()
                return nc
            S.op("dve", lambda e: e.tensor_tensor(out=mod[:], in0=pm[:], in1=bc(colp[:, 16:64], [128, 48, 2], 2),
                                                  op=ALU.add), reads=[r_pm, r_colp], writes=[r_mod])
            S.op("dve", lambda e: e.scalar_tensor_tensor(out=s1[:], in0=mod[:, 8:16, :], scalar=1.0,
                                                         in1=bc(colp[:, 0:8], [128, 8, 2], 2),
                                                         op0=ALU.add, op1=ALU.mult), reads=[r_mod, r_colp], writes=[r_mod])
            S.op("dve", lambda e: e.scalar_tensor_tensor(out=s2[:], in0=mod[:, 32:40, :], scalar=1.0,
                                                         in1=bc(colp[:, 8:16], [128, 8, 2], 2),
                                                         op0=ALU.add, op1=ALU.mult), reads=[r_mod, r_colp], writes=[r_mod])
            if upto <= -0.5:
                S.barrier()
                return nc
            lt = SB(ph, "lt", [128, 128], F32)
            r_lt = Res()
            l2 = SB(ph, "l2", [128, 2], F32)
            S.op("dve", lambda e: e.tensor_tensor(out=lt[:].rearrange("p (a d) -> p a d", a=2),
                                                  in0=RW("lam").rearrange("p (a b d) -> p a b d", a=2, b=2)[:, :, 0, :],
                                                  in1=RW("lam").rearrange("p (a b d) -> p a b d", a=2, b=2)[:, :, 1, :],
                                                  op=ALU.mult), reads=[r_rowp], writes=[r_lt])
            S.op("dve", lambda e: e.tensor_reduce(out=l2[:], in_=lt[:].rearrange("p (a d) -> p a d", a=2),
                                                  axis=AX.X, op=ALU.add), reads=[r_lt], writes=[r_lt])
            if upto <= -0.3:
                S.barrier()
                return nc
            l3 = SB(ph, "l3", [128, 2], F32)
            S.op("act", lambda e: e.activation(out=l3[:], in_=l2[:], func=AF.Exp), reads=[r_lt], writes=[r_lamc])
            if upto <= -0.2:
                S.barrier()
                return nc
            S.op("dve", lambda e: e.tensor_scalar(out=lamc[:, 0:1], in0=l3[:, 1:2], scalar1=l3[:, 0:1], scalar2=-0.2,
                                                  op0=ALU.subtract, op1=ALU.add), reads=[r_lamc], writes=[r_lamc])
            S.end_phase()
        if upto <= 0:
            S.barrier()
            return nc

        S.begin_phase()
        with ExitStack() as ph:
            roper = Ring(lambda i: SB(ph, "ropet%d" % i, [128, 4, 2, 64], F32), 2)
            wlu = SB(ph, "wlu", [128, 512], F32)
            alu = SB(ph, "alu", [128, 512], F32)
            r_lu = Res()
            S.dma("sp", wlu[:], wlu_d[:, :], writes=[r_lu], chan_res=r_lu)
            r_lu2 = Res()
            S.dma("sp", alu[:], alu_d[:, :], writes=[r_lu2], chan_res=r_lu2)
            oma = SB(ph, "oma", [128, 512], F32)
            r_oma = Res()
            S.op("dve", lambda e: e.tensor_scalar(out=oma[:], in0=RW("k_a"), scalar1=-1.0, scalar2=1.0,
                                                  op0=ALU.mult, op1=ALU.add), reads=[r_rowp], writes=[r_oma])
            xr = Ring(lambda i: SB(ph, "x%d" % i, [128, 1024], F32), 2)
            xb = SB(ph, "xb", [128, 1024], BF16)
            r_xb = Res()
            junk = xb
            r_junk = r_xb
            st4 = Ring(lambda i: SB(ph, "st4_%d" % i, [128, 4], F32), 4)
            hTr = Ring(lambda i: SB(ph, "hT%d" % i, [128, 8, 512], BF16), 2)
            wr = Ring(lambda i: SB(ph, "w%d" % i, [128, 8, 512], BF16), 3)
            wlT = SB(ph, "wlT", [128, 512], F32)
            alT = SB(ph, "alT", [128, 512], F32)
            r_wlT = Res()
            r_alT = Res()
            rsb = SB(ph, "rsb", [128, 4, 512], F32)
            ksb = SB(ph, "ksb", [128, 4, 512], F32)
            vsb = SB(ph, "vsb", [128, 4, 512], F32)
            r_rsb = [Res() for _ in range(4)]
            r_ksb = [Res() for _ in range(4)]
            r_vsb = [Res() for _ in range(4)]
            tf = Ring(lambda i: SB(ph, "tf%d" % i, [128, 512], F32), 18)
            kkrk = Ring(lambda i: SB(ph, "kkrk%d" % i, [128, 512], F32), 2)
            tb = Ring(lambda i: SB(ph, "tb%d" % i, [128, 512], BF16), 8)
            tmz = Ring(lambda i: SB(ph, "tmz%d" % i, [128, 3, 512], BF16), 3)
            fmz = Ring(lambda i: SB(ph, "fmz%d" % i, [128, 16, 128], BF16), 3)
            qkst = Ring(lambda i: SB(ph, "qkst%d" % i, [128, 4, 512], BF16), 2)
            gst = Ring(lambda i: SB(ph, "gst%d" % i, [128, 512], BF16), 2)
            vbr = Ring(lambda i: SB(ph, "vbr%d" % i, [128, 512], BF16), 2)
            vfr = Ring(lambda i: SB(ph, "vfr%d" % i, [128, 512], F32), 1)
            s8 = Ring(lambda i: SB(ph, "s8_%d" % i, [128, 8], F32), 8)
            pT = PS(ph, "pT", [128, 8, 128], BF16)
            r_pT = Res(True)
            pr = Ring(lambda i: PS(ph, "pr%d" % i, [128, 512]), 6, psum=True)
            pcc = PS(ph, "pcc", [64, 16, 2])
            r_pcc = Res(True)

            def rstd_from(ssum, r_ss, n, eps):
                S.op("dve", lambda e: e.tensor_scalar(out=ssum, in0=ssum, scalar1=1.0 / n, scalar2=eps,
                                                      op0=ALU.mult, op1=ALU.add), reads=[r_ss], writes=[r_ss])
                rsqrt_pool(ssum, r_ss, ssum.shape[-1])

            def wload(c0, ncol):
                w, r_w = wr.next()
                S.dma("pool", w[:, :, 0:ncol], w_in_b[:, c0:c0 + ncol].rearrange("(k p) n -> p k n", p=128),
                      reads=[r_wcv["w_in"]], writes=[r_w], chan_res=r_w)
                return w, r_w

            def xnorm_gen(blk, out):
                cnd = 0 if blk < 8 else 1
                hT, r_hT = hTr.next()
                out["hT"] = (hT, r_hT)
                for ti in range(4):
                    g = blk * 4 + ti
                    xt, r_xt = xr.next()
                    S.dma("sp", xt[:], xall[g * 128:(g + 1) * 128, :], writes=[r_xt], chan_res=r_xt)
                    ss, r_ss = st4.next()
                    S.op("act", lambda e, xt=xt, ss=ss: e.activation(out=junk[:], in_=xt[:], func=AF.Square,
                                                                     accum_out=ss[:, 0:1]),
                         reads=[r_xt], writes=[r_junk, r_ss])
                    rstd_from(ss[:, 0:1], r_ss, 1024, EPS_RMS)
                    S.op("act", lambda e, xt=xt, ss=ss: e.activation(out=xb[:], in_=xt[:], func=AF.Copy, scale=ss[:, 0:1]),
                         reads=[r_xt, r_ss], writes=[r_xb])
                    yield
                    for kc in range(8):
                        S.op("pe", lambda e, kc=kc: e.transpose(out=pT[:, kc, :], in_=xb[:, kc * 128:(kc + 1) * 128],
                                                                identity=identb[:]),
                             reads=[r_xb, r_idb], writes=[r_pT])
                    for kc in range(8):
                        S.op("dve", lambda e, kc=kc, hT=hT, ti=ti: e.tensor_scalar(
                            out=hT[:, kc, ti * 128:(ti + 1) * 128], in0=pT[:, kc, :],
                            scalar1=s1[:, kc, cnd:cnd + 1], scalar2=mod[:, kc, cnd:cnd + 1],
                            op0=ALU.mult, op1=ALU.add), reads=[r_pT, r_mod], writes=[r_hT])
                    yield

            def drain(gen):
                if gen is None:
                    return
                for _ in gen:
                    pass

            xo = {}
            drain(xnorm_gen(0, xo))
            for blk in range(10):
                latent = blk < 8
                cnd = 0 if latent else 1
                hT, r_hT = xo["hT"]
                xo = {}
                xg = xnorm_gen(blk + 1, xo) if blk + 1 < 10 else None
                if latent:
                    ropet, r_rope = roper.next()
                    S.dma("sp", ropet[:], rope_d[:, blk * 4:(blk + 1) * 4, :, :], writes=[r_rope], chan_res=r_rope)

                chk(0.1)
                def mm_tok(w, r_w, ti, ncol=512, wc0=0):
                    p, r_p = pr.next()
                    for kc in range(8):
                        S.op("pe", lambda e, kc=kc, p=p: e.matmul(p[:, 0:ncol], lhsT=hT[:, kc, ti * 128:(ti + 1) * 128],
                                                                  rhs=w[:, kc, wc0:wc0 + ncol], start=(kc == 0), stop=(kc == 7)),
                             reads=[r_hT, r_w], writes=[r_p])
                    return p, r_p

                def mm_feat(w, r_w, wc0):
                    p, r_p = pr.next()
                    for kc in range(8):
                        S.op("pe", lambda e, kc=kc, p=p: e.matmul(p[:, :], lhsT=w[:, kc, wc0:wc0 + 128], rhs=hT[:, kc, :],
                                                                  start=(kc == 0), stop=(kc == 7)),
                             reads=[r_hT, r_w], writes=[r_p])
                    return p, r_p

                w, r_w = wload(3584, 256)
                p, r_p = mm_feat(w, r_w, 0)
                S.op("act", lambda e, p=p: e.activation(out=wlT[:], in_=p[:], func=AF.Tanh), reads=[r_p], writes=[r_wlT])
                p, r_p = mm_feat(w, r_w, 128)
                S.op("act", lambda e, p=p: e.activation(out=alT[:], in_=p[:], func=AF.Copy), reads=[r_p], writes=[r_alT])
                chk(0.2)
                w, r_w = wload(1536, 512)
                for ti in range(4):
                    p, r_p = mm_tok(w, r_w, ti)
                    S.op("act", lambda e, p=p, ti=ti: e.activation(out=rsb[:, ti, :], in_=p[:], func=AF.Copy),
                         reads=[r_p], writes=[r_rsb[ti]])
                chk(0.22)
                w, r_w = wload(2048, 512)
                for ti in range(4):
                    p, r_p = mm_tok(w, r_w, ti)
                    S.op("act", lambda e, p=p, ti=ti: e.activation(out=ksb[:, ti, :], in_=p[:], func=AF.Copy),
                         reads=[r_p], writes=[r_ksb[ti]])
                chk(0.24)
                w, r_w = wload(2560, 512)
                for ti in range(4):
                    g = blk * 4 + ti
                    p, r_p = mm_tok(w, r_w, ti)
                    S.op("act", lambda e, p=p, ti=ti: e.activation(out=vsb[:, ti, :], in_=p[:], func=AF.Copy),
                         reads=[r_p], writes=[r_vsb[ti]])
                    vb, r_vb = tb.next()
                    S.op("dve", lambda e, ti=ti, vb=vb: e.tensor_copy(out=vb[:], in_=vsb[:, ti, :]), reads=[r_vsb[ti]], writes=[r_vb])
                    S.dma("sp", VR_d[g * 128:(g + 1) * 128, :], vb[:], reads=[r_vb], chan_res=r_vb)
                chk(0.26)
                w, r_w = wload(3072, 512)
                for ti in range(4):
                    g = blk * 4 + ti
                    p, r_p = mm_tok(w, r_w, ti)
                    sg, r_sg = tb.next()
                    S.op("act", lambda e, p=p, sg=sg: e.activation(out=sg[:], in_=p[:], func=AF.Sigmoid),
                         reads=[r_p], writes=[r_sg])
                    S.dma("sp", SG_d[g * 128:(g + 1) * 128, :], sg[:], reads=[r_sg], chan_res=r_sg)

                def filler():
                    w, r_w = wload(1024, 512)
                    for ti in range(4):
                        g = blk * 4 + ti
                        p, r_p = mm_tok(w, r_w, ti)
                        vb, r_vb = vbr.next()
                        S.op("act", lambda e, p=p, vb=vb: e.activation(out=vb[:], in_=p[:], func=AF.Copy), reads=[r_p], writes=[r_vb])
                        S.dma("sp", V_d[g * 128:(g + 1) * 128, :], vb[:], reads=[r_vb], chan_res=r_vb)
                        if not latent:
                            vf, r_vf = vfr.next()
                            S.op("act", lambda e, p=p, vf=vf: e.activation(out=vf[:], in_=p[:], func=AF.Copy), reads=[r_p], writes=[r_vf])
                            S.dma("sp", nv[(g - 32) * 128:(g - 31) * 128, :], vf[:], reads=[r_vf], chan_res=r_vf)
                        yield
                    for c in range(4):
                        w, r_w = wload(3840 + c * 512, 512)
                        for jj in range(4):
                            j = c * 4 + jj
                            p, r_p = mm_feat(w, r_w, jj * 128)
                            gs, r_gs = gst.next()
                            S.op("act", lambda e, p=p, gs=gs: e.activation(out=gs[:], in_=p[:], func=AF.Sigmoid), reads=[r_p], writes=[r_gs])
                            S.dma("sp", GT_d[j, :, blk * 512:(blk + 1) * 512], gs[:], reads=[r_gs], chan_res=r_gs)
                            yield
                fill = filler()
                chk(0.3)
                for ti in range(4):
                    g = blk * 4 + ti
                    rt = rsb[:, ti, :]
                    kt = ksb[:, ti, :]
                    vt = vsb[:, ti, :]
                    kkr, r_kkr = tf.next()
                    S.op("dve", lambda e, kkr=kkr: e.tensor_tensor(out=kkr[:], in0=kt, in1=RW("k_k"), op=ALU.mult),
                         reads=[r_ksb[ti], r_rowp], writes=[r_kkr])
                    sq, r_sq = tf.next()
                    S.op("act", lambda e, sq=sq, kkr=kkr: e.activation(out=sq[:], in_=kkr[:], func=AF.Square),
                         reads=[r_kkr], writes=[r_sq])
                    rn, r_rn = s8.next()
                    S.op("dve", lambda e, sq=sq, rn=rn: e.tensor_reduce(out=rn[:], in_=sq[:].rearrange("p (h d) -> p h d", h=8),
                                                                        axis=AX.X, op=ALU.add), reads=[r_sq], writes=[r_rn])
                    S.op("dve", lambda e, rn=rn: e.tensor_scalar(out=rn[:], in0=rn[:], scalar1=1e-24, scalar2=None,
                                                                 op0=ALU.max), reads=[r_rn], writes=[r_rn])
                    rsqrt_pool(rn[:], r_rn, 8)
                    kk, r_kk = kkrk.next()
                    S.op("dve", lambda e, kk=kk, kkr=kkr, rn=rn: e.tensor_tensor(
                        out=kk[:].rearrange("p (h d) -> p h d", h=8), in0=kkr[:].rearrange("p (h d) -> p h d", h=8),
                        in1=bc(rn[:], [128, 8, 64], 2), op=ALU.mult), reads=[r_kkr, r_rn], writes=[r_kk])
                    rk, r_rk = kkrk.next()
                    S.op("dve", lambda e, rk=rk: e.tensor_tensor(out=rk[:], in0=rt, in1=RW("r_k"), op=ALU.mult),
                         reads=[r_rsb[ti], r_rowp], writes=[r_rk])
                    chk(0.4)
                    sz = []
                    def zprep(z):
                        pw, r_pw = pr.next()
                        S.op("pe", lambda e, pw=pw, z=z: e.matmul(pw[:], lhsT=wlT[z * 64:(z + 1) * 64, ti * 128:(ti + 1) * 128],
                                                                  rhs=wlu[z * 64:(z + 1) * 64, :], start=True, stop=False),
                             reads=[r_wlT, r_lu], writes=[r_pw])
                        S.op("pe", lambda e, pw=pw, z=z: e.matmul(pw[:], lhsT=ones_f[0:1, :],
                                                                  rhs=RW("w0", z * 512, (z + 1) * 512)[0:1, :],
                                                                  start=False, stop=True),
                             reads=[r_ones, r_rowp], writes=[r_pw])
                        sgw, r_sgw = tf.next()
                        S.op("act", lambda e, pw=pw, sgw=sgw: e.activation(out=sgw[:], in_=pw[:], func=AF.Sigmoid),
                             reads=[r_pw], writes=[r_sgw])
                        yield
                        pa, r_pa = pr.next()
                        S.op("pe", lambda e, pa=pa, z=z: e.matmul(pa[:], lhsT=alT[z * 64:(z + 1) * 64, ti * 128:(ti + 1) * 128],
                                                                  rhs=alu[z * 64:(z + 1) * 64, :], start=True, stop=False),
                             reads=[r_alT, r_lu2], writes=[r_pa])
                        S.op("pe", lambda e, pa=pa, z=z: e.matmul(pa[:], lhsT=ones_f[0:1, :],
                                                                  rhs=RW("a0", z * 512, (z + 1) * 512)[0:1, :],
                                                                  start=False, stop=True),
                             reads=[r_ones, r_rowp], writes=[r_pa])
                        az, r_az = tf.next()
                        S.op("act", lambda e, pa=pa, az=az: e.activation(out=az[:], in_=pa[:], func=AF.Sigmoid),
                             reads=[r_pa], writes=[r_az])
                        yield
                        p1, r_p1 = pr.next()
                        S.op("pe", lambda e, p1=p1, z=z, sgw=sgw: e.matmul(p1[:], lhsT=CM(3 * z + 0), rhs=sgw[:], start=True, stop=True),
                             reads=[r_cst, r_sgw], writes=[r_p1])
                        p0, r_p0 = pr.next()
                        S.op("pe", lambda e, p0=p0, z=z, sgw=sgw: e.matmul(p0[:], lhsT=CM(3 * z + 1), rhs=sgw[:], start=True, stop=True),
                             reads=[r_cst, r_sgw], writes=[r_p0])
                        p2, r_p2 = pr.next()
                        S.op("pe", lambda e, p2=p2, z=z, sgw=sgw: e.matmul(p2[:], lhsT=CM(3 * z + 2), rhs=sgw[:], start=True, stop=True),
                             reads=[r_cst, r_sgw], writes=[r_p2])
                        for h in range(8):
                            S.op("pe", lambda e, z=z, h=h, sgw=sgw: e.matmul(pcc[:, z * 8 + h, :], lhsT=sgw[:, h * 64:(h + 1) * 64],
                                                                             rhs=ccol[:, 2 * z:2 * z + 2], start=True, stop=True),
                                 reads=[r_cst, r_sgw], writes=[r_pcc])
                        E1, r_E1 = tf.next()
                        S.op("act", lambda e, p1=p1, E1=E1: e.activation(out=E1[:], in_=p1[:], func=AF.Exp), reads=[r_p1], writes=[r_E1])
                        Ei, r_Ei = tf.next()
                        S.op("act", lambda e, p1=p1, Ei=Ei: e.activation(out=Ei[:], in_=p1[:], func=AF.Exp, scale=-1.0),
                             reads=[r_p1], writes=[r_Ei])
                        E0, r_E0 = tf.next()
                        S.op("act", lambda e, p0=p0, E0=E0: e.activation(out=E0[:], in_=p0[:], func=AF.Exp), reads=[r_p0], writes=[r_E0])
                        Et, r_Et = tf.next()
                        S.op("act", lambda e, p2=p2, Et=Et: e.activation(out=Et[:], in_=p2[:], func=AF.Exp), reads=[r_p2], writes=[r_Et])
                        yield
                        chk(0.5)
                        kd, r_kd = tf.next()
                        S.op("dve", lambda e, kd=kd, az=az: e.tensor_tensor(out=kd[:], in0=az[:], in1=RW("k_a"), op=ALU.mult),
                             reads=[r_az, r_rowp], writes=[r_kd])
                        S.op("dve", lambda e, kd=kd: e.tensor_tensor(out=kd[:], in0=kd[:], in1=oma[:], op=ALU.add),
                             reads=[r_kd, r_oma], writes=[r_kd])
                        S.op("dve", lambda e, kd=kd: e.tensor_tensor(out=kd[:], in0=kd[:], in1=kt, op=ALU.mult),
                             reads=[r_kd, r_ksb[ti]], writes=[r_kd])
                        S.op("dve", lambda e, az=az, kk=kk: e.tensor_tensor(out=az[:], in0=az[:], in1=kk[:], op=ALU.mult),
                             reads=[r_az, r_kk], writes=[r_az])
                        yield
                        tm, r_tm = tmz.next()
                        S.op("dve", lambda e, tm=tm, kk=kk, E0=E0: e.tensor_tensor(out=tm[:, 0, :], in0=kk[:], in1=E0[:], op=ALU.mult),
                             reads=[r_kk, r_E0], writes=[r_tm])
                        S.op("dve", lambda e, tm=tm, kd=kd, Et=Et: e.tensor_tensor(out=tm[:, 1, :], in0=kd[:], in1=Et[:], op=ALU.mult),
                             reads=[r_kd, r_Et], writes=[r_tm])
                        S.op("dve", lambda e, tm=tm, az=az, Et=Et: e.tensor_tensor(out=tm[:, 2, :], in0=az[:], in1=Et[:], op=ALU.mult),
                             reads=[r_az, r_Et], writes=[r_tm])
                        S.dma("sp", TM_d[g * 128:(g + 1) * 128, z, :, :], tm[:], reads=[r_tm], chan_res=r_tm)
                        yield
                        hb = []
                        for (a_, b_, ra, rb) in ((rt, E1, r_rsb[ti], r_E1), (az[:], Ei, r_az, r_Ei), (kd[:], Ei, r_kd, r_Ei)):
                            t_, r_t = tb.next()
                            S.op("dve", lambda e, t_=t_, a_=a_, b_=b_: e.tensor_tensor(out=t_[:], in0=a_, in1=b_[:], op=ALU.mult),
                                 reads=[ra, rb], writes=[r_t])
                            hb.append((t_, r_t))
                        yield
                        chk(0.6)
                        fm, r_fm = fmz.next()
                        srcs = [(tm[:, 0, :], r_tm), (hb[0][0][:], hb[0][1]), (hb[1][0][:], hb[1][1]), (hb[2][0][:], hb[2][1])]
                        for half in range(2):
                            for qi in range(2):
                                src, r_src = srcs[half * 2 + qi]
                                for cb in range(4):
                                    S.op("pe", lambda e, src=src, cb=cb, qi=qi: e.transpose(
                                        out=pT[:, qi * 4 + cb, :], in_=src[:, cb * 128:(cb + 1) * 128], identity=identb[:]),
                                        reads=[r_src, r_idb], writes=[r_pT])
                            S.op("act", lambda e, fm=fm, half=half: e.activation(out=fm[:, half * 8:(half + 1) * 8, :], in_=pT[:],
                                                                                 func=AF.Copy), reads=[r_pT], writes=[r_fm])
                        S.dma("sp", FM_d[g, z].rearrange("t (cb p) k -> p (t cb) k", p=128), fm[:], reads=[r_fm], chan_res=r_fm)
                        yield
                        S.op("dve", lambda e, kd=kd, rk=rk: e.tensor_tensor(out=kd[:], in0=kd[:], in1=rk[:], op=ALU.mult),
                             reads=[r_kd, r_rk], writes=[r_kd])
                        s_, r_s = s8.next()
                        S.op("dve", lambda e, kd=kd, s_=s_: e.tensor_reduce(out=s_[:], in_=kd[:].rearrange("p (h d) -> p h d", h=8),
                                                                            axis=AX.X, op=ALU.add), reads=[r_kd], writes=[r_s])
                        sz.append((s_, r_s))
                        yield
                    gens_ = [zprep(0), zprep(1)]
                    rnd = 0
                    while gens_:
                        for gn_ in list(gens_):
                            try:
                                next(gn_)
                            except StopIteration:
                                gens_.remove(gn_)
                        rnd += 1
                        if fill is not None and rnd % 2 == 0:
                            try:
                                next(fill)
                            except StopIteration:
                                fill = None
                    S.op("act", lambda e, g=g: e.activation(out=PCM[:, g, :, :], in_=pcc[:], func=AF.Exp),
                         reads=[r_pcc], writes=[r_pcm[g]])
                    S.op("dve", lambda e: e.tensor_tensor(out=sz[0][0][:], in0=sz[0][0][:], in1=sz[1][0][:], op=ALU.add),
                         reads=[sz[0][1], sz[1][1]], writes=[sz[0][1]])
                    bon, r_bon = tb.next()
                    S.op("dve", lambda e, bon=bon: e.tensor_tensor(out=bon[:].rearrange("p (h d) -> p h d", h=8),
                                                                   in0=vt.rearrange("p (h d) -> p h d", h=8),
                                                                   in1=bc(sz[0][0][:], [128, 8, 64], 2), op=ALU.mult),
                         reads=[r_vsb[ti], sz[0][1]], writes=[r_bon])
                    S.dma("sp", BON_d[g * 128:(g + 1) * 128, :], bon[:], reads=[r_bon], chan_res=r_bon)

                while fill is not None:
                    try:
                        next(fill)
                    except StopIteration:
                        fill = None
                chk(0.7)
                def qkgen(which, tis, w, r_w, qs, r_qs):
                    nwn = "qnw" if which == 0 else "knw"
                    for ti in tis:
                        g = blk * 4 + ti
                        p, r_p = mm_tok(w, r_w, ti)
                        sq, r_sq = tf.next()
                        S.op("act", lambda e, p=p, sq=sq: e.activation(out=sq[:], in_=p[:], func=AF.Square), reads=[r_p], writes=[r_sq])
                        rs, r_rs = s8.next()
                        S.op("dve", lambda e, sq=sq, rs=rs: e.tensor_reduce(out=rs[:], in_=sq[:].rearrange("p (h d) -> p h d", h=8),
                                                                            axis=AX.X, op=ALU.add), reads=[r_sq], writes=[r_rs])
                        rstd_from(rs[:], r_rs, 64, EPS_RMS)
                        yield
                        xw, r_xw = tf.next()
                        S.op("dve", lambda e, p=p, xw=xw: e.tensor_tensor(out=xw[:].rearrange("p (h d) -> p h d", h=8),
                                                                          in0=p[:].rearrange("p (h d) -> p h d", h=8),
                                                                          in1=bc(RW(nwn), [128, 8, 64], 1), op=ALU.mult),
                             reads=[r_p, r_rowp], writes=[r_xw])
                        yield
                        ob, r_ob = tb.next()
                        if latent:
                            t1, r_t1 = tf.next()
                            S.op("dve", lambda e, t1=t1, xw=xw, ti=ti: e.tensor_tensor(
                                out=t1[:].rearrange("p (h d) -> p h d", h=8), in0=xw[:].rearrange("p (h d) -> p h d", h=8),
                                in1=bc(ropet[:, ti, 0, :], [128, 8, 64], 1), op=ALU.mult), reads=[r_xw, r_rope], writes=[r_t1])
                            yield
                            t2, r_t2 = tf.next()
                            for bl in range(2):
                                S.op("dve", lambda e, t2=t2, xw=xw, ti=ti, bl=bl: e.tensor_tensor(
                                    out=t2[:].rearrange("p (h a b i) -> p h a b i", h=8, a=2, b=2)[:, :, :, bl, :],
                                    in0=xw[:].rearrange("p (h a b i) -> p h a b i", h=8, a=2, b=2)[:, :, :, 1 - bl, :],
                                    in1=bc(ropet[:, ti, 1, :].rearrange("p (a b i) -> p a b i", a=2, b=2)[:, :, bl, :], [128, 8, 2, 16], 1),
                                    op=ALU.mult), reads=[r_xw, r_rope], writes=[r_t2])
                            yield
                            S.op("dve", lambda e, t1=t1, t2=t2: e.tensor_tensor(out=t1[:], in0=t1[:], in1=t2[:], op=ALU.add),
                                 reads=[r_t1, r_t2], writes=[r_t1])
                            yield
                            S.op("dve", lambda e, t1=t1, ob=ob, rs=rs: e.tensor_tensor(
                                out=ob[:].rearrange("p (h d) -> p h d", h=8), in0=t1[:].rearrange("p (h d) -> p h d", h=8),
                                in1=bc(rs[:], [128, 8, 64], 2), op=ALU.mult), reads=[r_t1, r_rs], writes=[r_ob])
                        else:
                            S.op("dve", lambda e, xw=xw, rs=rs: e.tensor_tensor(
                                out=xw[:].rearrange("p (h d) -> p h d", h=8), in0=xw[:].rearrange("p (h d) -> p h d", h=8),
                                in1=bc(rs[:], [128, 8, 64], 2), op=ALU.mult), reads=[r_xw, r_rs], writes=[r_xw])
                            if which == 1:
                                S.dma("sp", nk[(g - 32) * 128:(g - 31) * 128, :], xw[:], reads=[r_xw], chan_res=r_xw)
                            S.op("act", lambda e, xw=xw, ob=ob: e.activation(out=ob[:], in_=xw[:], func=AF.Copy),
                                 reads=[r_xw], writes=[r_ob])
                        yield
                        for h in range(4):
                            S.op("pe", lambda e, ob=ob, h=h: e.transpose(out=pT[:, h, :], in_=ob[:, h * 128:(h + 1) * 128],
                                                                         identity=identb[:]), reads=[r_ob, r_idb], writes=[r_pT])
                        S.op("act", lambda e, qs=qs, ti=ti: e.activation(out=qs[:, :, ti * 128:(ti + 1) * 128], in_=pT[:, 0:4, :],
                                                                         func=AF.Copy), reads=[r_pT], writes=[r_qs])
                    yield
                qkw = []
                gens_ = []
                for which in range(2):
                    w, r_w = wload(which * 512, 512)
                    qs, r_qs = qkst.next()
                    qkw.append((qs, r_qs))
                    gens_.append(qkgen(which, [0, 2], w, r_w, qs, r_qs))
                    gens_.append(qkgen(which, [1, 3], w, r_w, qs, r_qs))
                while gens_:
                    for gn_ in list(gens_):
                        try:
                            next(gn_)
                        except StopIteration:
                            gens_.remove(gn_)
                    if xg is not None:
                        try:
                            next(xg)
                        except StopIteration:
                            xg = None
                drain(xg)
                for which in range(2):
                    qs, r_qs = qkw[which]
                    dst = QT_d if which == 0 else KT_d
                    S.dma("sp", dst[:, :, blk * 512:(blk + 1) * 512].rearrange("h p t -> p h t"), qs[:], reads=[r_qs], chan_res=r_qs)
            S.end_phase()
        chk(1)

        S.begin_phase()
        with ExitStack() as ph:
            def mkring(name, shape, dt, n):
                return Ring(lambda i: SB(ph, "%s_%d" % (name, i), shape, dt), n)
            fmr = [mkring("fm%d" % z, [64, 4, 8, 128], BF16, 2) for z in range(2)]
            tmr = [mkring("tm%d" % z, [128, 3, 512], BF16, 2) for z in range(2)]
            vrr = [mkring("vr%d" % z, [128, 512], BF16, 2) for z in range(2)]
            gr = [[mkring("gr%d%d" % (z, q), [128, HQ, 128], BF16, 3) for q in range(8 // HQ)] for z in range(2)]
            sqr = [[mkring("sq%d%d" % (z, q), [128, HQ, 128], BF16, 4) for q in range(8 // HQ)] for z in range(2)]
            wrg = [[mkring("wg%d%d" % (z, q), [128, HQ, 128], BF16, 2) for q in range(8 // HQ)] for z in range(2)]
            avr = [[mkring("av%d%d" % (z, q), [128, HQ, 64], BF16, 1) for q in range(8 // HQ)] for z in range(2)]
            apr = [[mkring("ap%d%d" % (z, q), [128, HQ, 64], BF16, 1) for q in range(8 // HQ)] for z in range(2)]
            vnr = [[mkring("vn%d%d" % (z, q), [128, HQ, 64], BF16, 1) for q in range(8 // HQ)] for z in range(2)]
            mtr = [mkring("mt%d" % z, [64, 8, 64], F32, 2) for z in range(2)]
            gsr = [mkring("gs%d" % z, [64, 8, 64], F32, 2) for z in range(2)]
            rpr = [mkring("rp%d" % z, [64, 8, 128], BF16, 2) for z in range(2)]
            y0r = [mkring("y0%d" % z, [128, 512], F32, 2) for z in range(2)]
            hmr = [mkring("hm%d" % z, [64, 8, 64], F32, 1) for z in range(2)]
            hbr = [mkring("hb%d" % z, [64, 8, 64], BF16, 1) for z in range(2)]
            h1r = [mkring("h1%d" % z, [64, 8, 64], F32, 1) for z in range(2)]
            Hr = [mkring("H%d" % z, [64, 8, 64], F32, 2) for z in range(2)]
            stl = SB(ph, "stl", [64, 16, 64], F32)
            r_stl = Res()
            f5 = mkring("f5", [128, 512], F32, 6)
            b5 = mkring("b5", [128, 512], BF16, 2)
            bsr = mkring("bsr", [128, 512], BF16, 6)
            pend = {}
            yfr = mkring("yf", [128, 512], F32, 2)
            r_yfd = [Res() for _ in range(NT)]
            ydone = {}
            s8d = mkring("s8d", [128, 8], F32, 8)
            ostr = mkring("ost", [128, 4, 128], BF16, 2)
            pTd = PS(ph, "pTd", [128, 4, 128], BF16)
            r_pTd = Res(True)
            prd = Ring(lambda i: PS(ph, "prd%d" % i, [128, 512]), 7, psum=True)

            def v3(t, n=4):
                return t[:].rearrange("p (a b) -> p a b", a=n)

            def loads(g, z, second):
                fm, r_fm = fmr[z].next()
                S.dma("sp", fm[:], FM_d[g, z].rearrange("t (h p) k -> p t h k", p=64), writes=[r_fm], chan_res=r_fm)
                tm, r_tm = tmr[z].next()
                S.dma("sp", tm[:], TM_d[g * 128:(g + 1) * 128, z, :, :], writes=[r_tm], chan_res=r_tm)
                vr, r_vr = vrr[z].next()
                S.dma("sp", vr[:], VR_d[g * 128:(g + 1) * 128, :], writes=[r_vr], chan_res=r_vr)
                L = dict(fm=fm, r_fm=r_fm, tm=tm, r_tm=r_tm, vr=vr, r_vr=r_vr, second=second)
                return L

            def pre(g, z, L):
                fm, r_fm, tm, r_tm, vr, r_vr = L["fm"], L["r_fm"], L["tm"], L["r_tm"], L["vr"], L["r_vr"]
                MT, r_MT = mtr[z].next()
                Gs, r_Gs = gsr[z].next()
                Rp, r_Rp = rpr[z].next()
                Y0, r_Y0 = y0r[z].next()
                QS = tuple(range(8 // HQ))
                v3 = lambda t: t[:, 0:HQ * 128].rearrange("p (a b) -> p a b", a=HQ)
                hsl = [list(range(q * HQ, q * HQ + HQ)) for q in QS]
                st_ = [dict() for _ in QS]

                def gram(q, lt, rt_, off, ring=None):
                    p, r_p = prd.next()
                    for i, h in enumerate(hsl[q]):
                        S.op("pe", lambda e, p=p, i=i, h=h: e.matmul(p[:, i * 128:(i + 1) * 128], lhsT=fm[:, lt, h, :],
                                                                     rhs=fm[:, rt_, h, :], start=True, stop=True),
                             reads=[r_fm], writes=[r_p])
                    o_, r_o = (ring or gr[z][q]).next()
                    S.op("dve", lambda e, p=p, o_=o_: e.tensor_tensor(out=o_[:], in0=v3(p), in1=bc(MSK(z, off, off + 128), [128, HQ, 128], 1),
                                                                     op=ALU.mult), reads=[r_p, r_cst], writes=[r_o])
                    return o_, r_o
                for q in QS:
                    st_[q]["P"] = gram(q, 0, 2, 0, sqr[z][q])
                    st_[q]["Q"] = gram(q, 2, 0, 128, sqr[z][q])
                yield
                for q in QS:
                    st_[q]["ArbT"] = gram(q, 2, 1, 256)
                    st_[q]["AakT"] = gram(q, 3, 0, 384)
                    st_[q]["ArkT"] = gram(q, 3, 1, 512)
                    W, r_W = wrg[z][q].next()
                    Q, r_Q = st_[q]["Q"]
                    S.op("pool", lambda e, W=W, Q=Q: e.tensor_tensor(out=W[:], in0=Q[:], in1=bc(identb[:], [128, HQ, 128], 1), op=ALU.add),
                         reads=[r_Q, r_idb], writes=[r_W])
                    st_[q]["W"] = (W, r_W)
                yield
                for j in range(1, 7):
                    for q in QS:
                        P, r_P = st_[q]["P"]
                        Q, r_Q = st_[q]["Q"]
                        pP, r_pP = prd.next()
                        for i in range(HQ):
                            S.op("pe", lambda e, pP=pP, i=i, P=P, Q=Q: e.matmul(pP[:, i * 128:(i + 1) * 128], lhsT=Q[:, i, :], rhs=P[:, i, :],
                                                                               start=True, stop=True), reads=[r_P, r_Q], writes=[r_pP])
                        Pn, r_Pn = sqr[z][q].next()
                        S.op("act", lambda e, pP=pP, Pn=Pn: e.activation(out=Pn[:], in_=v3(pP), func=AF.Copy), reads=[r_pP], writes=[r_Pn])
                        if j < 6:
                            pQ, r_pQ = prd.next()
                            for i in range(HQ):
                                S.op("pe", lambda e, pQ=pQ, i=i, P=P, Q=Q: e.matmul(pQ[:, i * 128:(i + 1) * 128], lhsT=P[:, i, :], rhs=Q[:, i, :],
                                                                                   start=True, stop=True), reads=[r_P, r_Q], writes=[r_pQ])
                            Qn, r_Qn = sqr[z][q].next()
                            if q % 2 == 0:
                                S.op("act", lambda e, pQ=pQ, Qn=Qn: e.activation(out=Qn[:], in_=v3(pQ), func=AF.Copy), reads=[r_pQ], writes=[r_Qn])
                            else:
                                S.op("dve", lambda e, pQ=pQ, Qn=Qn: e.tensor_copy(out=Qn[:], in_=v3(pQ)), reads=[r_pQ], writes=[r_Qn])
                            st_[q]["Q"] = (Qn, r_Qn)
                        st_[q]["P"] = (Pn, r_Pn)
                    for q in QS:
                        P, r_P = st_[q]["P"]
                        W, r_W = st_[q]["W"]
                        pW, r_pW = prd.next()
                        for i in range(HQ):
                            S.op("pe", lambda e, pW=pW, i=i, P=P, W=W: e.matmul(pW[:, i * 128:(i + 1) * 128], lhsT=P[:, i, :], rhs=W[:, i, :],
                                                                               start=True, stop=True), reads=[r_P, r_W], writes=[r_pW])
                        Wn, r_Wn = wrg[z][q].next()
                        S.op("dve", lambda e, pW=pW, W=W, Wn=Wn: e.tensor_tensor(out=Wn[:], in0=v3(pW), in1=W[:], op=ALU.add),
                             reads=[r_pW, r_W], writes=[r_Wn])
                        st_[q]["W"] = (Wn, r_Wn)
                    yield
                for q in QS:
                    AakT, r_AakT = st_[q]["AakT"]
                    pA, r_pA = prd.next()
                    for i, h in enumerate(hsl[q]):
                        S.op("pe", lambda e, i=i, h=h, pA=pA, AakT=AakT: e.matmul(pA[:, i * 64:(i + 1) * 64], lhsT=AakT[:, i, :], rhs=vr[:, h * 64:(h + 1) * 64],
                                                                                 start=True, stop=True), reads=[r_AakT, r_vr], writes=[r_pA])
                    av, r_av = avr[z][q].next()
                    S.op("act", lambda e, pA=pA, av=av: e.activation(out=av[:], in_=pA[:, 0:HQ * 64].rearrange("p (a b) -> p a b", a=HQ), func=AF.Copy),
                         reads=[r_pA], writes=[r_av])
                    st_[q]["av"] = (av, r_av)
                yield
                for q in QS:
                    W, r_W = st_[q]["W"]
                    av, r_av = st_[q]["av"]
                    pZ, r_pZ = prd.next()
                    for i, h in enumerate(hsl[q]):
                        S.op("pe", lambda e, i=i, h=h, pZ=pZ, W=W: e.matmul(pZ[:, i * 128:i * 128 + 64], lhsT=W[:, i, :],
                                                                          rhs=tm[:, 0, h * 64:(h + 1) * 64], start=True, stop=True),
                             reads=[r_W, r_tm], writes=[r_pZ])
                        S.op("pe", lambda e, i=i, h=h, pZ=pZ, W=W, av=av: e.matmul(pZ[:, i * 128 + 64:(i + 1) * 128], lhsT=W[:, i, :],
                                                                                 rhs=av[:, i, :], start=True, stop=True),
                             reads=[r_W, r_av], writes=[r_pZ])
                    Ap, r_Ap = apr[z][q].next()
                    Vn, r_Vn = vnr[z][q].next()
                    S.op("act", lambda e, pZ=pZ, Ap=Ap: e.activation(out=Ap[:], in_=v3(pZ)[:, :, 0:64], func=AF.Copy), reads=[r_pZ], writes=[r_Ap])
                    S.op("act", lambda e, pZ=pZ, Vn=Vn: e.activation(out=Vn[:], in_=v3(pZ)[:, :, 64:128], func=AF.Copy, scale=-1.0),
                         reads=[r_pZ], writes=[r_Vn])
                    st_[q]["Ap"] = (Ap, r_Ap)
                    st_[q]["Vn"] = (Vn, r_Vn)
                yield
                for q in QS:
                    Ap, r_Ap = st_[q]["Ap"]
                    Vn, r_Vn = st_[q]["Vn"]
                    ArbT, r_ArbT = st_[q]["ArbT"]
                    ArkT, r_ArkT = st_[q]["ArkT"]
                    pM, r_pM = prd.next()
                    for i, h in enumerate(hsl[q]):
                        S.op("pe", lambda e, i=i, h=h, pM=pM, Ap=Ap: e.matmul(pM[0:64, i * 64:(i + 1) * 64], lhsT=Ap[:, i, :], rhs=tm[:, 2, h * 64:(h + 1) * 64],
                                                                             start=True, stop=True), reads=[r_Ap, r_tm], writes=[r_pM])
                    S.op("act", lambda e, pM=pM, q=q: e.activation(out=MT[:, q * HQ:(q + 1) * HQ, :], in_=pM[0:64, 0:HQ * 64].rearrange("p (a b) -> p a b", a=HQ),
                                                                   func=AF.Copy), reads=[r_pM], writes=[r_MT])
                    pR, r_pR = prd.next()
                    for i, h in enumerate(hsl[q]):
                        S.op("pe", lambda e, i=i, h=h, pR=pR, Ap=Ap, ArbT=ArbT: e.matmul(pR[0:64, i * 128:(i + 1) * 128], lhsT=Ap[:, i, :], rhs=ArbT[:, i, :],
                                                                                       start=True, stop=True), reads=[r_Ap, r_ArbT], writes=[r_pR])
                    S.op("dve", lambda e, pR=pR, q=q: e.tensor_tensor(out=Rp[:, q * HQ:(q + 1) * HQ, :], in0=fm[:, 1, q * HQ:(q + 1) * HQ, :],
                                                                      in1=pR[0:64, 0:HQ * 128].rearrange("p (a b) -> p a b", a=HQ), op=ALU.subtract),
                         reads=[r_pR, r_fm], writes=[r_Rp])
                    pG, r_pG = prd.next()
                    for i, h in enumerate(hsl[q]):
                        S.op("pe", lambda e, i=i, h=h, pG=pG: e.matmul(pG[0:64, i * 64:(i + 1) * 64], lhsT=tm[:, 1, h * 64:(h + 1) * 64],
                                                                     rhs=vr[:, h * 64:(h + 1) * 64], start=True, stop=False),
                             reads=[r_tm, r_vr], writes=[r_pG])
                        S.op("pe", lambda e, i=i, h=h, pG=pG, Vn=Vn: e.matmul(pG[0:64, i * 64:(i + 1) * 64], lhsT=tm[:, 2, h * 64:(h + 1) * 64],
                                                                             rhs=Vn[:, i, :], start=False, stop=True),
                             reads=[r_tm, r_Vn], writes=[r_pG])
                    S.op("act", lambda e, pG=pG, q=q: e.activation(out=Gs[:, q * HQ:(q + 1) * HQ, :], in_=pG[0:64, 0:HQ * 64].rearrange("p (a b) -> p a b", a=HQ),
                                                                   func=AF.Copy), reads=[r_pG], writes=[r_Gs])
                    pY0, r_pY0 = prd.next()
                    for i, h in enumerate(hsl[q]):
                        S.op("pe", lambda e, i=i, h=h, pY0=pY0, ArkT=ArkT: e.matmul(pY0[:, i * 64:(i + 1) * 64], lhsT=ArkT[:, i, :], rhs=vr[:, h * 64:(h + 1) * 64],
                                                                                   start=True, stop=False), reads=[r_ArkT, r_vr], writes=[r_pY0])
                        S.op("pe", lambda e, i=i, h=h, pY0=pY0, ArbT=ArbT, Vn=Vn: e.matmul(pY0[:, i * 64:(i + 1) * 64], lhsT=ArbT[:, i, :], rhs=Vn[:, i, :],
                                                                                          start=False, stop=True), reads=[r_ArbT, r_Vn], writes=[r_pY0])
                    S.op("act", lambda e, pY0=pY0, q=q: e.activation(out=Y0[:, q * HQ * 64:(q + 1) * HQ * 64], in_=pY0[:, 0:HQ * 64], func=AF.Copy),
                         reads=[r_pY0], writes=[r_Y0])
                yield
                return dict(MT=MT, r_MT=r_MT, Gs=Gs, r_Gs=r_Gs, Rp=Rp, r_Rp=r_Rp, Y0=Y0, r_Y0=r_Y0)

            def post(g, ysum, r_ysum):
                bon, r_bon = bsr.next()
                S.dma("sp", bon[:], BON_d[g * 128:(g + 1) * 128, :], writes=[r_bon], chan_res=r_bon)
                sg, r_sg = bsr.next()
                S.dma("sp", sg[:], SG_d[g * 128:(g + 1) * 128, :], writes=[r_sg], chan_res=r_sg)
                yf, r_yf = yfr.next()
                S.dma("sp", yf[:], YF_d[g * 128:(g + 1) * 128, :], reads=[r_yfd[g]], writes=[r_yf], chan_res=r_yf)
                yield
                yt, r_yt = ysum, r_ysum
                S.op("pool", lambda e: e.tensor_tensor(out=yt[:], in0=yt[:], in1=yf[:], op=ALU.add),
                     reads=[r_yt, r_yf], writes=[r_yt])
                yield
                sm, r_sm = s8d.next()
                S.op("dve", lambda e: e.tensor_reduce(out=sm[:], in_=yt[:].rearrange("p (h d) -> p h d", h=8), axis=AX.X, op=ALU.add),
                     reads=[r_yt], writes=[r_sm])
                yield
                S.op("dve", lambda e: e.tensor_scalar(out=sm[:], in0=sm[:], scalar1=-1.0 / 64, scalar2=None, op0=ALU.mult),
                     reads=[r_sm], writes=[r_sm])
                yield
                S.op("pool", lambda e: e.tensor_tensor(out=yt[:].rearrange("p (h d) -> p h d", h=8), in0=yt[:].rearrange("p (h d) -> p h d", h=8),
                                                       in1=bc(sm[:], [128, 8, 64], 2), op=ALU.add), reads=[r_yt, r_sm], writes=[r_yt])
                yield
                sq, r_sq = f5.next()
                S.op("act", lambda e: e.activation(out=sq[:], in_=yt[:], func=AF.Square), reads=[r_yt], writes=[r_sq])
                yield
                vv, r_vv = s8d.next()
                S.op("dve", lambda e: e.tensor_reduce(out=vv[:], in_=sq[:].rearrange("p (h d) -> p h d", h=8), axis=AX.X, op=ALU.add),
                     reads=[r_sq], writes=[r_vv])
                yield
                S.op("dve", lambda e: e.tensor_scalar(out=vv[:], in0=vv[:], scalar1=1.0 / 64, scalar2=EPS_GN, op0=ALU.mult, op1=ALU.add),
                     reads=[r_vv], writes=[r_vv])
                yield
                rsqrt_pool(vv[:], r_vv, 8)
                yield
                S.op("pool", lambda e: e.tensor_tensor(out=yt[:].rearrange("p (h d) -> p h d", h=8), in0=yt[:].rearrange("p (h d) -> p h d", h=8),
                                                       in1=bc(vv[:], [128, 8, 64], 2), op=ALU.mult), reads=[r_yt, r_vv], writes=[r_yt])
                yield
                S.op("pool", lambda e: e.tensor_tensor(out=yt[:], in0=yt[:], in1=RW("lnw"), op=ALU.mult), reads=[r_yt, r_rowp], writes=[r_yt])
                yield
                S.op("pool", lambda e: e.tensor_tensor(out=yt[:], in0=yt[:], in1=RW("lnb"), op=ALU.add), reads=[r_yt, r_rowp], writes=[r_yt])
                yield
                S.op("pool", lambda e: e.tensor_tensor(out=yt[:], in0=yt[:], in1=bon[:], op=ALU.add), reads=[r_yt, r_bon], writes=[r_yt])
                yield
                ob, r_ob = b5.next()
                S.op("pool", lambda e: e.tensor_tensor(out=ob[:], in0=yt[:], in1=sg[:], op=ALU.mult), reads=[r_yt, r_sg], writes=[r_ob])
                yield
                for cb in range(4):
                    S.op("pe", lambda e, cb=cb: e.transpose(out=pTd[:, cb, :], in_=ob[:, cb * 128:(cb + 1) * 128], identity=identb[:]),
                         reads=[r_ob, r_idb], writes=[r_pTd])
                ost, r_ost = ostr.next()
                S.op("act", lambda e: e.activation(out=ost[:], in_=pTd[:], func=AF.Copy), reads=[r_pTd], writes=[r_ost])
                yield
                S.dma("sp", ORT_d[:, :, g * 128:(g + 1) * 128].rearrange("h p t -> p h t"), ost[:], reads=[r_ost], chan_res=r_ost)
                yield

            def seq(g, z, B, H, r_H):
                Hm, r_Hm = hmr[z].next()
                S.op("dve", lambda e: e.tensor_tensor(out=Hm[:], in0=H[:], in1=bc(PCM[:, g, z * 8:(z + 1) * 8, 1], [64, 8, 64], 2), op=ALU.mult),
                     reads=[r_H, r_pcm[g]], writes=[r_Hm])
                Hb, r_Hb = hbr[z].next()
                S.op("act", lambda e: e.activation(out=Hb[:], in_=Hm[:], func=AF.Copy), reads=[r_Hm], writes=[r_Hb])
                MT, Gs, Rp, Y0 = B["MT"], B["Gs"], B["Rp"], B["Y0"]
                pH, r_pH = prd.next()
                for h in range(8):
                    S.op("pe", lambda e, h=h: e.matmul(pH[0:64, h * 64:(h + 1) * 64], lhsT=MT[:, h, :], rhs=Hm[:, h, :], start=True, stop=True),
                         reads=[B["r_MT"], r_Hm], writes=[r_pH])
                h1, r_h1 = h1r[z].next()
                S.op("dve", lambda e: e.tensor_tensor(out=h1[:], in0=H[:], in1=bc(PCM[:, g, z * 8:(z + 1) * 8, 0], [64, 8, 64], 2), op=ALU.mult),
                     reads=[r_H, r_pcm[g]], writes=[r_h1])
                S.op("dve", lambda e: e.tensor_tensor(out=h1[:], in0=h1[:], in1=pH[0:64, :].rearrange("p (a b) -> p a b", a=8), op=ALU.subtract),
                     reads=[r_h1, r_pH], writes=[r_h1])
                S.op("dve", lambda e: e.tensor_tensor(out=H[:], in0=h1[:], in1=Gs[:], op=ALU.add), reads=[r_h1, B["r_Gs"]], writes=[r_H])
                pY, r_pY = prd.next()
                for h in range(8):
                    S.op("pe", lambda e, h=h: e.matmul(pY[:, h * 64:(h + 1) * 64], lhsT=Rp[:, h, :], rhs=Hb[:, h, :], start=True, stop=True),
                         reads=[B["r_Rp"], r_Hb], writes=[r_pY])
                ys, r_ys = f5.next()
                S.op("dve", lambda e: e.tensor_tensor(out=ys[:], in0=pY[:], in1=Y0[:], op=ALU.add), reads=[r_pY, B["r_Y0"]], writes=[r_ys])
                if g not in ydone:
                    ydone[g] = True
                    S.dma("sp", YF_d[g * 128:(g + 1) * 128, :], ys[:], reads=[r_ys], writes=[r_yfd[g]], chan_res=r_ys)
                else:
                    return post(g, ys, r_ys)
                return None

            segs = [(0, 32, True)] + [(32 + 2 * i, 2, False) for i in range(4)]

            def init_H(z, lat):
                H, r_H = Hr[z].next()
                if lat:
                    if z == 0:
                        S.dma("sp", stl[:], st0.rearrange("a v k -> v a k"), writes=[r_stl], chan_res=r_stl)
                    for hp in range(4):
                        pS, r_pS = prd.next()
                        for i in range(2):
                            h = hp * 2 + i
                            S.op("pe", lambda e, pS=pS, i=i, h=h: e.transpose(out=pS[0:64, i * 64:(i + 1) * 64], in_=stl[:, z * 8 + h, :],
                                                                             identity=identf[0:64, 0:64]), reads=[r_stl, r_cst], writes=[r_pS])
                        S.op("act", lambda e, pS=pS, hp=hp: e.activation(out=H[:, hp * 2:hp * 2 + 2, :],
                                                                         in_=pS[0:64, 0:128].rearrange("p (a b) -> p a b", a=2), func=AF.Copy),
                             reads=[r_pS], writes=[r_H])
                else:
                    S.op("dve", lambda e: e.memset(H[:], 0.0), writes=[r_H])
                return H, r_H

            def final_out(z, si, H, r_H):
                so, r_so = stl[:, 0:8, :], r_stl
                for hp in range(4):
                    pS, r_pS = prd.next()
                    for i in range(2):
                        h = hp * 2 + i
                        S.op("pe", lambda e, pS=pS, i=i, h=h: e.transpose(out=pS[0:64, i * 64:(i + 1) * 64], in_=H[:, h, :],
                                                                         identity=identf[0:64, 0:64]), reads=[r_H, r_cst], writes=[r_pS])
                    S.op("act", lambda e, pS=pS, hp=hp: e.activation(out=so[:, hp * 2:hp * 2 + 2, :],
                                                                     in_=pS[0:64, 0:128].rearrange("p (a b) -> p a b", a=2), func=AF.Copy),
                         reads=[r_pS], writes=[r_so])
                S.dma("sp", ns[si - 1, z * 8:(z + 1) * 8, :, :].rearrange("h v k -> v h k"), so, reads=[r_so], chan_res=r_so)

            def chain(z):
                items = []
                for si, (g0, n, lat) in enumerate(segs):
                    order = list(range(g0, g0 + n)) if z == 0 else list(range(g0 + n - 1, g0 - 1, -1))
                    for i_, g in enumerate(order):
                        items.append((si, g, i_, n, lat))

                pp = [None]

                def drain_post():
                    if pp[0] is not None:
                        for _ in pp[0]:
                            pass
                        pp[0] = None

                def do_seq(p):
                    g, B, (H, r_H), si, last, lat = p
                    drain_post()
                    pp[0] = seq(g, z, B, H, r_H)
                    if last and not lat:
                        final_out(z, si, H, r_H)
                prev = None
                cur = loads(items[0][1], z, items[0][2] * 2 >= items[0][3])
                Hc = None
                for k, (si, g, i_, n, lat) in enumerate(items):
                    if i_ == 0:
                        Hc = init_H(z, lat)
                        yield
                    nxt = None
                    if k + 1 < len(items):
                        nxt = loads(items[k + 1][1], z, items[k + 1][2] * 2 >= items[k + 1][3])
                    pg_ = pre(g, z, cur)
                    while True:
                        try:
                            next(pg_)
                        except StopIteration as e_:
                            B = e_.value
                            break
                        if pp[0] is not None and INTERLEAVE_POST:
                            try:
                                next(pp[0])
                            except StopIteration:
                                pp[0] = None
                        yield
                    cur = nxt
                    if prev is not None:
                        do_seq(prev)
                        yield
                    prev = (g, B, Hc, si, i_ == n - 1, lat)
                do_seq(prev)
                drain_post()
                yield

            gens = [chain(0), chain(1)]
            for _ in range(STAGGER):
                next(gens[0])
            while gens:
                for gn in list(gens):
                    try:
                        next(gn)
                    except StopIteration:
                        gens.remove(gn)
            S.end_phase()
        chk(2)

        S.begin_phase()
        with ExitStack() as ph:
            KT = SB(ph, "KTs", [128, 4, 4352], BF16)
            r_KTk = [Res() for _ in range(34)]
            r_VAk = [Res() for _ in range(34)]
            kch = [Res() for _ in range(4)]
            vch = [Res() for _ in range(8)]
            VA = SB(ph, "VAs", [128, 34, 4, 129], BF16)
            S.op("pool", lambda e: e.memset(VA[:, :, :, 128:129], 1.0), writes=r_VAk)
            qtr = Ring(lambda i: SB(ph, "qt%d" % i, [128, 4, 512], BF16), 2)
            ptr_ = Ring(lambda i: SB(ph, "pt%d" % i, [128, 2, 512], BF16), 3)
            ckf = Ring(lambda i: SB(ph, "ckf%d" % i, [128, 512], F32), 2)
            ckb = Ring(lambda i: SB(ph, "ckb%d" % i, [128, 512], BF16), 2)
            ot = SB(ph, "ot", [128, 4, 4, 128], F32)
            r_ot = Res()
            t128 = Ring(lambda i: SB(ph, "t128_%d" % i, [128, 128], F32), 3)
            rrr = Ring(lambda i: SB(ph, "rr%d" % i, [128, 8], F32), 2)
            accr = Ring(lambda i: SB(ph, "accs%d" % i, [128, 3, 387], F32), 2)
            osq = SB(ph, "osq", [128, 2048], F32)
            r_osq = Res()
            s16 = Ring(lambda i: SB(ph, "s16_%d" % i, [128, 16], F32), 2)
            onb = SB(ph, "onb", [128, 4, 4, 128], BF16)
            r_onb = Res()
            oast = Ring(lambda i: SB(ph, "oast%d" % i, [128, 4, 512], BF16), 2)
            subw8 = SB(ph, "subw8", [128, 128], F32)
            r_subw8 = Res()
            S.op("dve", lambda e: e.tensor_scalar(out=subw8[:], in0=RW("subw"), scalar1=0.8, scalar2=None, op0=ALU.mult),
                 reads=[r_rowp], writes=[r_subw8])
            pTc = PS(ph, "pTc", [128, 4, 128], BF16)
            r_pTc = Res(True)
            psr = Ring(lambda i: PS(ph, "psr%d" % i, [128, 2, 512]), 2, psum=True)
            accb = [PS(ph, "acc%d" % i, [128, 512]) for i in range(3)]
            r_acc = [Res(True) for _ in range(3)]
            slot = {}
            for idx in range(8):
                slot[(idx // 4, idx % 4)] = (idx // 3, (idx % 3) * 129)

            segs = [(0, 4096, True)] + [(4096 + 256 * i, 256, False) for i in range(4)]
            for si_, (t0, T, lat) in enumerate(segs):
                nkt = (T + (256 if lat else 0)) // 128
                koff = 256 if lat else 0
                kb = 0 if lat else 2 * (si_ - 1)
                if lat:
                    for i in range(2):
                        cf, r_cf = ckf.next()
                        S.dma("sp", cf[:], ck[i * 128:(i + 1) * 128, :], writes=[r_cf], chan_res=r_cf)
                        cb_, r_cb = ckb.next()
                        S.op("dve", lambda e, cf=cf, cb_=cb_: e.tensor_copy(out=cb_[:], in_=cf[:]), reads=[r_cf], writes=[r_cb])
                        for h in range(4):
                            S.op("pe", lambda e, cb_=cb_, h=h: e.transpose(out=pTc[:, h, :], in_=cb_[:, h * 128:(h + 1) * 128], identity=identb[:]),
                                 reads=[r_cb, r_idb], writes=[r_pTc])
                        S.op("act", lambda e, i=i: e.activation(out=KT[:, :, i * 128:(i + 1) * 128], in_=pTc[:], func=AF.Copy),
                             reads=[r_pTc], writes=[r_KTk[i]])
                    for kt in range(2):
                        S.dma("pool", VA[:, kt, :, 0:128], cv[kt * 128:(kt + 1) * 128, :].rearrange("p (h e) -> p h e", h=4),
                              writes=[r_VAk[kt], vch[kt % 8]], chan_res=vch[kt % 8])
                k0 = kb + koff // 128
                nown = T // 128
                grp = 8 if lat else 2
                for gi in range(nown // grp):
                    ks = list(range(k0 + gi * grp, k0 + (gi + 1) * grp))
                    S.dma("sp", KT[:, :, ks[0] * 128:(ks[-1] + 1) * 128],
                          KT_d[:, :, t0 + gi * grp * 128:t0 + (gi + 1) * grp * 128].rearrange("h p t -> p h t"),
                          writes=[r_KTk[k] for k in ks] + [kch[gi % 4]], chan_res=kch[gi % 4])
                for kt in range(nown):
                    S.dma("sp", VA[:, k0 + kt, :, 0:128], V_d[t0 + kt * 128:t0 + (kt + 1) * 128, :].rearrange("p (h e) -> p h e", h=4),
                          writes=[r_VAk[k0 + kt], vch[kt % 8]], chan_res=vch[kt % 8])
                QB = min(512, T)
                nqs = QB // 128
                def qload(qb_):
                    q0_ = t0 + qb_ * QB
                    QT_, r_QT_ = qtr.next()
                    S.dma("sp", QT_[:, :, 0:QB], QT_d[:, :, q0_:q0_ + QB].rearrange("h p t -> p h t"), writes=[r_QT_], chan_res=r_QT_)
                    return QT_, r_QT_
                qnext = qload(0)
                for qb in range(T // QB):
                    q0 = t0 + qb * QB
                    QT, r_QT = qnext
                    if qb + 1 < T // QB:
                        qnext = qload(qb + 1)
                    for h in range(4):
                        started = set()

                        def qk(kt):
                            pS, r_pS = psr.next()
                            for m in range(2):
                                S.op("pe", lambda e, pS=pS, m=m: e.matmul(pS[:, m, 0:QB], lhsT=KT[m * 64:(m + 1) * 64, h, (kb + kt) * 128:(kb + kt + 1) * 128],
                                                                          rhs=QT[m * 64:(m + 1) * 64, h, 0:QB], start=True, stop=True),
                                     reads=[r_KTk[kb + kt], r_QT], writes=[r_pS])
                            PT, r_PT = ptr_.next()
                            S.op("act", lambda e, pS=pS, PT=PT: e.activation(out=PT[:, :, 0:QB], in_=pS[:, :, 0:QB], func=AF.Exp, scale=0.125),
                                 reads=[r_pS], writes=[r_PT])
                            return PT, r_PT

                        def av(kt, PT, r_PT):
                            last = (kt == nkt - 1)
                            for m in range(2):
                                for qs in range(nqs):
                                    bk, co = slot[(m, qs)]
                                    first = bk not in started
                                    started.add(bk)
                                    S.op("pe", lambda e, qs=qs, bk=bk, co=co, first=first, m=m: e.matmul(
                                        accb[bk][:, co:co + 129], lhsT=PT[:, m, qs * 128:(qs + 1) * 128], rhs=VA[:, kb + kt, h, :],
                                        start=first, stop=last, skip_group_check=True), reads=[r_PT, r_VAk[kb + kt]], writes=[r_acc[bk]])
                        cur = qk(0)
                        for kt in range(nkt):
                            nxt = qk(kt + 1) if kt + 1 < nkt else None
                            av(kt, cur[0], cur[1])
                            cur = nxt
                        accs, r_accs = accr.next()
                        for bk in range(3):
                            S.op("dve", lambda e, bk=bk, accs=accs: e.tensor_copy(out=accs[:, bk, :], in_=accb[bk][:, 0:387]),
                                 reads=[r_acc[bk]], writes=[r_accs])
                        rr, r_rr = rrr.next()
                        nacc = 4 + nqs
                        av_ = accs[:].rearrange("p a b -> p (a b)")[:, 0:8 * 129].rearrange("p (i c) -> p i c", c=129)
                        S.op("dve", lambda e, rr=rr, av_=av_: e.reciprocal(out=rr[:, 0:nacc], in_=av_[:, 0:nacc, 128]),
                             reads=[r_accs], writes=[r_rr])
                        S.op("dve", lambda e, rr=rr: e.tensor_scalar(out=rr[:, 4:8], in0=rr[:, 4:8], scalar1=lamc[:, 0:1], scalar2=None, op0=ALU.mult),
                             reads=[r_rr, r_lamc], writes=[r_rr])
                        for qs in range(nqs):
                            tt_, r_tt = t128.next()
                            S.op("dve", lambda e, qs=qs, tt_=tt_, av_=av_, rr=rr: e.tensor_scalar(out=tt_[:], in0=av_[:, 4 + qs, 0:128],
                                                                                            scalar1=rr[:, 4 + qs:5 + qs], scalar2=None, op0=ALU.mult),
                                 reads=[r_accs, r_rr], writes=[r_tt])
                            S.op("dve", lambda e, qs=qs, tt_=tt_, av_=av_, rr=rr: e.scalar_tensor_tensor(
                                out=ot[:, qs, h, :], in0=av_[:, qs, 0:128], scalar=rr[:, qs:qs + 1], in1=tt_[:],
                                op0=ALU.mult, op1=ALU.add), reads=[r_accs, r_rr, r_tt], writes=[r_ot])
                    nn = nqs * 4
                    S.op("act", lambda e: e.activation(out=osq[:, 0:nn * 128], in_=ot[:, 0:nqs, :, :].rearrange("p a b c -> p (a b c)"), func=AF.Square),
                         reads=[r_ot], writes=[r_osq])
                    ss, r_ss = s16.next()
                    S.op("dve", lambda e: e.tensor_reduce(out=ss[:, 0:nn], in_=osq[:, 0:nn * 128].rearrange("p (a c) -> p a c", c=128), axis=AX.X, op=ALU.add),
                         reads=[r_osq], writes=[r_ss])
                    S.op("dve", lambda e: e.tensor_scalar(out=ss[:, 0:nn], in0=ss[:, 0:nn], scalar1=1.0 / 128, scalar2=EPS_RMS, op0=ALU.mult, op1=ALU.add),
                         reads=[r_ss], writes=[r_ss])
                    rsqrt_pool(ss[:, 0:nn], r_ss, nn)
                    S.op("dve", lambda e: e.tensor_tensor(out=osq[:, 0:nn * 128].rearrange("p (a c) -> p a c", c=128),
                                                          in0=ot[:, 0:nqs, :, :].rearrange("p a b c -> p (a b) c"),
                                                          in1=bc(ss[:, 0:nn], [128, nn, 128], 2), op=ALU.mult), reads=[r_ot, r_ss], writes=[r_osq])
                    S.op("dve", lambda e: e.tensor_tensor(out=onb[:, 0:nqs, :, :].rearrange("p a b c -> p (a b) c"),
                                                          in0=osq[:, 0:nn * 128].rearrange("p (a c) -> p a c", c=128),
                                                          in1=bc(subw8[:], [128, nn, 128], 1), op=ALU.mult), reads=[r_osq, r_subw8], writes=[r_onb])
                    oa, r_oa = oast.next()
                    for qs in range(nqs):
                        for h in range(4):
                            S.op("pe", lambda e, qs=qs, h=h: e.transpose(out=pTc[:, h, :], in_=onb[:, qs, h, :], identity=identb[:]),
                                 reads=[r_onb, r_idb], writes=[r_pTc])
                        S.op("act", lambda e, qs=qs, oa=oa: e.activation(out=oa[:, :, qs * 128:(qs + 1) * 128], in_=pTc[:], func=AF.Copy),
                             reads=[r_pTc], writes=[r_oa])
                    S.dma("sp", OAT_d[:, :, q0:q0 + QB].rearrange("h p t -> p h t"), oa[:, :, 0:QB], reads=[r_oa], chan_res=r_oa)
            S.end_phase()
        chk(3)

        S.begin_phase()
        with ExitStack() as ph:
            wf2 = SB(ph, "wf2", [128, 22, 1024], BF16)
            r_wf2 = Res()
            for c in range(6):
                n = min(4, 22 - c * 4)
                r_tmp = Res()
                S.dma("pool", wf2[:, c * 4:c * 4 + n, :], w_f2_b[c * 512:c * 512 + n * 128, :].rearrange("(k p) n -> p k n", p=128),
                      reads=[r_wcv["w_f2"]], writes=[r_wf2], chan_res=r_wf2)
            wre = Ring(lambda i: SB(ph, "we%d" % i, [128, 8, 512], BF16), 4)
            gbc = SB(ph, "gbc", [128, 2, 1024], F32)
            r_gbc = Res()
            gb = Ring(lambda i: SB(ph, "gb%d" % i, [128, 128], F32), 2)
            aT = SB(ph, "aT", [128, 22, 512], BF16)
            r_aT = Res()
            h2T = SB(ph, "h2T", [128, 8, 512], BF16)
            r_h2T = Res()
            mgT = SB(ph, "mgT", [128, 8, 512], BF16)
            r_mgT = Res()
            oar = SB(ph, "oar", [128, 4, 512], BF16)
            orr = SB(ph, "orr", [128, 4, 512], BF16)
            r_oar = Res()
            r_orr = Res()
            gtl = Ring(lambda i: SB(ph, "gtl%d" % i, [128, 512], BF16), 8)
            f6 = Ring(lambda i: SB(ph, "f6_%d" % i, [128, 512], F32), 3)
            xe = Ring(lambda i: SB(ph, "xe%d" % i, [128, 1024], F32), 4)
            x1r = Ring(lambda i: SB(ph, "x1_%d" % i, [128, 1024], F32), 2)
            xb2 = SB(ph, "xb2", [128, 1024], BF16)
            r_xb2 = Res()
            junk2 = xb2
            r_junk2 = r_xb2
            st5 = Ring(lambda i: SB(ph, "st5_%d" % i, [128, 4], F32), 4)
            pTe = PS(ph, "pTe", [128, 8, 128], BF16)
            r_pTe = Res(True)
            pe_ = Ring(lambda i: PS(ph, "pe%d" % i, [128, 512]), 6, psum=True)
            r_x1d = [Res() for _ in range(NT)]

            def fill_gbc(cnd, wh_):
                for wh, base in (((0, 16),) if wh_ == 0 else ((1, 40),)):
                    for half in range(2):
                        p, r_p = pe_.next()
                        for jj in range(4):
                            j = half * 4 + jj
                            g_, r_g = gb.next()
                            S.op("dve", lambda e, g_=g_, j=j, base=base, cnd=cnd: e.tensor_scalar(
                                out=g_[:], in0=ones_f[:], scalar1=mod[:, base + j, cnd:cnd + 1], scalar2=None, op0=ALU.mult),
                                reads=[r_ones, r_mod], writes=[r_g])
                            S.op("pe", lambda e, p=p, jj=jj, g_=g_: e.matmul(p[:, jj * 128:(jj + 1) * 128], lhsT=g_[:], rhs=identf,
                                                                             start=True, stop=True), reads=[r_g, r_cst], writes=[r_p])
                        S.op("act", lambda e, p=p, cnd=cnd, wh=wh, half=half: e.activation(
                            out=gbc[:, wh, half * 512:(half + 1) * 512], in_=p[:], func=AF.Copy), reads=[r_p], writes=[r_gbc])

            def wl_(nm, shape_k, c0, ncol):
                src = {"w_abr": w_abr_b, "w_rbr": w_rbr_b, "w_out": w_out_b, "w_f1": w_f1_b}[nm]
                w, r_w = wre.next()
                S.dma("pool", w[:, 0:shape_k, 0:ncol], src[:, c0:c0 + ncol].rearrange("(k p) n -> p k n", p=128),
                      reads=[r_wcv[nm]], writes=[r_w], chan_res=r_w)
                return w, r_w

            XT = {}

            def merge_part(blk):
                cnd = 0 if blk < 8 else 1
                tk0 = blk * 512
                S.dma("sp", oar[:], OAT_d[:, :, tk0:tk0 + 512].rearrange("h p t -> p h t"), writes=[r_oar], chan_res=r_oar)
                S.dma("sp", orr[:], ORT_d[:, :, tk0:tk0 + 512].rearrange("h p t -> p h t"), writes=[r_orr], chan_res=r_orr)
                for half in range(2):
                    wa, r_wa = wl_("w_abr", 4, half * 512, 512)
                    wb, r_wb = wl_("w_rbr", 4, half * 512, 512)
                    gl = []
                    for jj in range(4):
                        j = half * 4 + jj
                        ga, r_ga = gtl.next()
                        S.dma("sp", ga[:], GT_d[j, :, tk0:tk0 + 512], writes=[r_ga], chan_res=r_ga)
                        gr_, r_gr = gtl.next()
                        S.dma("sp", gr_[:], GT_d[8 + j, :, tk0:tk0 + 512], writes=[r_gr], chan_res=r_gr)
                        gl.append((ga, r_ga, gr_, r_gr))
                    for jj in range(4):
                        j = half * 4 + jj
                        ga, r_ga, gr_, r_gr = gl[jj]
                        pa_, r_pa = pe_.next()
                        for kc in range(4):
                            S.op("pe", lambda e, pa_=pa_, kc=kc, wa=wa, jj=jj: e.matmul(pa_[:], lhsT=wa[:, kc, jj * 128:(jj + 1) * 128], rhs=oar[:, kc, :],
                                                                                       start=(kc == 0), stop=(kc == 3)), reads=[r_wa, r_oar], writes=[r_pa])
                        pb_, r_pb = pe_.next()
                        for kc in range(4):
                            S.op("pe", lambda e, pb_=pb_, kc=kc, wb=wb, jj=jj: e.matmul(pb_[:], lhsT=wb[:, kc, jj * 128:(jj + 1) * 128], rhs=orr[:, kc, :],
                                                                                       start=(kc == 0), stop=(kc == 3)), reads=[r_wb, r_orr], writes=[r_pb])
                        ta, r_ta = f6.next()
                        S.op("dve", lambda e, ta=ta, pa_=pa_, ga=ga: e.tensor_tensor(out=ta[:], in0=pa_[:], in1=ga[:], op=ALU.mult),
                             reads=[r_pa, r_ga], writes=[r_ta])
                        tb_, r_tb = f6.next()
                        S.op("dve", lambda e, tb_=tb_, pb_=pb_, gr_=gr_: e.tensor_tensor(out=tb_[:], in0=pb_[:], in1=gr_[:], op=ALU.mult),
                             reads=[r_pb, r_gr], writes=[r_tb])
                        S.op("dve", lambda e, ta=ta, tb_=tb_, j=j: e.tensor_tensor(out=mgT[:, j, :], in0=ta[:], in1=tb_[:], op=ALU.add),
                             reads=[r_ta, r_tb], writes=[r_mgT])

            def outproj_part(blk):
                cnd = 0 if blk < 8 else 1
                tk0 = blk * 512
                xts = []
                for ti in range(4):
                    g = blk * 4 + ti
                    xt, r_xt = xe.next()
                    S.dma("sp", xt[:], xall[g * 128:(g + 1) * 128, :], writes=[r_xt], chan_res=r_xt)
                    xts.append((xt, r_xt))
                XT[blk] = xts
                wo = [wl_("w_out", 8, nb * 512, 512) for nb in range(2)]
                for ti in range(4):
                    g = blk * 4 + ti
                    xt, r_xt = xts[ti]
                    x1, r_x1 = x1r.next()
                    for nb in range(2):
                        po, r_po = pe_.next()
                        for j in range(8):
                            S.op("pe", lambda e, po=po, j=j, nb=nb: e.matmul(po[:], lhsT=mgT[:, j, ti * 128:(ti + 1) * 128], rhs=wo[nb][0][:, j, :],
                                                                            start=(j == 0), stop=(j == 7)), reads=[r_mgT, wo[nb][1]], writes=[r_po])
                        tt_, r_tt = f6.next()
                        S.op("dve", lambda e, tt_=tt_, po=po, nb=nb: e.tensor_tensor(out=tt_[:], in0=po[:], in1=gbc[:, 0, nb * 512:(nb + 1) * 512],
                                                                                    op=ALU.mult), reads=[r_po, r_gbc], writes=[r_tt])
                        S.op("dve", lambda e, tt_=tt_, nb=nb, x1=x1, xt=xt: e.tensor_tensor(out=x1[:, nb * 512:(nb + 1) * 512], in0=tt_[:],
                                                                                            in1=xt[:, nb * 512:(nb + 1) * 512], op=ALU.add),
                             reads=[r_tt, r_xt], writes=[r_x1])
                    S.dma("sp", X1_d[g * 128:(g + 1) * 128, :], x1[:], reads=[r_x1], writes=[r_x1d[g]], chan_res=r_x1)
                    ss, r_ss = st5.next()
                    S.op("act", lambda e, x1=x1, ss=ss: e.activation(out=junk2[:], in_=x1[:], func=AF.Square, accum_out=ss[:, 0:1]),
                         reads=[r_x1], writes=[r_junk2, r_ss])
                    S.op("dve", lambda e, ss=ss: e.tensor_scalar(out=ss[:, 0:1], in0=ss[:, 0:1], scalar1=1.0 / 1024, scalar2=EPS_RMS,
                                                                 op0=ALU.mult, op1=ALU.add), reads=[r_ss], writes=[r_ss])
                    rsqrt_pool(ss[:, 0:1], r_ss, 1)
                    S.op("act", lambda e, x1=x1, ss=ss: e.activation(out=xb2[:], in_=x1[:], func=AF.Copy, scale=ss[:, 0:1]),
                         reads=[r_x1, r_ss], writes=[r_xb2])
                    for kc in range(8):
                        S.op("pe", lambda e, kc=kc: e.transpose(out=pTe[:, kc, :], in_=xb2[:, kc * 128:(kc + 1) * 128], identity=identb[:]),
                             reads=[r_xb2, r_idb], writes=[r_pTe])
                    for kc in range(8):
                        S.op("dve", lambda e, kc=kc, ti=ti: e.tensor_scalar(out=h2T[:, kc, ti * 128:(ti + 1) * 128], in0=pTe[:, kc, :],
                                                                           scalar1=s2[:, kc, cnd:cnd + 1], scalar2=mod[:, 24 + kc, cnd:cnd + 1],
                                                                           op0=ALU.mult, op1=ALU.add), reads=[r_pTe, r_mod], writes=[r_h2T])

            def ffn_in_part(blk):
                cnd = 0 if blk < 8 else 1
                for c in range(6):
                    n = min(4, 22 - c * 4)
                    wu, r_wu = wl_("w_f1", 8, c * 512, n * 128)
                    wg, r_wg = wl_("w_f1", 8, D_FF + c * 512, n * 128)
                    for jj in range(n):
                        j = c * 4 + jj
                        pu, r_pu = pe_.next()
                        for kc in range(8):
                            S.op("pe", lambda e, pu=pu, kc=kc, wu=wu, jj=jj: e.matmul(pu[:], lhsT=wu[:, kc, jj * 128:(jj + 1) * 128], rhs=h2T[:, kc, :],
                                                                                     start=(kc == 0), stop=(kc == 7)), reads=[r_wu, r_h2T], writes=[r_pu])
                        pg, r_pg = pe_.next()
                        for kc in range(8):
                            S.op("pe", lambda e, pg=pg, kc=kc, wg=wg, jj=jj: e.matmul(pg[:], lhsT=wg[:, kc, jj * 128:(jj + 1) * 128], rhs=h2T[:, kc, :],
                                                                                     start=(kc == 0), stop=(kc == 7)), reads=[r_wg, r_h2T], writes=[r_pg])
                        su, r_su = f6.next()
                        S.op("act", lambda e, su=su, pu=pu: e.activation(out=su[:], in_=pu[:], func=AF.Silu), reads=[r_pu], writes=[r_su])
                        S.op("dve", lambda e, su=su, pg=pg, j=j: e.tensor_tensor(out=aT[:, j, :], in0=pg[:], in1=su[:], op=ALU.mult),
                             reads=[r_pg, r_su], writes=[r_aT])

            def ffn_out_part(blk, tis=(0, 1, 2, 3)):
                cnd = 0 if blk < 8 else 1
                xts = XT[blk]
                for ti in tis:
                    g = blk * 4 + ti
                    x1, r_x1 = x1r.next()
                    S.dma("sp", x1[:], X1_d[g * 128:(g + 1) * 128, :], reads=[r_x1d[g]], writes=[r_x1], chan_res=r_x1)
                    yo, r_yo = xts[ti]
                    for nb in range(2):
                        po, r_po = pe_.next()
                        for j in range(22):
                            S.op("pe", lambda e, po=po, j=j, nb=nb: e.matmul(po[:], lhsT=aT[:, j, ti * 128:(ti + 1) * 128], rhs=wf2[:, j, nb * 512:(nb + 1) * 512],
                                                                            start=(j == 0), stop=(j == 21)), reads=[r_aT, r_wf2], writes=[r_po])
                        tt_, r_tt = f6.next()
                        S.op("dve", lambda e, tt_=tt_, po=po, nb=nb: e.tensor_tensor(out=tt_[:], in0=po[:], in1=gbc[:, 1, nb * 512:(nb + 1) * 512],
                                                                                    op=ALU.mult), reads=[r_po, r_gbc], writes=[r_tt])
                        S.op("dve", lambda e, tt_=tt_, nb=nb, x1=x1, yo=yo: e.tensor_tensor(out=yo[:, nb * 512:(nb + 1) * 512], in0=tt_[:],
                                                                                            in1=x1[:, nb * 512:(nb + 1) * 512], op=ALU.add),
                             reads=[r_tt, r_x1], writes=[r_yo])
                    S.dma("sp", yall[g * 128:(g + 1) * 128, :], yo[:], reads=[r_yo], chan_res=r_yo)

            fill_gbc(0, 0)
            fill_gbc(0, 1)
            merge_part(0)
            outproj_part(0)
            for blk in range(10):
                ffn_in_part(blk)
                if blk + 1 < 10:
                    merge_part(blk + 1)
                if blk == 8:
                    fill_gbc(1, 1)
                ffn_out_part(blk, (0, 1))
                if blk + 1 < 10:
                    if blk + 1 == 8:
                        fill_gbc(1, 0)
                    outproj_part(blk + 1)
                ffn_out_part(blk, (2, 3))
            S.end_phase()
        chk(5)
        S.barrier()


def _host_consts():
    c0 = math.exp(-0.5)
    s = np.arange(128)[:, None]
    t = np.arange(128)[None, :]
    cst = np.zeros((128, 2184), np.float32)
    cst[:, 0:128] = np.eye(128, dtype=np.float32)
    f = lambda m: m.astype(np.float32)
    mats = [
        -c0 * (f(s <= t) - f(s <= 63)), -c0 * (f(s < t) - f(s <= 63)), -c0 * f(s > t),
        -c0 * (f(s >= t) - f(s >= 64)), -c0 * (f(s > t) - f(s >= 64)), -c0 * f(s < t),
    ]
    for i, m in enumerate(mats):
        cst[:, 128 + i * 128: 256 + i * 128] = m
    sv = np.arange(128)
    cst[:, 896] = -c0
    cst[:, 897] = -c0 * (sv <= 63)
    cst[:, 898] = -c0
    cst[:, 899] = -c0 * (sv >= 64)
    for z in range(2):
        o = 904 + z * 640
        r = np.arange(128)[:, None]
        c = np.arange(128)[None, :]
        if z == 0:
            before_rc = f(c < r)
            before_st = f(r < c)
            incl_st = f(r <= c)
        else:
            before_rc = f(c > r)
            before_st = f(r > c)
            incl_st = f(r >= c)
        cst[:, o:o + 128] = -before_rc
        cst[:, o + 128:o + 256] = -before_st
        cst[:, o + 256:o + 384] = incl_st
        cst[:, o + 384:o + 512] = before_st
        cst[:, o + 512:o + 640] = incl_st
    tok = np.arange(4096)
    row = (tok // 64).astype(np.float32)
    col = (tok % 64).astype(np.float32)
    inv = (np.float32(10000.0) ** (-np.arange(16, dtype=np.float32) / np.float32(16))).astype(np.float32)
    ar = (row[:, None] * inv).astype(np.float32)
    ac = (col[:, None] * inv).astype(np.float32)
    cos = np.concatenate([np.cos(ar), np.cos(ar), np.cos(ac), np.cos(ac)], -1)
    sin = np.concatenate([-np.sin(ar), np.sin(ar), -np.sin(ac), np.sin(ac)], -1)
    rope = np.stack([cos, sin], 1).astype(np.float32)
    rope = rope.reshape(32, 128, 2, 64).transpose(1, 0, 2, 3)
    return cst, np.ascontiguousarray(rope)


def _in_maps(inp):
    A = lambda a: np.ascontiguousarray(np.asarray(a, dtype=np.float32))
    cst, rope = _host_consts()
    colp = np.concatenate([A(inp["norm1_w"])[0].reshape(8, 128).T, A(inp["norm2_w"])[0].reshape(8, 128).T,
                           A(inp["ada_b"])[0].reshape(48, 128).T], 1)
    rowp = np.concatenate([A(inp["q_norm_w"])[0], A(inp["k_norm_w"])[0], A(inp["subln_w"])[0], A(inp["k_k"])[0],
                           A(inp["k_a"])[0], A(inp["r_k"])[0].reshape(-1), A(inp["ln_x_w"])[0], A(inp["ln_x_b"])[0],
                           A(inp["lambda_q1"])[0], A(inp["lambda_k1"])[0], A(inp["lambda_q2"])[0], A(inp["lambda_k2"])[0],
                           A(inp["w0"])[0].reshape(-1), A(inp["a0"])[0].reshape(-1)])[None, :]
    shared = {
        "colp": A(colp), "rowp": A(rowp), "cst": cst, "rope": rope,
        "ada_w": A(inp["ada_w"])[0], "w_in": A(inp["w_in"])[0],
        "wlu": A(inp["w_lora_up"])[0].reshape(128, 512), "alu": A(inp["a_lora_up"])[0].reshape(128, 512),
        "w_abr": A(inp["w_attn_br"])[0], "w_rbr": A(inp["w_rwkv_br"])[0], "w_out": A(inp["w_out"])[0],
        "w_f1": A(inp["w_ffn_in"])[0], "w_f2": A(inp["w_ffn_out"])[0],
    }
    maps = []
    xp = A(inp["x_prompt"])
    xs = A(inp["x_sample"])
    for b in range(8):
        m = dict(shared)
        m["xall"] = np.concatenate([xs[b], xp[4 * b:4 * b + 4].reshape(1024, 1024)], 0)
        m["ck"] = A(inp["cache_k"])[b, 0].reshape(256, 512)
        m["cv"] = A(inp["cache_v"])[b, 0].reshape(256, 512)
        m["st0"] = A(inp["state_rwkv"])[b, 0].reshape(16, 64, 64)
        cond = np.stack([A(inp["c"])[b], A(inp["c_ctx"])], 0)
        m["condT"] = np.ascontiguousarray(cond.reshape(2, 8, 128).transpose(2, 1, 0).reshape(128, 16))
        maps.append(m)
    return maps


_NC_CACHE = {}


def kernel(**inputs):
    if "nc" not in _NC_CACHE:
        _NC_CACHE["nc"] = build_nc()
    nc = _NC_CACHE["nc"]
    maps = _in_maps(inputs)
    res = run_bass_kernel_spmd(nc, maps, core_ids=list(range(8)))
    R = res.results
    y_sample = np.stack([R[b]["yall"][:4096] for b in range(8)], 0)
    y_prompt = np.concatenate([R[b]["yall"][4096:].reshape(4, 256, 1024) for b in range(8)], 0)
    new_k = np.concatenate([R[b]["nk"].reshape(4, 1, 256, 4, 2, 64) for b in range(8)], 0)
    new_v = np.concatenate([R[b]["nv"].reshape(4, 1, 256, 4, 128) for b in range(8)], 0)
    new_s = np.concatenate([R[b]["ns"].reshape(4, 1, 2, 8, 64, 64) for b in range(8)], 0)
    return (y_prompt.astype(np.float32), y_sample.astype(np.float32), new_k.astype(np.float32),
            new_v.astype(np.float32), new_s.astype(np.float32))
```

```python
import math
import numpy as np
import ml_dtypes
import concourse.bass as bass
import concourse.mybir as mybir
from concourse.bass_utils import run_bass_kernel_spmd
from contextlib import ExitStack

F32 = mybir.dt.float32
BF16 = mybir.dt.bfloat16
AF = mybir.ActivationFunctionType
ALU = mybir.AluOpType
AX = mybir.AxisListType

ENGS = ("pe", "act", "dve", "pool", "sp")
NT = 40
NTOK = NT * 128
C0 = math.exp(-0.5)
EPS_RMS = 1e-6
EPS_GN = 64e-5
D_FF = 2816
import os
STAGGER = 0
INTERLEAVE_POST = 1
HQ = 4


class Res:
    __slots__ = ("w", "r", "chan", "psum")

    def __init__(self, psum=False):
        self.w = None
        self.r = []
        self.chan = None
        self.psum = psum


class Sched:
    def __init__(self, nc, stack):
        self.nc = nc
        self.stack = stack
        self.cnt = {}
        self.known = {e: {} for e in ENGS}
        self.sems = {}
        self.nchan = 0
        self.free_chans = []
        self.phase_chans = None
        self.engobj = {"pe": nc.tensor, "act": nc.scalar, "dve": nc.vector, "pool": nc.gpsimd, "sp": nc.sync}
        for e in ("pe", "act", "dve", "pool"):
            self._mksem(e)

    def _mksem(self, key):
        s = self.stack.enter_context(self.nc.semaphore("s_" + str(key)))
        self.sems[key] = s
        self.cnt[key] = 0
        return s

    def chan_of(self, res):
        if res.chan is None:
            if self.free_chans:
                res.chan = self.free_chans.pop()
            else:
                self.nchan += 1
                res.chan = "c%d" % self.nchan
                self._mksem(res.chan)
            if self.phase_chans is not None:
                self.phase_chans.append(res.chan)
        return res.chan

    def begin_phase(self):
        self.phase_chans = []

    def end_phase(self):
        self.barrier()
        self.free_chans.extend(self.phase_chans)
        self.phase_chans = None

    def _deps(self, eng, reads, writes):
        deps = {}

        def add(kv):
            if kv is None:
                return
            k, v = kv
            if deps.get(k, 0) < v:
                deps[k] = v
        for r in reads:
            add(r.w)
            if r.psum:
                for x in r.r:
                    if x[0] != eng:
                        add(x)
        for w in writes:
            add(w.w)
            for x in w.r:
                add(x)
        out = []
        for k, v in deps.items():
            if k == eng and eng == "pe":
                continue
            if self.known[eng].get(k, 0) >= v:
                continue
            self.known[eng][k] = v
            out.append((self.sems[k], v))
        return out

    def op(self, eng, fn, reads=(), writes=()):
        waits = self._deps(eng, reads, writes)
        self.cnt[eng] += 1
        v = self.cnt[eng]
        e = self.engobj[eng]
        for s, val in waits:
            e.wait_ge(s, val)
        fn(e).then_inc(self.sems[eng], 1)
        for r in reads:
            r.r.append((eng, v))
            if len(r.r) > 64:
                r.r = r.r[-48:] if False else r.r
        for w in writes:
            w.w = (eng, v)
            w.r = []

    def dma(self, q, out, in_, reads=(), writes=(), chan_res=None, **kw):
        waits = self._deps(q, reads, writes)
        ch = self.chan_of(chan_res)
        self.cnt[ch] += 16
        v = self.cnt[ch]
        e = self.engobj[q]
        for s, val in waits:
            e.wait_ge(s, val)
        e.dma_start(out=out, in_=in_, **kw).then_inc(self.sems[ch], 16)
        for r in reads:
            r.r.append((ch, v))
        for w in writes:
            w.w = (ch, v)
            w.r = []

    def barrier(self, engs=ENGS):
        for eng in engs:
            e = self.engobj[eng]
            for k, s in self.sems.items():
                v = self.cnt[k]
                if v > 0 and self.known[eng].get(k, 0) < v:
                    if k == eng:
                        continue
                    self.known[eng][k] = v
                    e.wait_ge(s, v)


class Ring:
    def __init__(self, alloc, n, psum=False):
        self.items = [(alloc(i), Res(psum)) for i in range(n)]
        self.i = 0

    def next(self):
        it = self.items[self.i % len(self.items)]
        self.i += 1
        return it


def bc(ap, shape, axis):
    return ap.unsqueeze(axis).to_broadcast(list(shape))


class _Stop(Exception):
    pass


def build_nc(upto=99):
    nc = bass.Bass("TRN2", target_bir_lowering=False)
    try:
        _build(nc, upto)
    except _Stop:
        pass
    return nc


def _build(nc, upto):

    def DT(name, shape, dt, kind):
        return nc.dram_tensor(name, list(shape), dt, kind=kind).ap()

    I = "ExternalInput"
    O = "ExternalOutput"
    xall = DT("xall", [NTOK, 1024], F32, I)
    ck = DT("ck", [256, 512], F32, I)
    cv = DT("cv", [256, 512], F32, I)
    st0 = DT("st0", [16, 64, 64], F32, I)
    condT_d = DT("condT", [128, 16], F32, I)
    colp_d = DT("colp", [128, 64], F32, I)
    rowp_d = DT("rowp", [1, 5120], F32, I)
    cst_d = DT("cst", [128, 2184], F32, I)
    rope_d = DT("rope", [128, 32, 2, 64], F32, I)
    ada_w = DT("ada_w", [1024, 6144], F32, I)
    w_in = DT("w_in", [1024, 5888], F32, I)
    wlu_d = DT("wlu", [128, 512], F32, I)
    alu_d = DT("alu", [128, 512], F32, I)
    w_abr = DT("w_abr", [512, 1024], F32, I)
    w_rbr = DT("w_rbr", [512, 1024], F32, I)
    w_out = DT("w_out", [1024, 1024], F32, I)
    w_f1 = DT("w_f1", [1024, 2 * D_FF], F32, I)
    w_f2 = DT("w_f2", [D_FF, 1024], F32, I)

    yall = DT("yall", [NTOK, 1024], F32, O)
    nk = DT("nk", [1024, 512], F32, O)
    nv = DT("nv", [1024, 512], F32, O)
    ns = DT("ns", [4, 16, 64, 64], F32, O)

    X = "Internal"
    QT_d = DT("QT_d", [4, 128, NTOK], BF16, X)
    KT_d = DT("KT_d", [4, 128, NTOK], BF16, X)
    V_d = DT("V_d", [NTOK, 512], BF16, X)
    TM_d = DT("TM_d", [NTOK, 2, 3, 512], BF16, X)
    VR_d = DT("VR_d", [NTOK, 512], BF16, X)
    FM_d = DT("FM_d", [NT, 2, 4, 512, 128], BF16, X)
    SG_d = DT("SG_d", [NTOK, 512], BF16, X)
    BON_d = DT("BON_d", [NTOK, 512], BF16, X)
    GT_d = DT("GT_d", [16, 128, NTOK], BF16, X)
    ORT_d = DT("ORT_d", [4, 128, NTOK], BF16, X)
    OAT_d = DT("OAT_d", [4, 128, NTOK], BF16, X)
    X1_d = DT("X1_d", [NTOK, 1024], F32, X)
    YF_d = DT("YF_d", [NTOK, 512], F32, X)
    w_in_b = DT("w_in_b", [1024, 5888], BF16, X)
    w_abr_b = DT("w_abr_b", [512, 1024], BF16, X)
    w_rbr_b = DT("w_rbr_b", [512, 1024], BF16, X)
    w_out_b = DT("w_out_b", [1024, 1024], BF16, X)
    w_f1_b = DT("w_f1_b", [1024, 2 * D_FF], BF16, X)
    w_f2_b = DT("w_f2_b", [D_FF, 1024], BF16, X)

    with ExitStack() as st:
        S = Sched(nc, st)

        def chk(level):
            if upto <= level:
                S.barrier()
                raise _Stop()

        def SB(stack, name, shape, dt):
            return stack.enter_context(nc.sbuf_tensor("sb_" + name, list(shape), dt))

        def PS(stack, name, shape, dt=F32):
            return stack.enter_context(nc.psum_tensor("ps_" + name, list(shape), dt))

        r_wcv = {}
        for nm, src, dst, rows in (("w_in", w_in, w_in_b, 1024), ("w_abr", w_abr, w_abr_b, 512), ("w_rbr", w_rbr, w_rbr_b, 512),
                                   ("w_out", w_out, w_out_b, 1024), ("w_f1", w_f1, w_f1_b, 1024), ("w_f2", w_f2, w_f2_b, D_FF)):
            r_ = Res()
            r_wcv[nm] = r_
            step = 256
            for r0 in range(0, rows, step):
                r1 = min(rows, r0 + step)
                S.dma("pool", dst[r0:r1, :], src[r0:r1, :], writes=[], chan_res=r_)
            r_.w = (r_.chan, S.cnt[r_.chan])
        cst = SB(st, "cst", [128, 2184], F32)
        r_cst = Res()
        S.dma("sp", cst[:], cst_d[:, :], writes=[r_cst], chan_res=r_cst)
        identf = cst[:, 0:128]
        def CM(i):
            return cst[:, 128 + i * 128: 256 + i * 128]
        ccol = cst[:, 896:900]
        def MSK(z, a, b):
            o = 904 + z * 640
            return cst[:, o + a: o + b]
        identb = SB(st, "identb", [128, 128], BF16)
        r_idb = Res()
        S.op("dve", lambda e: e.tensor_copy(out=identb[:], in_=identf), reads=[r_cst], writes=[r_idb])
        rowp = SB(st, "rowp", [128, 5120], F32)
        r_rowp = Res()
        S.dma("sp", rowp[:], rowp_d.partition_broadcast(128), writes=[r_rowp], chan_res=r_rowp)
        RP = {}
        o = 0
        for nm, n in (("qnw", 64), ("knw", 64), ("subw", 128), ("k_k", 512), ("k_a", 512), ("r_k", 512),
                      ("lnw", 512), ("lnb", 512), ("lam", 256), ("w0", 1024), ("a0", 1024)):
            RP[nm] = (o, o + n)
            o += n
        def RW(nm, a=0, b=None):
            lo, hi = RP[nm]
            return rowp[:, lo + a: (lo + b) if b is not None else hi]
        ones_f = SB(st, "ones_f", [128, 128], F32)
        r_ones = Res()
        S.op("dve", lambda e: e.memset(ones_f[:], 1.0), writes=[r_ones])
        mhalf = SB(st, "mhalf", [128, 16], F32)
        r_mh = Res()
        S.op("dve", lambda e: e.memset(mhalf[:], -0.5), writes=[r_mh])

        def rsqrt_pool(ap, r_ap, w):
            S.op("pool", lambda e: e.tensor_tensor(out=ap, in0=ap, in1=mhalf[:, 0:w], op=ALU.pow), reads=[r_ap, r_mh], writes=[r_ap])
        mod = SB(st, "mod", [128, 48, 2], F32)
        r_mod = Res()
        s1 = SB(st, "s1", [128, 8, 2], F32)
        s2 = SB(st, "s2", [128, 8, 2], F32)
        PCM = SB(st, "PCM", [64, NT, 16, 2], F32)
        r_pcm = [Res() for _ in range(NT)]
        lamc = SB(st, "lamc", [128, 2], F32)
        r_lamc = Res()

        if upto <= -3:
            S.barrier()
            return nc
        S.begin_phase()
        with ExitStack() as ph:
            condT = SB(ph, "condT", [128, 8, 2], F32)
            r_cond = Res()
            S.dma("sp", condT[:], condT_d.rearrange("p (k c) -> p k c", c=2), writes=[r_cond], chan_res=r_cond)
            colp = SB(ph, "colp", [128, 64], F32)
            r_colp = Res()
            S.dma("sp", colp[:], colp_d[:, :], writes=[r_colp], chan_res=r_colp)
            scn = SB(ph, "scn", [128, 8, 2], F32)
            r_scn = Res()
            S.op("act", lambda e: e.activation(out=scn[:], in_=condT[:], func=AF.Silu), reads=[r_cond], writes=[r_scn])
            if upto <= -2:
                S.barrier()
                return nc
            awr = Ring(lambda i: SB(ph, "aw%d" % i, [128, 8, 512], F32), 2)
            pm = PS(ph, "pm", [128, 48, 2])
            r_pm = Res(True)
            for c in range(12):
                aw, r_aw = awr.next()
                S.dma("sp", aw[:], ada_w[:, c * 512:(c + 1) * 512].rearrange("(k p) n -> p k n", p=128),
                      writes=[r_aw], chan_res=r_aw)
                for jj in range(4):
                    j = c * 4 + jj
                    for kc in range(8):
                        S.op("pe", lambda e, aw=aw, jj=jj, j=j, kc=kc: e.matmul(
                            pm[:, j, :], lhsT=aw[:, kc, jj * 128:(jj + 1) * 128], rhs=scn[:, kc, :],
                            start=(kc == 0), stop=(kc == 7)), reads=[r_aw, r_scn], writes=[r_pm])
            if upto <= -1:
                S.barrier()
                return nc
            S.op("dve", lambda e: e.tensor_tensor(out=mod[:], in0=pm[:], in1=bc(colp[:, 16:64], [128, 48, 2], 2),
                                                  op=ALU.add), reads=[r_pm, r_colp], writes=[r_mod])
            S.op("dve", lambda e: e.scalar_tensor_tensor(out=s1[:], in0=mod[:, 8:16, :], scalar=1.0,
                                                         in1=bc(colp[:, 0:8], [128, 8, 2], 2),
                                                         op0=ALU.add, op1=ALU.mult), reads=[r_mod, r_colp], writes=[r_mod])
            S.op("dve", lambda e: e.scalar_tensor_tensor(out=s2[:], in0=mod[:, 32:40, :], scalar=1.0,
                                                         in1=bc(colp[:, 8:16], [128, 8, 2], 2),
                                                         op0=ALU.add, op1=ALU.mult), reads=[r_mod, r_colp], writes=[r_mod])
            if upto <= -0.5:
                S.barrier()
                return nc
            lt = SB(ph, "lt", [128, 128], F32)
            r_lt = Res()
            l2 = SB(ph, "l2", [128, 2], F32)
            S.op("dve", lambda e: e.tensor_tensor(out=lt[:].rearrange("p (a d) -> p a d", a=2),
                                                  in0=RW("lam").rearrange("p (a b d) -> p a b d", a=2, b=2)[:, :, 0, :],
                                                  in1=RW("lam").rearrange("p (a b d) -> p a b d", a=2, b=2)[:, :, 1, :],
                                                  op=ALU.mult), reads=[r_rowp], writes=[r_lt])
            S.op("dve", lambda e: e.tensor_reduce(out=l2[:], in_=lt[:].rearrange("p (a d) -> p a d", a=2),
                                                  axis=AX.X, op=ALU.add), reads=[r_lt], writes=[r_lt])
            if upto <= -0.3:
                S.barrier()
                return nc
            l3 = SB(ph, "l3", [128, 2], F32)
            S.op("act", lambda e: e.activation(out=l3[:], in_=l2[:], func=AF.Exp), reads=[r_lt], writes=[r_lamc])
            if upto <= -0.2:
                S.barrier()
                return nc
            S.op("dve", lambda e: e.tensor_scalar(out=lamc[:, 0:1], in0=l3[:, 1:2], scalar1=l3[:, 0:1], scalar2=-0.2,
                                                  op0=ALU.subtract, op1=ALU.add), reads=[r_lamc], writes=[r_lamc])
            S.end_phase()
        if upto <= 0:
            S.barrier()
            return nc

        S.begin_phase()
        with ExitStack() as ph:
            roper = Ring(lambda i: SB(ph, "ropet%d" % i, [128, 4, 2, 64], F32), 2)
            wlu = SB(ph, "wlu", [128, 512], F32)
            alu = SB(ph, "alu", [128, 512], F32)
            r_lu = Res()
            S.dma("sp", wlu[:], wlu_d[:, :], writes=[r_lu], chan_res=r_lu)
            r_lu2 = Res()
            S.dma("sp", alu[:], alu_d[:, :], writes=[r_lu2], chan_res=r_lu2)
            oma = SB(ph, "oma", [128, 512], F32)
            r_oma = Res()
            S.op("dve", lambda e: e.tensor_scalar(out=oma[:], in0=RW("k_a"), scalar1=-1.0, scalar2=1.0,
                                                  op0=ALU.mult, op1=ALU.add), reads=[r_rowp], writes=[r_oma])
            xr = Ring(lambda i: SB(ph, "x%d" % i, [128, 1024], F32), 2)
            xb = SB(ph, "xb", [128, 1024], BF16)
            r_xb = Res()
            junk = xb
            r_junk = r_xb
            st4 = Ring(lambda i: SB(ph, "st4_%d" % i, [128, 4], F32), 4)
            hTr = Ring(lambda i: SB(ph, "hT%d" % i, [128, 8, 512], BF16), 2)
            wr = Ring(lambda i: SB(ph, "w%d" % i, [128, 8, 512], BF16), 3)
            wlT = SB(ph, "wlT", [128, 512], F32)
            alT = SB(ph, "alT", [128, 512], F32)
            r_wlT = Res()
            r_alT = Res()
            rsb = SB(ph, "rsb", [128, 4, 512], F32)
            ksb = SB(ph, "ksb", [128, 4, 512], F32)
            vsb = SB(ph, "vsb", [128, 4, 512], F32)
            r_rsb = [Res() for _ in range(4)]
            r_ksb = [Res() for _ in range(4)]
            r_vsb = [Res() for _ in range(4)]
            tf = Ring(lambda i: SB(ph, "tf%d" % i, [128, 512], F32), 18)
            kkrk = Ring(lambda i: SB(ph, "kkrk%d" % i, [128, 512], F32), 2)
            tb = Ring(lambda i: SB(ph, "tb%d" % i, [128, 512], BF16), 8)
            tmz = Ring(lambda i: SB(ph, "tmz%d" % i, [128, 3, 512], BF16), 3)
            fmz = Ring(lambda i: SB(ph, "fmz%d" % i, [128, 16, 128], BF16), 3)
            qkst = Ring(lambda i: SB(ph, "qkst%d" % i, [128, 4, 512], BF16), 2)
            gst = Ring(lambda i: SB(ph, "gst%d" % i, [128, 512], BF16), 2)
            vbr = Ring(lambda i: SB(ph, "vbr%d" % i, [128, 512], BF16), 2)
            vfr = Ring(lambda i: SB(ph, "vfr%d" % i, [128, 512], F32), 1)
            s8 = Ring(lambda i: SB(ph, "s8_%d" % i, [128, 8], F32), 8)
            pT = PS(ph, "pT", [128, 8, 128], BF16)
            r_pT = Res(True)
            pr = Ring(lambda i: PS(ph, "pr%d" % i, [128, 512]), 6, psum=True)
            pcc = PS(ph, "pcc", [64, 16, 2])
            r_pcc = Res(True)

            def rstd_from(ssum, r_ss, n, eps):
                S.op("dve", lambda e: e.tensor_scalar(out=ssum, in0=ssum, scalar1=1.0 / n, scalar2=eps,
                                                      op0=ALU.mult, op1=ALU.add), reads=[r_ss], writes=[r_ss])
                rsqrt_pool(ssum, r_ss, ssum.shape[-1])

            def wload(c0, ncol):
                w, r_w = wr.next()
                S.dma("pool", w[:, :, 0:ncol], w_in_b[:, c0:c0 + ncol].rearrange("(k p) n -> p k n", p=128),
                      reads=[r_wcv["w_in"]], writes=[r_w], chan_res=r_w)
                return w, r_w

            def xnorm_gen(blk, out):
                cnd = 0 if blk < 8 else 1
                hT, r_hT = hTr.next()
                out["hT"] = (hT, r_hT)
                for ti in range(4):
                    g = blk * 4 + ti
                    xt, r_xt = xr.next()
                    S.dma("sp", xt[:], xall[g * 128:(g + 1) * 128, :], writes=[r_xt], chan_res=r_xt)
                    ss, r_ss = st4.next()
                    S.op("act", lambda e, xt=xt, ss=ss: e.activation(out=junk[:], in_=xt[:], func=AF.Square,
                                                                     accum_out=ss[:, 0:1]),
                         reads=[r_xt], writes=[r_junk, r_ss])
                    rstd_from(ss[:, 0:1], r_ss, 1024, EPS_RMS)
                    S.op("act", lambda e, xt=xt, ss=ss: e.activation(out=xb[:], in_=xt[:], func=AF.Copy, scale=ss[:, 0:1]),
                         reads=[r_xt, r_ss], writes=[r_xb])
                    yield
                    for kc in range(8):
                        S.op("pe", lambda e, kc=kc: e.transpose(out=pT[:, kc, :], in_=xb[:, kc * 128:(kc + 1) * 128],
                                                                identity=identb[:]),
                             reads=[r_xb, r_idb], writes=[r_pT])
                    for kc in range(8):
                        S.op("act", lambda e, kc=kc, hT=hT, ti=ti: e.activation(
                            out=hT[:, kc, ti * 128:(ti + 1) * 128], in_=pT[:, kc, :], func=AF.Identity,
                            scale=s1[:, kc, cnd:cnd + 1], bias=mod[:, kc, cnd:cnd + 1]), reads=[r_pT, r_mod], writes=[r_hT])
                    yield

            def drain(gen):
                if gen is None:
                    return
                for _ in gen:
                    pass

            xo = {}
            drain(xnorm_gen(0, xo))
            for blk in range(10):
                latent = blk < 8
                cnd = 0 if latent else 1
                hT, r_hT = xo["hT"]
                xo = {}
                xg = xnorm_gen(blk + 1, xo) if blk + 1 < 10 else None
                if latent:
                    ropet, r_rope = roper.next()
                    S.dma("sp", ropet[:], rope_d[:, blk * 4:(blk + 1) * 4, :, :], writes=[r_rope], chan_res=r_rope)

                chk(0.1)
                def mm_tok(w, r_w, ti, ncol=512, wc0=0):
                    p, r_p = pr.next()
                    for kc in range(8):
                        S.op("pe", lambda e, kc=kc, p=p: e.matmul(p[:, 0:ncol], lhsT=hT[:, kc, ti * 128:(ti + 1) * 128],
                                                                  rhs=w[:, kc, wc0:wc0 + ncol], start=(kc == 0), stop=(kc == 7)),
                             reads=[r_hT, r_w], writes=[r_p])
                    return p, r_p

                def mm_feat(w, r_w, wc0):
                    p, r_p = pr.next()
                    for kc in range(8):
                        S.op("pe", lambda e, kc=kc, p=p: e.matmul(p[:, :], lhsT=w[:, kc, wc0:wc0 + 128], rhs=hT[:, kc, :],
                                                                  start=(kc == 0), stop=(kc == 7)),
                             reads=[r_hT, r_w], writes=[r_p])
                    return p, r_p

                w, r_w = wload(3584, 256)
                p, r_p = mm_feat(w, r_w, 0)
                S.op("act", lambda e, p=p: e.activation(out=wlT[:], in_=p[:], func=AF.Tanh), reads=[r_p], writes=[r_wlT])
                p, r_p = mm_feat(w, r_w, 128)
                S.op("act", lambda e, p=p: e.activation(out=alT[:], in_=p[:], func=AF.Copy), reads=[r_p], writes=[r_alT])
                chk(0.2)
                w, r_w = wload(1536, 512)
                for ti in range(4):
                    p, r_p = mm_tok(w, r_w, ti)
                    S.op("act", lambda e, p=p, ti=ti: e.activation(out=rsb[:, ti, :], in_=p[:], func=AF.Copy),
                         reads=[r_p], writes=[r_rsb[ti]])
                chk(0.22)
                w, r_w = wload(2048, 512)
                for ti in range(4):
                    p, r_p = mm_tok(w, r_w, ti)
                    S.op("act", lambda e, p=p, ti=ti: e.activation(out=ksb[:, ti, :], in_=p[:], func=AF.Copy),
                         reads=[r_p], writes=[r_ksb[ti]])
                chk(0.24)
                w, r_w = wload(2560, 512)
                for ti in range(4):
                    g = blk * 4 + ti
                    p, r_p = mm_tok(w, r_w, ti)
                    S.op("act", lambda e, p=p, ti=ti: e.activation(out=vsb[:, ti, :], in_=p[:], func=AF.Copy),
                         reads=[r_p], writes=[r_vsb[ti]])
                    vb, r_vb = tb.next()
                    S.op("dve", lambda e, ti=ti, vb=vb: e.tensor_copy(out=vb[:], in_=vsb[:, ti, :]), reads=[r_vsb[ti]], writes=[r_vb])
                    S.dma("sp", VR_d[g * 128:(g + 1) * 128, :], vb[:], reads=[r_vb], chan_res=r_vb)
                chk(0.26)
                w, r_w = wload(3072, 512)
                for ti in range(4):
                    g = blk * 4 + ti
                    p, r_p = mm_tok(w, r_w, ti)
                    sg, r_sg = tb.next()
                    S.op("act", lambda e, p=p, sg=sg: e.activation(out=sg[:], in_=p[:], func=AF.Sigmoid),
                         reads=[r_p], writes=[r_sg])
                    S.dma("sp", SG_d[g * 128:(g + 1) * 128, :], sg[:], reads=[r_sg], chan_res=r_sg)

                def filler():
                    w, r_w = wload(1024, 512)
                    for ti in range(4):
                        g = blk * 4 + ti
                        p, r_p = mm_tok(w, r_w, ti)
                        vb, r_vb = vbr.next()
                        S.op("act", lambda e, p=p, vb=vb: e.activation(out=vb[:], in_=p[:], func=AF.Copy), reads=[r_p], writes=[r_vb])
                        S.dma("sp", V_d[g * 128:(g + 1) * 128, :], vb[:], reads=[r_vb], chan_res=r_vb)
                        if not latent:
                            vf, r_vf = vfr.next()
                            S.op("act", lambda e, p=p, vf=vf: e.activation(out=vf[:], in_=p[:], func=AF.Copy), reads=[r_p], writes=[r_vf])
                            S.dma("sp", nv[(g - 32) * 128:(g - 31) * 128, :], vf[:], reads=[r_vf], chan_res=r_vf)
                        yield
                    for c in range(4):
                        w, r_w = wload(3840 + c * 512, 512)
                        for jj in range(4):
                            j = c * 4 + jj
                            p, r_p = mm_feat(w, r_w, jj * 128)
                            gs, r_gs = gst.next()
                            S.op("act", lambda e, p=p, gs=gs: e.activation(out=gs[:], in_=p[:], func=AF.Sigmoid), reads=[r_p], writes=[r_gs])
                            S.dma("sp", GT_d[j, :, blk * 512:(blk + 1) * 512], gs[:], reads=[r_gs], chan_res=r_gs)
                            yield
                fill = filler()
                chk(0.3)
                for ti in range(4):
                    g = blk * 4 + ti
                    rt = rsb[:, ti, :]
                    kt = ksb[:, ti, :]
                    vt = vsb[:, ti, :]
                    kkr, r_kkr = tf.next()
                    S.op("dve", lambda e, kkr=kkr: e.tensor_tensor(out=kkr[:], in0=kt, in1=RW("k_k"), op=ALU.mult),
                         reads=[r_ksb[ti], r_rowp], writes=[r_kkr])
                    sq, r_sq = tf.next()
                    S.op("act", lambda e, sq=sq, kkr=kkr: e.activation(out=sq[:], in_=kkr[:], func=AF.Square),
                         reads=[r_kkr], writes=[r_sq])
                    rn, r_rn = s8.next()
                    S.op("dve", lambda e, sq=sq, rn=rn: e.tensor_reduce(out=rn[:], in_=sq[:].rearrange("p (h d) -> p h d", h=8),
                                                                        axis=AX.X, op=ALU.add), reads=[r_sq], writes=[r_rn])
                    S.op("dve", lambda e, rn=rn: e.tensor_scalar(out=rn[:], in0=rn[:], scalar1=1e-24, scalar2=None,
                                                                 op0=ALU.max), reads=[r_rn], writes=[r_rn])
                    rsqrt_pool(rn[:], r_rn, 8)
                    kk, r_kk = kkrk.next()
                    S.op("dve", lambda e, kk=kk, kkr=kkr, rn=rn: e.tensor_tensor(
                        out=kk[:].rearrange("p (h d) -> p h d", h=8), in0=kkr[:].rearrange("p (h d) -> p h d", h=8),
                        in1=bc(rn[:], [128, 8, 64], 2), op=ALU.mult), reads=[r_kkr, r_rn], writes=[r_kk])
                    rk, r_rk = kkrk.next()
                    S.op("dve", lambda e, rk=rk: e.tensor_tensor(out=rk[:], in0=rt, in1=RW("r_k"), op=ALU.mult),
                         reads=[r_rsb[ti], r_rowp], writes=[r_rk])
                    chk(0.4)
                    sz = []
                    def zprep(z):
                        pw, r_pw = pr.next()
                        S.op("pe", lambda e, pw=pw, z=z: e.matmul(pw[:], lhsT=wlT[z * 64:(z + 1) * 64, ti * 128:(ti + 1) * 128],
                                                                  rhs=wlu[z * 64:(z + 1) * 64, :], start=True, stop=False),
                             reads=[r_wlT, r_lu], writes=[r_pw])
                        S.op("pe", lambda e, pw=pw, z=z: e.matmul(pw[:], lhsT=ones_f[0:1, :],
                                                                  rhs=RW("w0", z * 512, (z + 1) * 512)[0:1, :],
                                                                  start=False, stop=True),
                             reads=[r_ones, r_rowp], writes=[r_pw])
                        sgw, r_sgw = tf.next()
                        S.op("act", lambda e, pw=pw, sgw=sgw: e.activation(out=sgw[:], in_=pw[:], func=AF.Sigmoid),
                             reads=[r_pw], writes=[r_sgw])
                        yield
                        pa, r_pa = pr.next()
                        S.op("pe", lambda e, pa=pa, z=z: e.matmul(pa[:], lhsT=alT[z * 64:(z + 1) * 64, ti * 128:(ti + 1) * 128],
                                                                  rhs=alu[z * 64:(z + 1) * 64, :], start=True, stop=False),
                             reads=[r_alT, r_lu2], writes=[r_pa])
                        S.op("pe", lambda e, pa=pa, z=z: e.matmul(pa[:], lhsT=ones_f[0:1, :],
                                                                  rhs=RW("a0", z * 512, (z + 1) * 512)[0:1, :],
                                                                  start=False, stop=True),
                             reads=[r_ones, r_rowp], writes=[r_pa])
                        az, r_az = tf.next()
                        S.op("act", lambda e, pa=pa, az=az: e.activation(out=az[:], in_=pa[:], func=AF.Sigmoid),
                             reads=[r_pa], writes=[r_az])
                        yield
                        p1, r_p1 = pr.next()
                        S.op("pe", lambda e, p1=p1, z=z, sgw=sgw: e.matmul(p1[:], lhsT=CM(3 * z + 0), rhs=sgw[:], start=True, stop=True),
                             reads=[r_cst, r_sgw], writes=[r_p1])
                        p0, r_p0 = pr.next()
                        S.op("pe", lambda e, p0=p0, z=z, sgw=sgw: e.matmul(p0[:], lhsT=CM(3 * z + 1), rhs=sgw[:], start=True, stop=True),
                             reads=[r_cst, r_sgw], writes=[r_p0])
                        p2, r_p2 = pr.next()
                        S.op("pe", lambda e, p2=p2, z=z, sgw=sgw: e.matmul(p2[:], lhsT=CM(3 * z + 2), rhs=sgw[:], start=True, stop=True),
                             reads=[r_cst, r_sgw], writes=[r_p2])
                        for h in range(8):
                            S.op("pe", lambda e, z=z, h=h, sgw=sgw: e.matmul(pcc[:, z * 8 + h, :], lhsT=sgw[:, h * 64:(h + 1) * 64],
                                                                             rhs=ccol[:, 2 * z:2 * z + 2], start=True, stop=True),
                                 reads=[r_cst, r_sgw], writes=[r_pcc])
                        E1, r_E1 = tf.next()
                        S.op("act", lambda e, p1=p1, E1=E1: e.activation(out=E1[:], in_=p1[:], func=AF.Exp), reads=[r_p1], writes=[r_E1])
                        Ei, r_Ei = tf.next()
                        S.op("act", lambda e, p1=p1, Ei=Ei: e.activation(out=Ei[:], in_=p1[:], func=AF.Exp, scale=-1.0),
                             reads=[r_p1], writes=[r_Ei])
                        E0, r_E0 = tf.next()
                        S.op("act", lambda e, p0=p0, E0=E0: e.activation(out=E0[:], in_=p0[:], func=AF.Exp), reads=[r_p0], writes=[r_E0])
                        Et, r_Et = tf.next()
                        S.op("act", lambda e, p2=p2, Et=Et: e.activation(out=Et[:], in_=p2[:], func=AF.Exp), reads=[r_p2], writes=[r_Et])
                        yield
                        chk(0.5)
                        kd, r_kd = tf.next()
                        S.op("dve", lambda e, kd=kd, az=az: e.tensor_tensor(out=kd[:], in0=az[:], in1=RW("k_a"), op=ALU.mult),
                             reads=[r_az, r_rowp], writes=[r_kd])
                        S.op("dve", lambda e, kd=kd: e.tensor_tensor(out=kd[:], in0=kd[:], in1=oma[:], op=ALU.add),
                             reads=[r_kd, r_oma], writes=[r_kd])
                        S.op("dve", lambda e, kd=kd: e.tensor_tensor(out=kd[:], in0=kd[:], in1=kt, op=ALU.mult),
                             reads=[r_kd, r_ksb[ti]], writes=[r_kd])
                        S.op("dve", lambda e, az=az, kk=kk: e.tensor_tensor(out=az[:], in0=az[:], in1=kk[:], op=ALU.mult),
                             reads=[r_az, r_kk], writes=[r_az])
                        yield
                        tm, r_tm = tmz.next()
                        S.op("dve", lambda e, tm=tm, kk=kk, E0=E0: e.tensor_tensor(out=tm[:, 0, :], in0=kk[:], in1=E0[:], op=ALU.mult),
                             reads=[r_kk, r_E0], writes=[r_tm])
                        S.op("dve", lambda e, tm=tm, kd=kd, Et=Et: e.tensor_tensor(out=tm[:, 1, :], in0=kd[:], in1=Et[:], op=ALU.mult),
                             reads=[r_kd, r_Et], writes=[r_tm])
                        S.op("dve", lambda e, tm=tm, az=az, Et=Et: e.tensor_tensor(out=tm[:, 2, :], in0=az[:], in1=Et[:], op=ALU.mult),
                             reads=[r_az, r_Et], writes=[r_tm])
                        S.dma("sp", TM_d[g * 128:(g + 1) * 128, z, :, :], tm[:], reads=[r_tm], chan_res=r_tm)
                        yield
                        hb = []
                        for (a_, b_, ra, rb) in ((rt, E1, r_rsb[ti], r_E1), (az[:], Ei, r_az, r_Ei), (kd[:], Ei, r_kd, r_Ei)):
                            t_, r_t = tb.next()
                            S.op("dve", lambda e, t_=t_, a_=a_, b_=b_: e.tensor_tensor(out=t_[:], in0=a_, in1=b_[:], op=ALU.mult),
                                 reads=[ra, rb], writes=[r_t])
                            hb.append((t_, r_t))
                        yield
                        chk(0.6)
                        fm, r_fm = fmz.next()
                        srcs = [(tm[:, 0, :], r_tm), (hb[0][0][:], hb[0][1]), (hb[1][0][:], hb[1][1]), (hb[2][0][:], hb[2][1])]
                        for half in range(2):
                            for qi in range(2):
                                src, r_src = srcs[half * 2 + qi]
                                for cb in range(4):
                                    S.op("pe", lambda e, src=src, cb=cb, qi=qi: e.transpose(
                                        out=pT[:, qi * 4 + cb, :], in_=src[:, cb * 128:(cb + 1) * 128], identity=identb[:]),
                                        reads=[r_src, r_idb], writes=[r_pT])
                            S.op("act", lambda e, fm=fm, half=half: e.activation(out=fm[:, half * 8:(half + 1) * 8, :], in_=pT[:],
                                                                                 func=AF.Copy), reads=[r_pT], writes=[r_fm])
                        S.dma("sp", FM_d[g, z].rearrange("t (cb p) k -> p (t cb) k", p=128), fm[:], reads=[r_fm], chan_res=r_fm)
                        yield
                        S.op("dve", lambda e, kd=kd, rk=rk: e.tensor_tensor(out=kd[:], in0=kd[:], in1=rk[:], op=ALU.mult),
                             reads=[r_kd, r_rk], writes=[r_kd])
                        s_, r_s = s8.next()
                        S.op("dve", lambda e, kd=kd, s_=s_: e.tensor_reduce(out=s_[:], in_=kd[:].rearrange("p (h d) -> p h d", h=8),
                                                                            axis=AX.X, op=ALU.add), reads=[r_kd], writes=[r_s])
                        sz.append((s_, r_s))
                        yield
                    gens_ = [zprep(0), zprep(1)]
                    rnd = 0
                    while gens_:
                        for gn_ in list(gens_):
                            try:
                                next(gn_)
                            except StopIteration:
                                gens_.remove(gn_)
                        rnd += 1
                        if fill is not None and rnd % 2 == 0:
                            try:
                                next(fill)
                            except StopIteration:
                                fill = None
                    S.op("act", lambda e, g=g: e.activation(out=PCM[:, g, :, :], in_=pcc[:], func=AF.Exp),
                         reads=[r_pcc], writes=[r_pcm[g]])
                    S.op("dve", lambda e: e.tensor_tensor(out=sz[0][0][:], in0=sz[0][0][:], in1=sz[1][0][:], op=ALU.add),
                         reads=[sz[0][1], sz[1][1]], writes=[sz[0][1]])
                    bon, r_bon = tb.next()
                    S.op("dve", lambda e, bon=bon: e.tensor_tensor(out=bon[:].rearrange("p (h d) -> p h d", h=8),
                                                                   in0=vt.rearrange("p (h d) -> p h d", h=8),
                                                                   in1=bc(sz[0][0][:], [128, 8, 64], 2), op=ALU.mult),
                         reads=[r_vsb[ti], sz[0][1]], writes=[r_bon])
                    S.dma("sp", BON_d[g * 128:(g + 1) * 128, :], bon[:], reads=[r_bon], chan_res=r_bon)

                while fill is not None:
                    try:
                        next(fill)
                    except StopIteration:
                        fill = None
                chk(0.7)
                def qkgen(which, tis, w, r_w, qs, r_qs):
                    nwn = "qnw" if which == 0 else "knw"
                    for ti in tis:
                        g = blk * 4 + ti
                        p, r_p = mm_tok(w, r_w, ti)
                        sq, r_sq = tf.next()
                        S.op("act", lambda e, p=p, sq=sq: e.activation(out=sq[:], in_=p[:], func=AF.Square), reads=[r_p], writes=[r_sq])
                        rs, r_rs = s8.next()
                        S.op("dve", lambda e, sq=sq, rs=rs: e.tensor_reduce(out=rs[:], in_=sq[:].rearrange("p (h d) -> p h d", h=8),
                                                                            axis=AX.X, op=ALU.add), reads=[r_sq], writes=[r_rs])
                        rstd_from(rs[:], r_rs, 64, EPS_RMS)
                        yield
                        xw, r_xw = tf.next()
                        S.op("dve", lambda e, p=p, xw=xw: e.tensor_tensor(out=xw[:].rearrange("p (h d) -> p h d", h=8),
                                                                          in0=p[:].rearrange("p (h d) -> p h d", h=8),
                                                                          in1=bc(RW(nwn), [128, 8, 64], 1), op=ALU.mult),
                             reads=[r_p, r_rowp], writes=[r_xw])
                        yield
                        ob, r_ob = tb.next()
                        if latent:
                            t1, r_t1 = tf.next()
                            S.op("dve", lambda e, t1=t1, xw=xw, ti=ti: e.tensor_tensor(
                                out=t1[:].rearrange("p (h d) -> p h d", h=8), in0=xw[:].rearrange("p (h d) -> p h d", h=8),
                                in1=bc(ropet[:, ti, 0, :], [128, 8, 64], 1), op=ALU.mult), reads=[r_xw, r_rope], writes=[r_t1])
                            yield
                            t2, r_t2 = tf.next()
                            for bl in range(2):
                                S.op("dve", lambda e, t2=t2, xw=xw, ti=ti, bl=bl: e.tensor_tensor(
                                    out=t2[:].rearrange("p (h a b i) -> p h a b i", h=8, a=2, b=2)[:, :, :, bl, :],
                                    in0=xw[:].rearrange("p (h a b i) -> p h a b i", h=8, a=2, b=2)[:, :, :, 1 - bl, :],
                                    in1=bc(ropet[:, ti, 1, :].rearrange("p (a b i) -> p a b i", a=2, b=2)[:, :, bl, :], [128, 8, 2, 16], 1),
                                    op=ALU.mult), reads=[r_xw, r_rope], writes=[r_t2])
                            yield
                            S.op("dve", lambda e, t1=t1, t2=t2: e.tensor_tensor(out=t1[:], in0=t1[:], in1=t2[:], op=ALU.add),
                                 reads=[r_t1, r_t2], writes=[r_t1])
                            yield
                            S.op("dve", lambda e, t1=t1, ob=ob, rs=rs: e.tensor_tensor(
                                out=ob[:].rearrange("p (h d) -> p h d", h=8), in0=t1[:].rearrange("p (h d) -> p h d", h=8),
                                in1=bc(rs[:], [128, 8, 64], 2), op=ALU.mult), reads=[r_t1, r_rs], writes=[r_ob])
                        else:
                            S.op("dve", lambda e, xw=xw, rs=rs: e.tensor_tensor(
                                out=xw[:].rearrange("p (h d) -> p h d", h=8), in0=xw[:].rearrange("p (h d) -> p h d", h=8),
                                in1=bc(rs[:], [128, 8, 64], 2), op=ALU.mult), reads=[r_xw, r_rs], writes=[r_xw])
                            if which == 1:
                                S.dma("sp", nk[(g - 32) * 128:(g - 31) * 128, :], xw[:], reads=[r_xw], chan_res=r_xw)
                            S.op("act", lambda e, xw=xw, ob=ob: e.activation(out=ob[:], in_=xw[:], func=AF.Copy),
                                 reads=[r_xw], writes=[r_ob])
                        yield
                        for h in range(4):
                            S.op("pe", lambda e, ob=ob, h=h: e.transpose(out=pT[:, h, :], in_=ob[:, h * 128:(h + 1) * 128],
                                                                         identity=identb[:]), reads=[r_ob, r_idb], writes=[r_pT])
                        S.op("act", lambda e, qs=qs, ti=ti: e.activation(out=qs[:, :, ti * 128:(ti + 1) * 128], in_=pT[:, 0:4, :],
                                                                         func=AF.Copy), reads=[r_pT], writes=[r_qs])
                    yield
                qkw = []
                gens_ = []
                for which in range(2):
                    w, r_w = wload(which * 512, 512)
                    qs, r_qs = qkst.next()
                    qkw.append((qs, r_qs))
                    gens_.append(qkgen(which, [0, 2], w, r_w, qs, r_qs))
                    gens_.append(qkgen(which, [1, 3], w, r_w, qs, r_qs))
                while gens_:
                    for gn_ in list(gens_):
                        try:
                            next(gn_)
                        except StopIteration:
                            gens_.remove(gn_)
                    if xg is not None:
                        try:
                            next(xg)
                        except StopIteration:
                            xg = None
                drain(xg)
                for which in range(2):
                    qs, r_qs = qkw[which]
                    dst = QT_d if which == 0 else KT_d
                    S.dma("sp", dst[:, :, blk * 512:(blk + 1) * 512].rearrange("h p t -> p h t"), qs[:], reads=[r_qs], chan_res=r_qs)
            S.end_phase()
        chk(1)

        S.begin_phase()
        with ExitStack() as ph:
            def mkring(name, shape, dt, n):
                return Ring(lambda i: SB(ph, "%s_%d" % (name, i), shape, dt), n)
            fmr = [mkring("fm%d" % z, [64, 4, 8, 128], BF16, 2) for z in range(2)]
            tmr = [mkring("tm%d" % z, [128, 3, 512], BF16, 2) for z in range(2)]
            vrr = [mkring("vr%d" % z, [128, 512], BF16, 2) for z in range(2)]
            gr = [[mkring("gr%d%d" % (z, q), [128, HQ, 128], BF16, 3) for q in range(8 // HQ)] for z in range(2)]
            sqr = [[mkring("sq%d%d" % (z, q), [128, HQ, 128], BF16, 4) for q in range(8 // HQ)] for z in range(2)]
            wrg = [[mkring("wg%d%d" % (z, q), [128, HQ, 128], BF16, 2) for q in range(8 // HQ)] for z in range(2)]
            avr = [[mkring("av%d%d" % (z, q), [128, HQ, 64], BF16, 1) for q in range(8 // HQ)] for z in range(2)]
            apr = [[mkring("ap%d%d" % (z, q), [128, HQ, 64], BF16, 1) for q in range(8 // HQ)] for z in range(2)]
            vnr = [[mkring("vn%d%d" % (z, q), [128, HQ, 64], BF16, 1) for q in range(8 // HQ)] for z in range(2)]
            mtr = [mkring("mt%d" % z, [64, 8, 64], F32, 2) for z in range(2)]
            gsr = [mkring("gs%d" % z, [64, 8, 64], F32, 2) for z in range(2)]
            rpr = [mkring("rp%d" % z, [64, 8, 128], BF16, 2) for z in range(2)]
            y0r = [mkring("y0%d" % z, [128, 512], F32, 2) for z in range(2)]
            hmr = [mkring("hm%d" % z, [64, 8, 64], F32, 1) for z in range(2)]
            hbr = [mkring("hb%d" % z, [64, 8, 64], BF16, 1) for z in range(2)]
            h1r = [mkring("h1%d" % z, [64, 8, 64], F32, 1) for z in range(2)]
            Hr = [mkring("H%d" % z, [64, 8, 64], F32, 2) for z in range(2)]
            stl = SB(ph, "stl", [64, 16, 64], F32)
            r_stl = Res()
            f5 = mkring("f5", [128, 512], F32, 6)
            b5 = mkring("b5", [128, 512], BF16, 2)
            bsr = mkring("bsr", [128, 512], BF16, 6)
            pend = {}
            yfr = mkring("yf", [128, 512], F32, 2)
            r_yfd = [Res() for _ in range(NT)]
            ydone = {}
            s8d = mkring("s8d", [128, 8], F32, 8)
            ostr = mkring("ost", [128, 4, 128], BF16, 2)
            pTd = PS(ph, "pTd", [128, 4, 128], BF16)
            r_pTd = Res(True)
            prd = Ring(lambda i: PS(ph, "prd%d" % i, [128, 512]), 7, psum=True)

            def v3(t, n=4):
                return t[:].rearrange("p (a b) -> p a b", a=n)

            def loads(g, z, second):
                fm, r_fm = fmr[z].next()
                S.dma("sp", fm[:], FM_d[g, z].rearrange("t (h p) k -> p t h k", p=64), writes=[r_fm], chan_res=r_fm)
                tm, r_tm = tmr[z].next()
                S.dma("sp", tm[:], TM_d[g * 128:(g + 1) * 128, z, :, :], writes=[r_tm], chan_res=r_tm)
                vr, r_vr = vrr[z].next()
                S.dma("sp", vr[:], VR_d[g * 128:(g + 1) * 128, :], writes=[r_vr], chan_res=r_vr)
                L = dict(fm=fm, r_fm=r_fm, tm=tm, r_tm=r_tm, vr=vr, r_vr=r_vr, second=second)
                return L

            def pre(g, z, L):
                fm, r_fm, tm, r_tm, vr, r_vr = L["fm"], L["r_fm"], L["tm"], L["r_tm"], L["vr"], L["r_vr"]
                MT, r_MT = mtr[z].next()
                Gs, r_Gs = gsr[z].next()
                Rp, r_Rp = rpr[z].next()
                Y0, r_Y0 = y0r[z].next()
                QS = tuple(range(8 // HQ))
                v3 = lambda t: t[:, 0:HQ * 128].rearrange("p (a b) -> p a b", a=HQ)
                hsl = [list(range(q * HQ, q * HQ + HQ)) for q in QS]
                st_ = [dict() for _ in QS]

                def gram(q, lt, rt_, off, ring=None):
                    p, r_p = prd.next()
                    for i, h in enumerate(hsl[q]):
                        S.op("pe", lambda e, p=p, i=i, h=h: e.matmul(p[:, i * 128:(i + 1) * 128], lhsT=fm[:, lt, h, :],
                                                                     rhs=fm[:, rt_, h, :], start=True, stop=True),
                             reads=[r_fm], writes=[r_p])
                    o_, r_o = (ring or gr[z][q]).next()
                    S.op("dve", lambda e, p=p, o_=o_: e.tensor_tensor(out=o_[:], in0=v3(p), in1=bc(MSK(z, off, off + 128), [128, HQ, 128], 1),
                                                                     op=ALU.mult), reads=[r_p, r_cst], writes=[r_o])
                    return o_, r_o
                for q in QS:
                    st_[q]["P"] = gram(q, 0, 2, 0, sqr[z][q])
                    st_[q]["Q"] = gram(q, 2, 0, 128, sqr[z][q])
                yield
                for q in QS:
                    st_[q]["ArbT"] = gram(q, 2, 1, 256)
                    st_[q]["AakT"] = gram(q, 3, 0, 384)
                    st_[q]["ArkT"] = gram(q, 3, 1, 512)
                    W, r_W = wrg[z][q].next()
                    Q, r_Q = st_[q]["Q"]
                    S.op("pool", lambda e, W=W, Q=Q: e.tensor_tensor(out=W[:], in0=Q[:], in1=bc(identb[:], [128, HQ, 128], 1), op=ALU.add),
                         reads=[r_Q, r_idb], writes=[r_W])
                    st_[q]["W"] = (W, r_W)
                yield
                for j in range(1, 7):
                    for q in QS:
                        P, r_P = st_[q]["P"]
                        Q, r_Q = st_[q]["Q"]
                        pP, r_pP = prd.next()
                        for i in range(HQ):
                            S.op("pe", lambda e, pP=pP, i=i, P=P, Q=Q: e.matmul(pP[:, i * 128:(i + 1) * 128], lhsT=Q[:, i, :], rhs=P[:, i, :],
                                                                               start=True, stop=True), reads=[r_P, r_Q], writes=[r_pP])
                        Pn, r_Pn = sqr[z][q].next()
                        S.op("act", lambda e, pP=pP, Pn=Pn: e.activation(out=Pn[:], in_=v3(pP), func=AF.Copy), reads=[r_pP], writes=[r_Pn])
                        if j < 6:
                            pQ, r_pQ = prd.next()
                            for i in range(HQ):
                                S.op("pe", lambda e, pQ=pQ, i=i, P=P, Q=Q: e.matmul(pQ[:, i * 128:(i + 1) * 128], lhsT=P[:, i, :], rhs=Q[:, i, :],
                                                                                   start=True, stop=True), reads=[r_P, r_Q], writes=[r_pQ])
                            Qn, r_Qn = sqr[z][q].next()
                            if q % 2 == 0:
                                S.op("act", lambda e, pQ=pQ, Qn=Qn: e.activation(out=Qn[:], in_=v3(pQ), func=AF.Copy), reads=[r_pQ], writes=[r_Qn])
                            else:
                                S.op("dve", lambda e, pQ=pQ, Qn=Qn: e.tensor_copy(out=Qn[:], in_=v3(pQ)), reads=[r_pQ], writes=[r_Qn])
                            st_[q]["Q"] = (Qn, r_Qn)
                        st_[q]["P"] = (Pn, r_Pn)
                    for q in QS:
                        P, r_P = st_[q]["P"]
                        W, r_W = st_[q]["W"]
                        pW, r_pW = prd.next()
                        for i in range(HQ):
                            S.op("pe", lambda e, pW=pW, i=i, P=P, W=W: e.matmul(pW[:, i * 128:(i + 1) * 128], lhsT=P[:, i, :], rhs=W[:, i, :],
                                                                               start=True, stop=True), reads=[r_P, r_W], writes=[r_pW])
                        Wn, r_Wn = wrg[z][q].next()
                        S.op("dve", lambda e, pW=pW, W=W, Wn=Wn: e.tensor_tensor(out=Wn[:], in0=v3(pW), in1=W[:], op=ALU.add),
                             reads=[r_pW, r_W], writes=[r_Wn])
                        st_[q]["W"] = (Wn, r_Wn)
                    yield
                for q in QS:
                    AakT, r_AakT = st_[q]["AakT"]
                    pA, r_pA = prd.next()
                    for i, h in enumerate(hsl[q]):
                        S.op("pe", lambda e, i=i, h=h, pA=pA, AakT=AakT: e.matmul(pA[:, i * 64:(i + 1) * 64], lhsT=AakT[:, i, :], rhs=vr[:, h * 64:(h + 1) * 64],
                                                                                 start=True, stop=True), reads=[r_AakT, r_vr], writes=[r_pA])
                    av, r_av = avr[z][q].next()
                    S.op("act", lambda e, pA=pA, av=av: e.activation(out=av[:], in_=pA[:, 0:HQ * 64].rearrange("p (a b) -> p a b", a=HQ), func=AF.Copy),
                         reads=[r_pA], writes=[r_av])
                    st_[q]["av"] = (av, r_av)
                yield
                for q in QS:
                    W, r_W = st_[q]["W"]
                    av, r_av = st_[q]["av"]
                    pZ, r_pZ = prd.next()
                    for i, h in enumerate(hsl[q]):
                        S.op("pe", lambda e, i=i, h=h, pZ=pZ, W=W: e.matmul(pZ[:, i * 128:i * 128 + 64], lhsT=W[:, i, :],
                                                                          rhs=tm[:, 0, h * 64:(h + 1) * 64], start=True, stop=True),
                             reads=[r_W, r_tm], writes=[r_pZ])
                        S.op("pe", lambda e, i=i, h=h, pZ=pZ, W=W, av=av: e.matmul(pZ[:, i * 128 + 64:(i + 1) * 128], lhsT=W[:, i, :],
                                                                                 rhs=av[:, i, :], start=True, stop=True),
                             reads=[r_W, r_av], writes=[r_pZ])
                    Ap, r_Ap = apr[z][q].next()
                    Vn, r_Vn = vnr[z][q].next()
                    S.op("act", lambda e, pZ=pZ, Ap=Ap: e.activation(out=Ap[:], in_=v3(pZ)[:, :, 0:64], func=AF.Copy), reads=[r_pZ], writes=[r_Ap])
                    S.op("act", lambda e, pZ=pZ, Vn=Vn: e.activation(out=Vn[:], in_=v3(pZ)[:, :, 64:128], func=AF.Copy, scale=-1.0),
                         reads=[r_pZ], writes=[r_Vn])
                    st_[q]["Ap"] = (Ap, r_Ap)
                    st_[q]["Vn"] = (Vn, r_Vn)
                yield
                for q in QS:
                    Ap, r_Ap = st_[q]["Ap"]
                    Vn, r_Vn = st_[q]["Vn"]
                    ArbT, r_ArbT = st_[q]["ArbT"]
                    ArkT, r_ArkT = st_[q]["ArkT"]
                    pM, r_pM = prd.next()
                    for i, h in enumerate(hsl[q]):
                        S.op("pe", lambda e, i=i, h=h, pM=pM, Ap=Ap: e.matmul(pM[0:64, i * 64:(i + 1) * 64], lhsT=Ap[:, i, :], rhs=tm[:, 2, h * 64:(h + 1) * 64],
                                                                             start=True, stop=True), reads=[r_Ap, r_tm], writes=[r_pM])
                    S.op("act", lambda e, pM=pM, q=q: e.activation(out=MT[:, q * HQ:(q + 1) * HQ, :], in_=pM[0:64, 0:HQ * 64].rearrange("p (a b) -> p a b", a=HQ),
                                                                   func=AF.Copy), reads=[r_pM], writes=[r_MT])
                    pR, r_pR = prd.next()
                    for i, h in enumerate(hsl[q]):
                        S.op("pe", lambda e, i=i, h=h, pR=pR, Ap=Ap, ArbT=ArbT: e.matmul(pR[0:64, i * 128:(i + 1) * 128], lhsT=Ap[:, i, :], rhs=ArbT[:, i, :],
                                                                                       start=True, stop=True), reads=[r_Ap, r_ArbT], writes=[r_pR])
                    S.op("dve", lambda e, pR=pR, q=q: e.tensor_tensor(out=Rp[:, q * HQ:(q + 1) * HQ, :], in0=fm[:, 1, q * HQ:(q + 1) * HQ, :],
                                                                      in1=pR[0:64, 0:HQ * 128].rearrange("p (a b) -> p a b", a=HQ), op=ALU.subtract),
                         reads=[r_pR, r_fm], writes=[r_Rp])
                    pG, r_pG = prd.next()
                    for i, h in enumerate(hsl[q]):
                        S.op("pe", lambda e, i=i, h=h, pG=pG: e.matmul(pG[0:64, i * 64:(i + 1) * 64], lhsT=tm[:, 1, h * 64:(h + 1) * 64],
                                                                     rhs=vr[:, h * 64:(h + 1) * 64], start=True, stop=False),
                             reads=[r_tm, r_vr], writes=[r_pG])
                        S.op("pe", lambda e, i=i, h=h, pG=pG, Vn=Vn: e.matmul(pG[0:64, i * 64:(i + 1) * 64], lhsT=tm[:, 2, h * 64:(h + 1) * 64],
                                                                             rhs=Vn[:, i, :], start=False, stop=True),
                             reads=[r_tm, r_Vn], writes=[r_pG])
                    S.op("act", lambda e, pG=pG, q=q: e.activation(out=Gs[:, q * HQ:(q + 1) * HQ, :], in_=pG[0:64, 0:HQ * 64].rearrange("p (a b) -> p a b", a=HQ),
                                                                   func=AF.Copy), reads=[r_pG], writes=[r_Gs])
                    pY0, r_pY0 = prd.next()
                    for i, h in enumerate(hsl[q]):
                        S.op("pe", lambda e, i=i, h=h, pY0=pY0, ArkT=ArkT: e.matmul(pY0[:, i * 64:(i + 1) * 64], lhsT=ArkT[:, i, :], rhs=vr[:, h * 64:(h + 1) * 64],
                                                                                   start=True, stop=False), reads=[r_ArkT, r_vr], writes=[r_pY0])
                        S.op("pe", lambda e, i=i, h=h, pY0=pY0, ArbT=ArbT, Vn=Vn: e.matmul(pY0[:, i * 64:(i + 1) * 64], lhsT=ArbT[:, i, :], rhs=Vn[:, i, :],
                                                                                          start=False, stop=True), reads=[r_ArbT, r_Vn], writes=[r_pY0])
                    S.op("act", lambda e, pY0=pY0, q=q: e.activation(out=Y0[:, q * HQ * 64:(q + 1) * HQ * 64], in_=pY0[:, 0:HQ * 64], func=AF.Copy),
                         reads=[r_pY0], writes=[r_Y0])
                yield
                return dict(MT=MT, r_MT=r_MT, Gs=Gs, r_Gs=r_Gs, Rp=Rp, r_Rp=r_Rp, Y0=Y0, r_Y0=r_Y0)

            def post(g, ysum, r_ysum):
                bon, r_bon = bsr.next()
                S.dma("sp", bon[:], BON_d[g * 128:(g + 1) * 128, :], writes=[r_bon], chan_res=r_bon)
                sg, r_sg = bsr.next()
                S.dma("sp", sg[:], SG_d[g * 128:(g + 1) * 128, :], writes=[r_sg], chan_res=r_sg)
                yf, r_yf = yfr.next()
                S.dma("sp", yf[:], YF_d[g * 128:(g + 1) * 128, :], reads=[r_yfd[g]], writes=[r_yf], chan_res=r_yf)
                yield
                yt, r_yt = ysum, r_ysum
                S.op("pool", lambda e: e.tensor_tensor(out=yt[:], in0=yt[:], in1=yf[:], op=ALU.add),
                     reads=[r_yt, r_yf], writes=[r_yt])
                yield
                sm, r_sm = s8d.next()
                S.op("dve", lambda e: e.tensor_reduce(out=sm[:], in_=yt[:].rearrange("p (h d) -> p h d", h=8), axis=AX.X, op=ALU.add),
                     reads=[r_yt], writes=[r_sm])
                yield
                S.op("dve", lambda e: e.tensor_scalar(out=sm[:], in0=sm[:], scalar1=-1.0 / 64, scalar2=None, op0=ALU.mult),
                     reads=[r_sm], writes=[r_sm])
                yield
                S.op("pool", lambda e: e.tensor_tensor(out=yt[:].rearrange("p (h d) -> p h d", h=8), in0=yt[:].rearrange("p (h d) -> p h d", h=8),
                                                       in1=bc(sm[:], [128, 8, 64], 2), op=ALU.add), reads=[r_yt, r_sm], writes=[r_yt])
                yield
                sq, r_sq = f5.next()
                S.op("act", lambda e: e.activation(out=sq[:], in_=yt[:], func=AF.Square), reads=[r_yt], writes=[r_sq])
                yield
                vv, r_vv = s8d.next()
                S.op("dve", lambda e: e.tensor_reduce(out=vv[:], in_=sq[:].rearrange("p (h d) -> p h d", h=8), axis=AX.X, op=ALU.add),
                     reads=[r_sq], writes=[r_vv])
                yield
                S.op("dve", lambda e: e.tensor_scalar(out=vv[:], in0=vv[:], scalar1=1.0 / 64, scalar2=EPS_GN, op0=ALU.mult, op1=ALU.add),
                     reads=[r_vv], writes=[r_vv])
                yield
                rsqrt_pool(vv[:], r_vv, 8)
                yield
                S.op("pool", lambda e: e.tensor_tensor(out=yt[:].rearrange("p (h d) -> p h d", h=8), in0=yt[:].rearrange("p (h d) -> p h d", h=8),
                                                       in1=bc(vv[:], [128, 8, 64], 2), op=ALU.mult), reads=[r_yt, r_vv], writes=[r_yt])
                yield
                S.op("pool", lambda e: e.tensor_tensor(out=yt[:], in0=yt[:], in1=RW("lnw"), op=ALU.mult), reads=[r_yt, r_rowp], writes=[r_yt])
                yield
                S.op("pool", lambda e: e.tensor_tensor(out=yt[:], in0=yt[:], in1=RW("lnb"), op=ALU.add), reads=[r_yt, r_rowp], writes=[r_yt])
                yield
                S.op("pool", lambda e: e.tensor_tensor(out=yt[:], in0=yt[:], in1=bon[:], op=ALU.add), reads=[r_yt, r_bon], writes=[r_yt])
                yield
                ob, r_ob = b5.next()
                S.op("pool", lambda e: e.tensor_tensor(out=ob[:], in0=yt[:], in1=sg[:], op=ALU.mult), reads=[r_yt, r_sg], writes=[r_ob])
                yield
                for cb in range(4):
                    S.op("pe", lambda e, cb=cb: e.transpose(out=pTd[:, cb, :], in_=ob[:, cb * 128:(cb + 1) * 128], identity=identb[:]),
                         reads=[r_ob, r_idb], writes=[r_pTd])
                ost, r_ost = ostr.next()
                S.op("act", lambda e: e.activation(out=ost[:], in_=pTd[:], func=AF.Copy), reads=[r_pTd], writes=[r_ost])
                yield
                S.dma("sp", ORT_d[:, :, g * 128:(g + 1) * 128].rearrange("h p t -> p h t"), ost[:], reads=[r_ost], chan_res=r_ost)
                yield

            def seq(g, z, B, H, r_H):
                Hm, r_Hm = hmr[z].next()
                S.op("dve", lambda e: e.tensor_tensor(out=Hm[:], in0=H[:], in1=bc(PCM[:, g, z * 8:(z + 1) * 8, 1], [64, 8, 64], 2), op=ALU.mult),
                     reads=[r_H, r_pcm[g]], writes=[r_Hm])
                Hb, r_Hb = hbr[z].next()
                S.op("act", lambda e: e.activation(out=Hb[:], in_=Hm[:], func=AF.Copy), reads=[r_Hm], writes=[r_Hb])
                MT, Gs, Rp, Y0 = B["MT"], B["Gs"], B["Rp"], B["Y0"]
                pH, r_pH = prd.next()
                for h in range(8):
                    S.op("pe", lambda e, h=h: e.matmul(pH[0:64, h * 64:(h + 1) * 64], lhsT=MT[:, h, :], rhs=Hm[:, h, :], start=True, stop=True),
                         reads=[B["r_MT"], r_Hm], writes=[r_pH])
                h1, r_h1 = h1r[z].next()
                S.op("dve", lambda e: e.tensor_tensor(out=h1[:], in0=H[:], in1=bc(PCM[:, g, z * 8:(z + 1) * 8, 0], [64, 8, 64], 2), op=ALU.mult),
                     reads=[r_H, r_pcm[g]], writes=[r_h1])
                S.op("dve", lambda e: e.tensor_tensor(out=h1[:], in0=h1[:], in1=pH[0:64, :].rearrange("p (a b) -> p a b", a=8), op=ALU.subtract),
                     reads=[r_h1, r_pH], writes=[r_h1])
                S.op("dve", lambda e: e.tensor_tensor(out=H[:], in0=h1[:], in1=Gs[:], op=ALU.add), reads=[r_h1, B["r_Gs"]], writes=[r_H])
                pY, r_pY = prd.next()
                for h in range(8):
                    S.op("pe", lambda e, h=h: e.matmul(pY[:, h * 64:(h + 1) * 64], lhsT=Rp[:, h, :], rhs=Hb[:, h, :], start=True, stop=True),
                         reads=[B["r_Rp"], r_Hb], writes=[r_pY])
                ys, r_ys = f5.next()
                S.op("dve", lambda e: e.tensor_tensor(out=ys[:], in0=pY[:], in1=Y0[:], op=ALU.add), reads=[r_pY, B["r_Y0"]], writes=[r_ys])
                if g not in ydone:
                    ydone[g] = True
                    S.dma("sp", YF_d[g * 128:(g + 1) * 128, :], ys[:], reads=[r_ys], writes=[r_yfd[g]], chan_res=r_ys)
                else:
                    return post(g, ys, r_ys)
                return None

            segs = [(0, 32, True)] + [(32 + 2 * i, 2, False) for i in range(4)]

            def init_H(z, lat):
                H, r_H = Hr[z].next()
                if lat:
                    if z == 0:
                        S.dma("sp", stl[:], st0.rearrange("a v k -> v a k"), writes=[r_stl], chan_res=r_stl)
                    for hp in range(4):
                        pS, r_pS = prd.next()
                        for i in range(2):
                            h = hp * 2 + i
                            S.op("pe", lambda e, pS=pS, i=i, h=h: e.transpose(out=pS[0:64, i * 64:(i + 1) * 64], in_=stl[:, z * 8 + h, :],
                                                                             identity=identf[0:64, 0:64]), reads=[r_stl, r_cst], writes=[r_pS])
                        S.op("act", lambda e, pS=pS, hp=hp: e.activation(out=H[:, hp * 2:hp * 2 + 2, :],
                                                                         in_=pS[0:64, 0:128].rearrange("p (a b) -> p a b", a=2), func=AF.Copy),
                             reads=[r_pS], writes=[r_H])
                else:
                    S.op("dve", lambda e: e.memset(H[:], 0.0), writes=[r_H])
                return H, r_H

            def final_out(z, si, H, r_H):
                so, r_so = stl[:, 0:8, :], r_stl
                for hp in range(4):
                    pS, r_pS = prd.next()
                    for i in range(2):
                        h = hp * 2 + i
                        S.op("pe", lambda e, pS=pS, i=i, h=h: e.transpose(out=pS[0:64, i * 64:(i + 1) * 64], in_=H[:, h, :],
                                                                         identity=identf[0:64, 0:64]), reads=[r_H, r_cst], writes=[r_pS])
                    S.op("act", lambda e, pS=pS, hp=hp: e.activation(out=so[:, hp * 2:hp * 2 + 2, :],
                                                                     in_=pS[0:64, 0:128].rearrange("p (a b) -> p a b", a=2), func=AF.Copy),
                         reads=[r_pS], writes=[r_so])
                S.dma("sp", ns[si - 1, z * 8:(z + 1) * 8, :, :].rearrange("h v k -> v h k"), so, reads=[r_so], chan_res=r_so)

            def chain(z):
                items = []
                for si, (g0, n, lat) in enumerate(segs):
                    order = list(range(g0, g0 + n)) if z == 0 else list(range(g0 + n - 1, g0 - 1, -1))
                    for i_, g in enumerate(order):
                        items.append((si, g, i_, n, lat))

                pp = [None]

                def drain_post():
                    if pp[0] is not None:
                        for _ in pp[0]:
                            pass
                        pp[0] = None

                def do_seq(p):
                    g, B, (H, r_H), si, last, lat = p
                    drain_post()
                    pp[0] = seq(g, z, B, H, r_H)
                    if last and not lat:
                        final_out(z, si, H, r_H)
                prev = None
                cur = loads(items[0][1], z, items[0][2] * 2 >= items[0][3])
                Hc = None
                for k, (si, g, i_, n, lat) in enumerate(items):
                    if i_ == 0:
                        Hc = init_H(z, lat)
                        yield
                    nxt = None
                    if k + 1 < len(items):
                        nxt = loads(items[k + 1][1], z, items[k + 1][2] * 2 >= items[k + 1][3])
                    pg_ = pre(g, z, cur)
                    while True:
                        try:
                            next(pg_)
                        except StopIteration as e_:
                            B = e_.value
                            break
                        if pp[0] is not None and INTERLEAVE_POST:
                            try:
                                next(pp[0])
                            except StopIteration:
                                pp[0] = None
                        yield
                    cur = nxt
                    if prev is not None:
                        do_seq(prev)
                        yield
                    prev = (g, B, Hc, si, i_ == n - 1, lat)
                do_seq(prev)
                drain_post()
                yield

            gens = [chain(0), chain(1)]
            for _ in range(STAGGER):
                next(gens[0])
            while gens:
                for gn in list(gens):
                    try:
                        next(gn)
                    except StopIteration:
                        gens.remove(gn)
            S.end_phase()
        chk(2)

        S.begin_phase()
        with ExitStack() as ph:
            KT = SB(ph, "KTs", [128, 4, 4352], BF16)
            r_KTk = [Res() for _ in range(34)]
            r_VAk = [Res() for _ in range(34)]
            kch = [Res() for _ in range(4)]
            vch = [Res() for _ in range(8)]
            VA = SB(ph, "VAs", [128, 34, 4, 129], BF16)
            S.op("pool", lambda e: e.memset(VA[:, :, :, 128:129], 1.0), writes=r_VAk)
            qtr = Ring(lambda i: SB(ph, "qt%d" % i, [128, 4, 512], BF16), 2)
            ptr_ = Ring(lambda i: SB(ph, "pt%d" % i, [128, 2, 512], BF16), 3)
            ckf = Ring(lambda i: SB(ph, "ckf%d" % i, [128, 512], F32), 2)
            ckb = Ring(lambda i: SB(ph, "ckb%d" % i, [128, 512], BF16), 2)
            ot = SB(ph, "ot", [128, 4, 4, 128], F32)
            r_ot = Res()
            t128 = Ring(lambda i: SB(ph, "t128_%d" % i, [128, 128], F32), 3)
            rrr = Ring(lambda i: SB(ph, "rr%d" % i, [128, 8], F32), 2)
            accr = Ring(lambda i: SB(ph, "accs%d" % i, [128, 3, 387], F32), 2)
            osq = SB(ph, "osq", [128, 2048], F32)
            r_osq = Res()
            s16 = Ring(lambda i: SB(ph, "s16_%d" % i, [128, 16], F32), 2)
            onb = SB(ph, "onb", [128, 4, 4, 128], BF16)
            r_onb = Res()
            oast = Ring(lambda i: SB(ph, "oast%d" % i, [128, 4, 512], BF16), 2)
            subw8 = SB(ph, "subw8", [128, 128], F32)
            r_subw8 = Res()
            S.op("dve", lambda e: e.tensor_scalar(out=subw8[:], in0=RW("subw"), scalar1=0.8, scalar2=None, op0=ALU.mult),
                 reads=[r_rowp], writes=[r_subw8])
            pTc = PS(ph, "pTc", [128, 4, 128], BF16)
            r_pTc = Res(True)
            psr = Ring(lambda i: PS(ph, "psr%d" % i, [128, 2, 512]), 2, psum=True)
            accb = [PS(ph, "acc%d" % i, [128, 512]) for i in range(3)]
            r_acc = [Res(True) for _ in range(3)]
            slot = {}
            for idx in range(8):
                slot[(idx // 4, idx % 4)] = (idx // 3, (idx % 3) * 129)

            segs = [(0, 4096, True)] + [(4096 + 256 * i, 256, False) for i in range(4)]
            for si_, (t0, T, lat) in enumerate(segs):
                nkt = (T + (256 if lat else 0)) // 128
                koff = 256 if lat else 0
                kb = 0 if lat else 2 * (si_ - 1)
                if lat:
                    for i in range(2):
                        cf, r_cf = ckf.next()
                        S.dma("sp", cf[:], ck[i * 128:(i + 1) * 128, :], writes=[r_cf], chan_res=r_cf)
                        cb_, r_cb = ckb.next()
                        S.op("dve", lambda e, cf=cf, cb_=cb_: e.tensor_copy(out=cb_[:], in_=cf[:]), reads=[r_cf], writes=[r_cb])
                        for h in range(4):
                            S.op("pe", lambda e, cb_=cb_, h=h: e.transpose(out=pTc[:, h, :], in_=cb_[:, h * 128:(h + 1) * 128], identity=identb[:]),
                                 reads=[r_cb, r_idb], writes=[r_pTc])
                        S.op("act", lambda e, i=i: e.activation(out=KT[:, :, i * 128:(i + 1) * 128], in_=pTc[:], func=AF.Copy),
                             reads=[r_pTc], writes=[r_KTk[i]])
                    for kt in range(2):
                        S.dma("pool", VA[:, kt, :, 0:128], cv[kt * 128:(kt + 1) * 128, :].rearrange("p (h e) -> p h e", h=4),
                              writes=[r_VAk[kt], vch[kt % 8]], chan_res=vch[kt % 8])
                k0 = kb + koff // 128
                nown = T // 128
                grp = 8 if lat else 2
                for gi in range(nown // grp):
                    ks = list(range(k0 + gi * grp, k0 + (gi + 1) * grp))
                    S.dma("sp", KT[:, :, ks[0] * 128:(ks[-1] + 1) * 128],
                          KT_d[:, :, t0 + gi * grp * 128:t0 + (gi + 1) * grp * 128].rearrange("h p t -> p h t"),
                          writes=[r_KTk[k] for k in ks] + [kch[gi % 4]], chan_res=kch[gi % 4])
                for kt in range(nown):
                    S.dma("sp", VA[:, k0 + kt, :, 0:128], V_d[t0 + kt * 128:t0 + (kt + 1) * 128, :].rearrange("p (h e) -> p h e", h=4),
                          writes=[r_VAk[k0 + kt], vch[kt % 8]], chan_res=vch[kt % 8])
                QB = min(512, T)
                nqs = QB // 128
                def qload(qb_):
                    q0_ = t0 + qb_ * QB
                    QT_, r_QT_ = qtr.next()
                    S.dma("sp", QT_[:, :, 0:QB], QT_d[:, :, q0_:q0_ + QB].rearrange("h p t -> p h t"), writes=[r_QT_], chan_res=r_QT_)
                    return QT_, r_QT_
                qnext = qload(0)
                for qb in range(T // QB):
                    q0 = t0 + qb * QB
                    QT, r_QT = qnext
                    if qb + 1 < T // QB:
                        qnext = qload(qb + 1)
                    for h in range(4):
                        started = set()

                        def qk(kt):
                            pS, r_pS = psr.next()
                            for m in range(2):
                                S.op("pe", lambda e, pS=pS, m=m: e.matmul(pS[:, m, 0:QB], lhsT=KT[m * 64:(m + 1) * 64, h, (kb + kt) * 128:(kb + kt + 1) * 128],
                                                                          rhs=QT[m * 64:(m + 1) * 64, h, 0:QB], start=True, stop=True),
                                     reads=[r_KTk[kb + kt], r_QT], writes=[r_pS])
                            PT, r_PT = ptr_.next()
                            S.op("act", lambda e, pS=pS, PT=PT: e.activation(out=PT[:, :, 0:QB], in_=pS[:, :, 0:QB], func=AF.Exp, scale=0.125),
                                 reads=[r_pS], writes=[r_PT])
                            return PT, r_PT

                        def av(kt, PT, r_PT):
                            last = (kt == nkt - 1)
                            for m in range(2):
                                for qs in range(nqs):
                                    bk, co = slot[(m, qs)]
                                    first = bk not in started
                                    started.add(bk)
                                    S.op("pe", lambda e, qs=qs, bk=bk, co=co, first=first, m=m: e.matmul(
                                        accb[bk][:, co:co + 129], lhsT=PT[:, m, qs * 128:(qs + 1) * 128], rhs=VA[:, kb + kt, h, :],
                                        start=first, stop=last, skip_group_check=True), reads=[r_PT, r_VAk[kb + kt]], writes=[r_acc[bk]])
                        cur = qk(0)
                        for kt in range(nkt):
                            nxt = qk(kt + 1) if kt + 1 < nkt else None
                            av(kt, cur[0], cur[1])
                            cur = nxt
                        accs, r_accs = accr.next()
                        for bk in range(3):
                            S.op("dve", lambda e, bk=bk, accs=accs: e.tensor_copy(out=accs[:, bk, :], in_=accb[bk][:, 0:387]),
                                 reads=[r_acc[bk]], writes=[r_accs])
                        rr, r_rr = rrr.next()
                        nacc = 4 + nqs
                        av_ = accs[:].rearrange("p a b -> p (a b)")[:, 0:8 * 129].rearrange("p (i c) -> p i c", c=129)
                        S.op("dve", lambda e, rr=rr, av_=av_: e.reciprocal(out=rr[:, 0:nacc], in_=av_[:, 0:nacc, 128]),
                             reads=[r_accs], writes=[r_rr])
                        S.op("dve", lambda e, rr=rr: e.tensor_scalar(out=rr[:, 4:8], in0=rr[:, 4:8], scalar1=lamc[:, 0:1], scalar2=None, op0=ALU.mult),
                             reads=[r_rr, r_lamc], writes=[r_rr])
                        for qs in range(nqs):
                            tt_, r_tt = t128.next()
                            S.op("dve", lambda e, qs=qs, tt_=tt_, av_=av_, rr=rr: e.tensor_scalar(out=tt_[:], in0=av_[:, 4 + qs, 0:128],
                                                                                            scalar1=rr[:, 4 + qs:5 + qs], scalar2=None, op0=ALU.mult),
                                 reads=[r_accs, r_rr], writes=[r_tt])
                            S.op("dve", lambda e, qs=qs, tt_=tt_, av_=av_, rr=rr: e.scalar_tensor_tensor(
                                out=ot[:, qs, h, :], in0=av_[:, qs, 0:128], scalar=rr[:, qs:qs + 1], in1=tt_[:],
                                op0=ALU.mult, op1=ALU.add), reads=[r_accs, r_rr, r_tt], writes=[r_ot])
                    nn = nqs * 4
                    S.op("act", lambda e: e.activation(out=osq[:, 0:nn * 128], in_=ot[:, 0:nqs, :, :].rearrange("p a b c -> p (a b c)"), func=AF.Square),
                         reads=[r_ot], writes=[r_osq])
                    ss, r_ss = s16.next()
                    S.op("dve", lambda e: e.tensor_reduce(out=ss[:, 0:nn], in_=osq[:, 0:nn * 128].rearrange("p (a c) -> p a c", c=128), axis=AX.X, op=ALU.add),
                         reads=[r_osq], writes=[r_ss])
                    S.op("dve", lambda e: e.tensor_scalar(out=ss[:, 0:nn], in0=ss[:, 0:nn], scalar1=1.0 / 128, scalar2=EPS_RMS, op0=ALU.mult, op1=ALU.add),
                         reads=[r_ss], writes=[r_ss])
                    rsqrt_pool(ss[:, 0:nn], r_ss, nn)
                    S.op("dve", lambda e: e.tensor_tensor(out=osq[:, 0:nn * 128].rearrange("p (a c) -> p a c", c=128),
                                                          in0=ot[:, 0:nqs, :, :].rearrange("p a b c -> p (a b) c"),
                                                          in1=bc(ss[:, 0:nn], [128, nn, 128], 2), op=ALU.mult), reads=[r_ot, r_ss], writes=[r_osq])
                    S.op("dve", lambda e: e.tensor_tensor(out=onb[:, 0:nqs, :, :].rearrange("p a b c -> p (a b) c"),
                                                          in0=osq[:, 0:nn * 128].rearrange("p (a c) -> p a c", c=128),
                                                          in1=bc(subw8[:], [128, nn, 128], 1), op=ALU.mult), reads=[r_osq, r_subw8], writes=[r_onb])
                    oa, r_oa = oast.next()
                    for qs in range(nqs):
                        for h in range(4):
                            S.op("pe", lambda e, qs=qs, h=h: e.transpose(out=pTc[:, h, :], in_=onb[:, qs, h, :], identity=identb[:]),
                                 reads=[r_onb, r_idb], writes=[r_pTc])
                        S.op("act", lambda e, qs=qs, oa=oa: e.activation(out=oa[:, :, qs * 128:(qs + 1) * 128], in_=pTc[:], func=AF.Copy),
                             reads=[r_pTc], writes=[r_oa])
                    S.dma("sp", OAT_d[:, :, q0:q0 + QB].rearrange("h p t -> p h t"), oa[:, :, 0:QB], reads=[r_oa], chan_res=r_oa)
            S.end_phase()
        chk(3)

        S.begin_phase()
        with ExitStack() as ph:
            wf2 = SB(ph, "wf2", [128, 22, 1024], BF16)
            r_wf2 = Res()
            for c in range(6):
                n = min(4, 22 - c * 4)
                r_tmp = Res()
                S.dma("pool", wf2[:, c * 4:c * 4 + n, :], w_f2_b[c * 512:c * 512 + n * 128, :].rearrange("(k p) n -> p k n", p=128),
                      reads=[r_wcv["w_f2"]], writes=[r_wf2], chan_res=r_wf2)
            wre = Ring(lambda i: SB(ph, "we%d" % i, [128, 8, 512], BF16), 4)
            gbc = SB(ph, "gbc", [128, 2, 1024], F32)
            r_gbc = Res()
            gb = Ring(lambda i: SB(ph, "gb%d" % i, [128, 128], F32), 2)
            aT = SB(ph, "aT", [128, 22, 512], BF16)
            r_aT = Res()
            h2T = SB(ph, "h2T", [128, 8, 512], BF16)
            r_h2T = Res()
            mgT = SB(ph, "mgT", [128, 8, 512], BF16)
            r_mgT = Res()
            oar = SB(ph, "oar", [128, 4, 512], BF16)
            orr = SB(ph, "orr", [128, 4, 512], BF16)
            r_oar = Res()
            r_orr = Res()
            gtl = Ring(lambda i: SB(ph, "gtl%d" % i, [128, 512], BF16), 8)
            f6 = Ring(lambda i: SB(ph, "f6_%d" % i, [128, 512], F32), 3)
            xe = Ring(lambda i: SB(ph, "xe%d" % i, [128, 1024], F32), 4)
            x1r = Ring(lambda i: SB(ph, "x1_%d" % i, [128, 1024], F32), 2)
            xb2 = SB(ph, "xb2", [128, 1024], BF16)
            r_xb2 = Res()
            junk2 = xb2
            r_junk2 = r_xb2
            st5 = Ring(lambda i: SB(ph, "st5_%d" % i, [128, 4], F32), 4)
            pTe = PS(ph, "pTe", [128, 8, 128], BF16)
            r_pTe = Res(True)
            pe_ = Ring(lambda i: PS(ph, "pe%d" % i, [128, 512]), 6, psum=True)
            r_x1d = [Res() for _ in range(NT)]

            def fill_gbc(cnd, wh_):
                for wh, base in (((0, 16),) if wh_ == 0 else ((1, 40),)):
                    for half in range(2):
                        p, r_p = pe_.next()
                        for jj in range(4):
                            j = half * 4 + jj
                            g_, r_g = gb.next()
                            S.op("dve", lambda e, g_=g_, j=j, base=base, cnd=cnd: e.tensor_scalar(
                                out=g_[:], in0=ones_f[:], scalar1=mod[:, base + j, cnd:cnd + 1], scalar2=None, op0=ALU.mult),
                                reads=[r_ones, r_mod], writes=[r_g])
                            S.op("pe", lambda e, p=p, jj=jj, g_=g_: e.matmul(p[:, jj * 128:(jj + 1) * 128], lhsT=g_[:], rhs=identf,
                                                                             start=True, stop=True), reads=[r_g, r_cst], writes=[r_p])
                        S.op("act", lambda e, p=p, cnd=cnd, wh=wh, half=half: e.activation(
                            out=gbc[:, wh, half * 512:(half + 1) * 512], in_=p[:], func=AF.Copy), reads=[r_p], writes=[r_gbc])

            def wl_(nm, shape_k, c0, ncol):
                src = {"w_abr": w_abr_b, "w_rbr": w_rbr_b, "w_out": w_out_b, "w_f1": w_f1_b}[nm]
                w, r_w = wre.next()
                S.dma("pool", w[:, 0:shape_k, 0:ncol], src[:, c0:c0 + ncol].rearrange("(k p) n -> p k n", p=128),
                      reads=[r_wcv[nm]], writes=[r_w], chan_res=r_w)
                return w, r_w

            XT = {}

            def merge_part(blk):
                cnd = 0 if blk < 8 else 1
                tk0 = blk * 512
                S.dma("sp", oar[:], OAT_d[:, :, tk0:tk0 + 512].rearrange("h p t -> p h t"), writes=[r_oar], chan_res=r_oar)
                S.dma("sp", orr[:], ORT_d[:, :, tk0:tk0 + 512].rearrange("h p t -> p h t"), writes=[r_orr], chan_res=r_orr)
                for half in range(2):
                    wa, r_wa = wl_("w_abr", 4, half * 512, 512)
                    wb, r_wb = wl_("w_rbr", 4, half * 512, 512)
                    gl = []
                    for jj in range(4):
                        j = half * 4 + jj
                        ga, r_ga = gtl.next()
                        S.dma("sp", ga[:], GT_d[j, :, tk0:tk0 + 512], writes=[r_ga], chan_res=r_ga)
                        gr_, r_gr = gtl.next()
                        S.dma("sp", gr_[:], GT_d[8 + j, :, tk0:tk0 + 512], writes=[r_gr], chan_res=r_gr)
                        gl.append((ga, r_ga, gr_, r_gr))
                    for jj in range(4):
                        j = half * 4 + jj
                        ga, r_ga, gr_, r_gr = gl[jj]
                        pa_, r_pa = pe_.next()
                        for kc in range(4):
                            S.op("pe", lambda e, pa_=pa_, kc=kc, wa=wa, jj=jj: e.matmul(pa_[:], lhsT=wa[:, kc, jj * 128:(jj + 1) * 128], rhs=oar[:, kc, :],
                                                                                       start=(kc == 0), stop=(kc == 3)), reads=[r_wa, r_oar], writes=[r_pa])
                        pb_, r_pb = pe_.next()
                        for kc in range(4):
                            S.op("pe", lambda e, pb_=pb_, kc=kc, wb=wb, jj=jj: e.matmul(pb_[:], lhsT=wb[:, kc, jj * 128:(jj + 1) * 128], rhs=orr[:, kc, :],
                                                                                       start=(kc == 0), stop=(kc == 3)), reads=[r_wb, r_orr], writes=[r_pb])
                        ta, r_ta = f6.next()
                        S.op("dve", lambda e, ta=ta, pa_=pa_, ga=ga: e.tensor_tensor(out=ta[:], in0=pa_[:], in1=ga[:], op=ALU.mult),
                             reads=[r_pa, r_ga], writes=[r_ta])
                        tb_, r_tb = f6.next()
                        S.op("dve", lambda e, tb_=tb_, pb_=pb_, gr_=gr_: e.tensor_tensor(out=tb_[:], in0=pb_[:], in1=gr_[:], op=ALU.mult),
                             reads=[r_pb, r_gr], writes=[r_tb])
                        S.op("dve", lambda e, ta=ta, tb_=tb_, j=j: e.tensor_tensor(out=mgT[:, j, :], in0=ta[:], in1=tb_[:], op=ALU.add),
                             reads=[r_ta, r_tb], writes=[r_mgT])

            def outproj_part(blk):
                cnd = 0 if blk < 8 else 1
                tk0 = blk * 512
                xts = []
                for ti in range(4):
                    g = blk * 4 + ti
                    xt, r_xt = xe.next()
                    S.dma("sp", xt[:], xall[g * 128:(g + 1) * 128, :], writes=[r_xt], chan_res=r_xt)
                    xts.append((xt, r_xt))
                XT[blk] = xts
                wo = [wl_("w_out", 8, nb * 512, 512) for nb in range(2)]
                for ti in range(4):
                    g = blk * 4 + ti
                    xt, r_xt = xts[ti]
                    x1, r_x1 = x1r.next()
                    for nb in range(2):
                        po, r_po = pe_.next()
                        for j in range(8):
                            S.op("pe", lambda e, po=po, j=j, nb=nb: e.matmul(po[:], lhsT=mgT[:, j, ti * 128:(ti + 1) * 128], rhs=wo[nb][0][:, j, :],
                                                                            start=(j == 0), stop=(j == 7)), reads=[r_mgT, wo[nb][1]], writes=[r_po])
                        tt_, r_tt = f6.next()
                        S.op("dve", lambda e, tt_=tt_, po=po, nb=nb: e.tensor_tensor(out=tt_[:], in0=po[:], in1=gbc[:, 0, nb * 512:(nb + 1) * 512],
                                                                                    op=ALU.mult), reads=[r_po, r_gbc], writes=[r_tt])
                        S.op("dve", lambda e, tt_=tt_, nb=nb, x1=x1, xt=xt: e.tensor_tensor(out=x1[:, nb * 512:(nb + 1) * 512], in0=tt_[:],
                                                                                            in1=xt[:, nb * 512:(nb + 1) * 512], op=ALU.add),
                             reads=[r_tt, r_xt], writes=[r_x1])
                    S.dma("sp", X1_d[g * 128:(g + 1) * 128, :], x1[:], reads=[r_x1], writes=[r_x1d[g]], chan_res=r_x1)
                    ss, r_ss = st5.next()
                    S.op("act", lambda e, x1=x1, ss=ss: e.activation(out=junk2[:], in_=x1[:], func=AF.Square, accum_out=ss[:, 0:1]),
                         reads=[r_x1], writes=[r_junk2, r_ss])
                    S.op("dve", lambda e, ss=ss: e.tensor_scalar(out=ss[:, 0:1], in0=ss[:, 0:1], scalar1=1.0 / 1024, scalar2=EPS_RMS,
                                                                 op0=ALU.mult, op1=ALU.add), reads=[r_ss], writes=[r_ss])
                    rsqrt_pool(ss[:, 0:1], r_ss, 1)
                    S.op("act", lambda e, x1=x1, ss=ss: e.activation(out=xb2[:], in_=x1[:], func=AF.Copy, scale=ss[:, 0:1]),
                         reads=[r_x1, r_ss], writes=[r_xb2])
                    for kc in range(8):
                        S.op("pe", lambda e, kc=kc: e.transpose(out=pTe[:, kc, :], in_=xb2[:, kc * 128:(kc + 1) * 128], identity=identb[:]),
                             reads=[r_xb2, r_idb], writes=[r_pTe])
                    for kc in range(8):
                        S.op("dve", lambda e, kc=kc, ti=ti: e.tensor_scalar(out=h2T[:, kc, ti * 128:(ti + 1) * 128], in0=pTe[:, kc, :],
                                                                           scalar1=s2[:, kc, cnd:cnd + 1], scalar2=mod[:, 24 + kc, cnd:cnd + 1],
                                                                           op0=ALU.mult, op1=ALU.add), reads=[r_pTe, r_mod], writes=[r_h2T])

            def ffn_in_part(blk):
                cnd = 0 if blk < 8 else 1
                for c in range(6):
                    n = min(4, 22 - c * 4)
                    wu, r_wu = wl_("w_f1", 8, c * 512, n * 128)
                    wg, r_wg = wl_("w_f1", 8, D_FF + c * 512, n * 128)
                    for jj in range(n):
                        j = c * 4 + jj
                        pu, r_pu = pe_.next()
                        for kc in range(8):
                            S.op("pe", lambda e, pu=pu, kc=kc, wu=wu, jj=jj: e.matmul(pu[:], lhsT=wu[:, kc, jj * 128:(jj + 1) * 128], rhs=h2T[:, kc, :],
                                                                                     start=(kc == 0), stop=(kc == 7)), reads=[r_wu, r_h2T], writes=[r_pu])
                        pg, r_pg = pe_.next()
                        for kc in range(8):
                            S.op("pe", lambda e, pg=pg, kc=kc, wg=wg, jj=jj: e.matmul(pg[:], lhsT=wg[:, kc, jj * 128:(jj + 1) * 128], rhs=h2T[:, kc, :],
                                                                                     start=(kc == 0), stop=(kc == 7)), reads=[r_wg, r_h2T], writes=[r_pg])
                        su, r_su = f6.next()
                        S.op("act", lambda e, su=su, pu=pu: e.activation(out=su[:], in_=pu[:], func=AF.Silu), reads=[r_pu], writes=[r_su])
                        S.op("dve", lambda e, su=su, pg=pg, j=j: e.tensor_tensor(out=aT[:, j, :], in0=pg[:], in1=su[:], op=ALU.mult),
                             reads=[r_pg, r_su], writes=[r_aT])

            def ffn_out_part(blk, tis=(0, 1, 2, 3)):
                cnd = 0 if blk < 8 else 1
                xts = XT[blk]
                for ti in tis:
                    g = blk * 4 + ti
                    x1, r_x1 = x1r.next()
                    S.dma("sp", x1[:], X1_d[g * 128:(g + 1) * 128, :], reads=[r_x1d[g]], writes=[r_x1], chan_res=r_x1)
                    yo, r_yo = xts[ti]
                    for nb in range(2):
                        po, r_po = pe_.next()
                        for j in range(22):
                            S.op("pe", lambda e, po=po, j=j, nb=nb: e.matmul(po[:], lhsT=aT[:, j, ti * 128:(ti + 1) * 128], rhs=wf2[:, j, nb * 512:(nb + 1) * 512],
                                                                            start=(j == 0), stop=(j == 21)), reads=[r_aT, r_wf2], writes=[r_po])
                        tt_, r_tt = f6.next()
                        S.op("dve", lambda e, tt_=tt_, po=po, nb=nb: e.tensor_tensor(out=tt_[:], in0=po[:], in1=gbc[:, 1, nb * 512:(nb + 1) * 512],
                                                                                    op=ALU.mult), reads=[r_po, r_gbc], writes=[r_tt])
                        S.op("dve", lambda e, tt_=tt_, nb=nb, x1=x1, yo=yo: e.tensor_tensor(out=yo[:, nb * 512:(nb + 1) * 512], in0=tt_[:],
                                                                                            in1=x1[:, nb * 512:(nb + 1) * 512], op=ALU.add),
                             reads=[r_tt, r_x1], writes=[r_yo])
                    S.dma("sp", yall[g * 128:(g + 1) * 128, :], yo[:], reads=[r_yo], chan_res=r_yo)

            fill_gbc(0, 0)
            fill_gbc(0, 1)
            merge_part(0)
            outproj_part(0)
            for blk in range(10):
                ffn_in_part(blk)
                if blk + 1 < 10:
                    merge_part(blk + 1)
                if blk == 8:
                    fill_gbc(1, 1)
                ffn_out_part(blk, (0, 1))
                if blk + 1 < 10:
                    if blk + 1 == 8:
                        fill_gbc(1, 0)
                    outproj_part(blk + 1)
                ffn_out_part(blk, (2, 3))
            S.end_phase()
        chk(5)
        S.barrier()


def _host_consts():
    c0 = math.exp(-0.5)
    s = np.arange(128)[:, None]
    t = np.arange(128)[None, :]
    cst = np.zeros((128, 2184), np.float32)
    cst[:, 0:128] = np.eye(128, dtype=np.float32)
    f = lambda m: m.astype(np.float32)
    mats = [
        -c0 * (f(s <= t) - f(s <= 63)), -c0 * (f(s < t) - f(s <= 63)), -c0 * f(s > t),
        -c0 * (f(s >= t) - f(s >= 64)), -c0 * (f(s > t) - f(s >= 64)), -c0 * f(s < t),
    ]
    for i, m in enumerate(mats):
        cst[:, 128 + i * 128: 256 + i * 128] = m
    sv = np.arange(128)
    cst[:, 896] = -c0
    cst[:, 897] = -c0 * (sv <= 63)
    cst[:, 898] = -c0
    cst[:, 899] = -c0 * (sv >= 64)
    for z in range(2):
        o = 904 + z * 640
        r = np.arange(128)[:, None]
        c = np.arange(128)[None, :]
        if z == 0:
            before_rc = f(c < r)
            before_st = f(r < c)
            incl_st = f(r <= c)
        else:
            before_rc = f(c > r)
            before_st = f(r > c)
            incl_st = f(r >= c)
        cst[:, o:o + 128] = -before_rc
        cst[:, o + 128:o + 256] = -before_st
        cst[:, o + 256:o + 384] = incl_st
        cst[:, o + 384:o + 512] = before_st
        cst[:, o + 512:o + 640] = incl_st
    tok = np.arange(4096)
    row = (tok // 64).astype(np.float32)
    col = (tok % 64).astype(np.float32)
    inv = (np.float32(10000.0) ** (-np.arange(16, dtype=np.float32) / np.float32(16))).astype(np.float32)
    ar = (row[:, None] * inv).astype(np.float32)
    ac = (col[:, None] * inv).astype(np.float32)
    cos = np.concatenate([np.cos(ar), np.cos(ar), np.cos(ac), np.cos(ac)], -1)
    sin = np.concatenate([-np.sin(ar), np.sin(ar), -np.sin(ac), np.sin(ac)], -1)
    rope = np.stack([cos, sin], 1).astype(np.float32)
    rope = rope.reshape(32, 128, 2, 64).transpose(1, 0, 2, 3)
    return cst, np.ascontiguousarray(rope)


def _in_maps(inp):
    A = lambda a: np.ascontiguousarray(np.asarray(a, dtype=np.float32))
    cst, rope = _host_consts()
    colp = np.concatenate([A(inp["norm1_w"])[0].reshape(8, 128).T, A(inp["norm2_w"])[0].reshape(8, 128).T,
                           A(inp["ada_b"])[0].reshape(48, 128).T], 1)
    rowp = np.concatenate([A(inp["q_norm_w"])[0], A(inp["k_norm_w"])[0], A(inp["subln_w"])[0], A(inp["k_k"])[0],
                           A(inp["k_a"])[0], A(inp["r_k"])[0].reshape(-1), A(inp["ln_x_w"])[0], A(inp["ln_x_b"])[0],
                           A(inp["lambda_q1"])[0], A(inp["lambda_k1"])[0], A(inp["lambda_q2"])[0], A(inp["lambda_k2"])[0],
                           A(inp["w0"])[0].reshape(-1), A(inp["a0"])[0].reshape(-1)])[None, :]
    shared = {
        "colp": A(colp), "rowp": A(rowp), "cst": cst, "rope": rope,
        "ada_w": A(inp["ada_w"])[0], "w_in": A(inp["w_in"])[0],
        "wlu": A(inp["w_lora_up"])[0].reshape(128, 512), "alu": A(inp["a_lora_up"])[0].reshape(128, 512),
        "w_abr": A(inp["w_attn_br"])[0], "w_rbr": A(inp["w_rwkv_br"])[0], "w_out": A(inp["w_out"])[0],
        "w_f1": A(inp["w_ffn_in"])[0], "w_f2": A(inp["w_ffn_out"])[0],
    }
    maps = []
    xp = A(inp["x_prompt"])
    xs = A(inp["x_sample"])
    for b in range(8):
        m = dict(shared)
        m["xall"] = np.concatenate([xs[b], xp[4 * b:4 * b + 4].reshape(1024, 1024)], 0)
        m["ck"] = A(inp["cache_k"])[b, 0].reshape(256, 512)
        m["cv"] = A(inp["cache_v"])[b, 0].reshape(256, 512)
        m["st0"] = A(inp["state_rwkv"])[b, 0].reshape(16, 64, 64)
        cond = np.stack([A(inp["c"])[b], A(inp["c_ctx"])], 0)
        m["condT"] = np.ascontiguousarray(cond.reshape(2, 8, 128).transpose(2, 1, 0).reshape(128, 16))
        maps.append(m)
    return maps


_NC_CACHE = {}


def kernel(**inputs):
    if "nc" not in _NC_CACHE:
        _NC_CACHE["nc"] = build_nc()
    nc = _NC_CACHE["nc"]
    maps = _in_maps(inputs)
    res = run_bass_kernel_spmd(nc, maps, core_ids=list(range(8)))
    R = res.results
    y_sample = np.stack([R[b]["yall"][:4096] for b in range(8)], 0)
    y_prompt = np.concatenate([R[b]["yall"][4096:].reshape(4, 256, 1024) for b in range(8)], 0)
    new_k = np.concatenate([R[b]["nk"].reshape(4, 1, 256, 4, 2, 64) for b in range(8)], 0)
    new_v = np.concatenate([R[b]["nv"].reshape(4, 1, 256, 4, 128) for b in range(8)], 0)
    new_s = np.concatenate([R[b]["ns"].reshape(4, 1, 2, 8, 64, 64) for b in range(8)], 0)
    return (y_prompt.astype(np.float32), y_sample.astype(np.float32), new_k.astype(np.float32),
            new_v.astype(np.float32), new_s.astype(np.float32))
```
